# Optimizing a Trainium2 kernel written in Bass

```python
import math
import jax, jax.numpy as jnp
from jax import lax
import numpy as np

D_MODEL = 2048
BATCH = 16
SEQ = 256
DEPTH = 2
DEC_BATCH = 8
DEC_SEQ = 4096
PAST_LEN = 512

GRID_W = 64
NORM_EPS = 1e-6
F_FLOOR = 1e-30
N_BRANCHES = 4
FOURIER_WIDTH = 512
FOURIER_GROUPS = 4
FOURIER_GROUP_DIM = FOURIER_WIDTH // FOURIER_GROUPS
S5_WIDTH = 512
S5_GROUP_DIM = 16
S5_GROUPS = S5_WIDTH // S5_GROUP_DIM
S5_STATE = 64
N_HEADS = 8
N_KV_HEADS = 2
HEAD_DIM = 128
Q_PER_KV = N_HEADS // N_KV_HEADS
ATTN_WIDTH = N_HEADS * HEAD_DIM
KV_WIDTH = N_KV_HEADS * HEAD_DIM
ROPE_THETA = 10000.0
Q_BLOCK = 128
HGRN_HEADS = 4
HGRN_DK = 128
HGRN_DV = 128
HGRN_WIDTH = HGRN_HEADS * HGRN_DK
HGRN_VWIDTH = HGRN_HEADS * HGRN_DV
HGRN_CHUNK = 64

IN_SIZES = (FOURIER_WIDTH, FOURIER_WIDTH,
            S5_WIDTH, S5_WIDTH,
            ATTN_WIDTH, KV_WIDTH, KV_WIDTH, ATTN_WIDTH,
            HGRN_WIDTH, HGRN_VWIDTH, HGRN_WIDTH, HGRN_WIDTH, HGRN_VWIDTH,
            N_BRANCHES * D_MODEL)
IN_COLS = sum(IN_SIZES)

kernel_name = 'hybrid_diffusion_gated_branch_step'


def rmsnorm(x, g):
    x32 = x.astype(jnp.float32)
    y = x32 * lax.rsqrt(jnp.mean(x32 * x32, axis=-1, keepdims=True) + NORM_EPS)
    return (y * g.astype(jnp.float32)).astype(x.dtype)


def split_cols(proj):
    idx = [int(v) for v in np.cumsum(IN_SIZES)[:-1]]
    return jnp.split(proj, idx, axis=-1)


def fourier_mix(u, w):
    b, l, _ = u.shape
    ug = u.astype(jnp.float32).reshape(b, l, FOURIER_GROUPS, FOURIER_GROUP_DIM)
    f = jnp.fft.fftn(ug, axes=(1, 3), norm='ortho').real
    y = jnp.einsum('blgc,gcd->blgd', f, w.astype(jnp.float32))
    return y.reshape(b, l, FOURIER_WIDTH).astype(u.dtype)


def s5_discretize(lam_re, lam_im, log_step, b_re, b_im):
    lam_re = lam_re.astype(jnp.float32)
    lam_im = lam_im.astype(jnp.float32)
    step = jnp.exp(log_step.astype(jnp.float32))[:, None]
    mag = jnp.exp(lam_re * step)
    lb_re = mag * jnp.cos(lam_im * step)
    lb_im = mag * jnp.sin(lam_im * step)
    nr = lb_re - 1.0
    den = lam_re * lam_re + lam_im * lam_im
    fr = (nr * lam_re + lb_im * lam_im) / den
    fi = (lb_im * lam_re - nr * lam_im) / den
    b_re = b_re.astype(jnp.float32)
    b_im = b_im.astype(jnp.float32)
    bb_re = fr[..., None] * b_re - fi[..., None] * b_im
    bb_im = fr[..., None] * b_im + fi[..., None] * b_re
    return lb_re, lb_im, bb_re, bb_im


def s5_combine(e1, e2):
    a1r, a1i, b1r, b1i = e1
    a2r, a2i, b2r, b2i = e2
    return (a2r * a1r - a2i * a1i,
            a2r * a1i + a2i * a1r,
            a2r * b1r - a2i * b1i + b2r,
            a2r * b1i + a2i * b1r + b2i)


def s5_scan(u, lb_re, lb_im, bb_re, bb_im, c_re, c_im, h0_re, h0_im):
    bu_re = jnp.einsum('blgp,gnp->blgn', u, bb_re)
    bu_im = jnp.einsum('blgp,gnp->blgn', u, bb_im)
    bu_re = bu_re.at[:, 0].add(lb_re * h0_re - lb_im * h0_im)
    bu_im = bu_im.at[:, 0].add(lb_re * h0_im + lb_im * h0_re)
    a_re = jnp.broadcast_to(lb_re, bu_re.shape)
    a_im = jnp.broadcast_to(lb_im, bu_im.shape)
    _, _, h_re, h_im = lax.associative_scan(s5_combine, (a_re, a_im, bu_re, bu_im), axis=1)
    c_re = c_re.astype(jnp.float32)
    c_im = c_im.astype(jnp.float32)
    y = jnp.einsum('blgn,gpn->blgp', h_re, c_re) - jnp.einsum('blgn,gpn->blgp', h_im, c_im)
    return y, h_re[:, -1], h_im[:, -1]


def s5_branch(u, p, h0_re, h0_im):
    b, l, _ = u.shape
    u32 = u.astype(jnp.float32).reshape(b, l, S5_GROUPS, S5_GROUP_DIM)
    ys, fin_re, fin_im = [], [], []
    for d in range(2):
        lb_re, lb_im, bb_re, bb_im = s5_discretize(p['s5_lambda_re'][d], p['s5_lambda_im'][d],
                                                   p['s5_log_step'][d], p['s5_b_re'][d], p['s5_b_im'][d])
        ud = u32 if d == 0 else jnp.flip(u32, axis=1)
        yd, hr, hi = s5_scan(ud, lb_re, lb_im, bb_re, bb_im, p['s5_c_re'][d], p['s5_c_im'][d],
                             h0_re[:, d].astype(jnp.float32), h0_im[:, d].astype(jnp.float32))
        ys.append(yd if d == 0 else jnp.flip(yd, axis=1))
        fin_re.append(hr)
        fin_im.append(hi)
    dskip = p['s5_d'].astype(jnp.float32).reshape(S5_GROUPS, S5_GROUP_DIM)
    y = (ys[0] + ys[1] + dskip * u32).reshape(b, l, S5_WIDTH)
    y = jax.nn.gelu(y)
    y = y * jax.nn.sigmoid(y @ p['s5_glu_w'].astype(jnp.float32) + p['s5_glu_b'].astype(jnp.float32))
    return y.astype(u.dtype), jnp.stack(fin_re, axis=1), jnp.stack(fin_im, axis=1)


def attention_heads(q, k, v, q_norm, k_norm):
    b, l, _ = q.shape
    q = rmsnorm(q.reshape(b, l, N_KV_HEADS, Q_PER_KV, HEAD_DIM), q_norm)
    k = rmsnorm(k.reshape(b, l, N_KV_HEADS, HEAD_DIM), k_norm)
    v = v.reshape(b, l, N_KV_HEADS, HEAD_DIM)
    return q, k, v


def axial_angles(l):
    rows = l // GRID_W
    row = jnp.broadcast_to(jnp.arange(rows, dtype=jnp.float32)[:, None], (rows, GRID_W)).reshape(-1)
    col = jnp.broadcast_to(jnp.arange(GRID_W, dtype=jnp.float32)[None, :], (rows, GRID_W)).reshape(-1)
    half = HEAD_DIM // 2
    inv = ROPE_THETA ** (-jnp.arange(0, half, 2, dtype=jnp.float32) / half)
    return jnp.stack([row[:, None] * inv, col[:, None] * inv], axis=1)


def apply_axial_rope(x, ang):
    l = x.shape[1]
    xr = x.astype(jnp.float32).reshape(x.shape[:-1] + (2, 2, HEAD_DIM // 4))
    a = ang.reshape((l,) + (1,) * (x.ndim - 3) + (2, HEAD_DIM // 4))
    cos, sin = jnp.cos(a), jnp.sin(a)
    x1, x2 = xr[..., 0, :], xr[..., 1, :]
    out = jnp.stack([x1 * cos - x2 * sin, x2 * cos + x1 * sin], axis=-2)
    return out.reshape(x.shape).astype(x.dtype)


def block_attention(q, k, v):
    b, lq, hkv, g, hd = q.shape
    nb = lq // Q_BLOCK
    qb = q.reshape(b, nb, Q_BLOCK, hkv, g, hd).transpose(1, 0, 2, 3, 4, 5)
    scale = HEAD_DIM ** -0.5

    def one_block(qblk):
        s = jnp.einsum('bqhgd,bkhd->bhgqk', qblk, k).astype(jnp.float32) * scale
        pr = jax.nn.softmax(s, axis=-1).astype(v.dtype)
        return jnp.einsum('bhgqk,bkhd->bqhgd', pr, v)

    o = lax.map(one_block, qb)
    return o.transpose(1, 0, 2, 3, 4, 5).reshape(b, lq, hkv * g * hd)


def hgrn_chunk_scan(q, k, v, log_f, s0):
    b, l, h, dk = q.shape
    n = l // HGRN_CHUNK

    def to_chunks(t):
        return t.reshape(b, n, HGRN_CHUNK, h, t.shape[-1]).transpose(1, 0, 3, 2, 4)

    lower = jnp.tril(jnp.ones((HGRN_CHUNK, HGRN_CHUNK), dtype=bool))[:, :, None]

    def step(s, inp):
        qc, kc, vc, lfc = inp
        cum = jnp.cumsum(lfc, axis=2)
        inter = jnp.einsum('bhtd,bhde->bhte', qc * jnp.exp(cum), s)
        diff = cum[:, :, :, None, :] - cum[:, :, None, :, :]
        decay = jnp.where(lower, jnp.exp(jnp.where(lower, diff, 0.0)), 0.0)
        scores = jnp.einsum('bhtd,bhtsd,bhsd->bhts', qc, decay, kc)
        intra = jnp.einsum('bhts,bhse->bhte', scores, vc)
        last = cum[:, :, -1, :]
        s_new = jnp.exp(last)[..., None] * s + jnp.einsum(
            'bhsd,bhse->bhde', kc * jnp.exp(last[:, :, None, :] - cum), vc)
        return s_new, inter + intra

    s_fin, out = lax.scan(step, s0, (to_chunks(q), to_chunks(k), to_chunks(v), to_chunks(log_f)))
    out = out.transpose(1, 0, 3, 2, 4).reshape(b, l, h, v.shape[-1])
    return out, s_fin


def hgrn_branch(q, i, z_fwd, z_bwd, lower_bound, s0, norm_g):
    b, l, _ = q.shape
    q32 = q.astype(jnp.float32).reshape(b, l, HGRN_HEADS, HGRN_DK)
    v32 = i.astype(jnp.float32).reshape(b, l, HGRN_HEADS, HGRN_DV)
    outs, finals = [], []
    for d, z in enumerate((z_fwd, z_bwd)):
        z32 = z.astype(jnp.float32).reshape(b, l, HGRN_HEADS, HGRN_DK)
        lb = lower_bound[d].astype(jnp.float32).reshape(HGRN_HEADS, HGRN_DK)
        f = lb + (1.0 - lb) * jax.nn.sigmoid(z32)
        log_f = jnp.log(jnp.maximum(f, F_FLOOR))
        kk = 1.0 - f
        qd, kd, vd, fd = q32, kk, v32, log_f
        if d == 1:
            qd, kd, vd, fd = (jnp.flip(t, axis=1) for t in (qd, kd, vd, fd))
        o, s_fin = hgrn_chunk_scan(qd, kd, vd, fd, s0[:, d].astype(jnp.float32))
        outs.append(o if d == 0 else jnp.flip(o, axis=1))
        finals.append(s_fin)
    o = rmsnorm(outs[0] + outs[1], norm_g)
    return o.reshape(b, l, HGRN_VWIDTH).astype(q.dtype), jnp.stack(finals, axis=1)


def trunk_layer(x, shift, scale, gate, p, ctx):
    b, l, _ = x.shape
    h = rmsnorm(x, p['norm_pre']) * (1.0 + scale) + shift
    proj = h @ p['w_in']
    (u_a, g_a, u_b, g_b, q_c, k_c, v_c, g_c,
     q_d, i_d, zf_d, zb_d, g_d, m) = split_cols(proj)
    if ctx is None:
        s5_h0_re = jnp.zeros((b, 2, S5_GROUPS, S5_STATE), jnp.float32)
        s5_h0_im = jnp.zeros((b, 2, S5_GROUPS, S5_STATE), jnp.float32)
        hgrn_s0 = jnp.zeros((b, 2, HGRN_HEADS, HGRN_DK, HGRN_DV), jnp.float32)
    else:
        ctx_k, ctx_v, s5_h0_re, s5_h0_im, hgrn_s0 = ctx

    y_a = fourier_mix(u_a, p['fourier_w'])
    y_b, s5_re, s5_im = s5_branch(u_b, p, s5_h0_re, s5_h0_im)
    q, k, v = attention_heads(q_c, k_c, v_c, p['q_norm'], p['k_norm'])
    if ctx is None:
        y_c = block_attention(q, k, v)
    else:
        ang = axial_angles(l)
        q = apply_axial_rope(q, ang)
        k_lat = apply_axial_rope(k, ang)
        keys = jnp.concatenate([k_lat, ctx_k.astype(k.dtype)], axis=1)
        vals = jnp.concatenate([v, ctx_v.astype(v.dtype)], axis=1)
        y_c = block_attention(q, keys, vals)
    y_d, hgrn_s = hgrn_branch(q_d, i_d, zf_d, zb_d, p['lower_bound'], hgrn_s0, p['hgrn_norm'])

    merge = jax.nn.sigmoid(m.reshape(b, l, N_BRANCHES, D_MODEL))
    branches = ((y_a, g_a, p['w_proj_a']), (y_b, g_b, p['w_proj_b']),
                (y_c, g_c, p['w_proj_c']), (y_d, g_d, p['w_proj_d']))
    mixed = None
    for j, (y, g, w) in enumerate(branches):
        term = merge[:, :, j] * ((y * jax.nn.silu(g)) @ w)
        mixed = term if mixed is None else mixed + term
    out = mixed @ p['w_out']
    x_new = x + gate * rmsnorm(out, p['norm_post'])
    if ctx is None:
        return x_new, (k, v, s5_re, s5_im, hgrn_s)
    return x_new, None


def setup_inputs(seed: int = 0) -> dict:
    key = jax.random.key(seed)
    ks = iter(jax.random.split(key, 48))
    f32 = jnp.float32

    def nrm(shape, s=1.0):
        return jax.random.normal(next(ks), shape, f32) * s

    def gain(shape):
        return 1.0 + nrm(shape, 0.02)

    inp = {}
    inp['x_prompt'] = nrm((BATCH, SEQ, D_MODEL))
    inp['x_sample'] = nrm((DEC_BATCH, DEC_SEQ, D_MODEL))
    inp['c'] = nrm((DEC_BATCH, D_MODEL))
    inp['cache_k'] = nrm((DEC_BATCH, DEPTH, PAST_LEN, N_KV_HEADS, HEAD_DIM))
    inp['cache_v'] = nrm((DEC_BATCH, DEPTH, PAST_LEN, N_KV_HEADS, HEAD_DIM))
    inp['state_s5_re'] = nrm((DEC_BATCH, DEPTH, 2, S5_GROUPS, S5_STATE), 0.1)
    inp['state_s5_im'] = nrm((DEC_BATCH, DEPTH, 2, S5_GROUPS, S5_STATE), 0.1)
    inp['state_hgrn'] = nrm((DEC_BATCH, DEPTH, 2, HGRN_HEADS, HGRN_DK, HGRN_DV), 0.1)
    inp['c_ctx'] = nrm((D_MODEL,))
    inp['norm_pre'] = gain((DEPTH, D_MODEL))
    inp['norm_post'] = gain((DEPTH, D_MODEL))
    inp['w_mod'] = nrm((DEPTH, D_MODEL, 3 * D_MODEL), 0.3 * D_MODEL ** -0.5)
    inp['b_mod'] = nrm((DEPTH, 3 * D_MODEL), 0.02)
    inp['w_in'] = nrm((DEPTH, D_MODEL, IN_COLS), D_MODEL ** -0.5)
    inp['fourier_w'] = nrm((DEPTH, FOURIER_GROUPS, FOURIER_GROUP_DIM, FOURIER_GROUP_DIM), FOURIER_GROUP_DIM ** -0.5)
    inp['s5_lambda_re'] = -0.5 + nrm((DEPTH, 2, S5_GROUPS, S5_STATE), 0.01)
    inp['s5_lambda_im'] = math.pi * jnp.arange(S5_STATE, dtype=f32) + nrm((DEPTH, 2, S5_GROUPS, S5_STATE), 0.01)
    inp['s5_log_step'] = jax.random.uniform(next(ks), (DEPTH, 2, S5_GROUPS), f32, math.log(1e-3), math.log(1e-1))
    inp['s5_b_re'] = nrm((DEPTH, 2, S5_GROUPS, S5_STATE, S5_GROUP_DIM), (2 * S5_GROUP_DIM) ** -0.5)
    inp['s5_b_im'] = nrm((DEPTH, 2, S5_GROUPS, S5_STATE, S5_GROUP_DIM), (2 * S5_GROUP_DIM) ** -0.5)
    inp['s5_c_re'] = nrm((DEPTH, 2, S5_GROUPS, S5_GROUP_DIM, S5_STATE), (2 * S5_STATE) ** -0.5)
    inp['s5_c_im'] = nrm((DEPTH, 2, S5_GROUPS, S5_GROUP_DIM, S5_STATE), (2 * S5_STATE) ** -0.5)
    inp['s5_d'] = nrm((DEPTH, S5_WIDTH))
    inp['s5_glu_w'] = nrm((DEPTH, S5_WIDTH, S5_WIDTH), S5_WIDTH ** -0.5)
    inp['s5_glu_b'] = nrm((DEPTH, S5_WIDTH), 0.02)
    inp['q_norm'] = gain((DEPTH, HEAD_DIM))
    inp['k_norm'] = gain((DEPTH, HEAD_DIM))
    inp['hgrn_lb_logits'] = nrm((DEPTH, 2, HGRN_WIDTH))
    inp['hgrn_norm'] = gain((DEPTH, HGRN_DV))
    inp['w_proj_a'] = nrm((DEPTH, FOURIER_WIDTH, D_MODEL), FOURIER_WIDTH ** -0.5)
    inp['w_proj_b'] = nrm((DEPTH, S5_WIDTH, D_MODEL), S5_WIDTH ** -0.5)
    inp['w_proj_c'] = nrm((DEPTH, ATTN_WIDTH, D_MODEL), ATTN_WIDTH ** -0.5)
    inp['w_proj_d'] = nrm((DEPTH, HGRN_VWIDTH, D_MODEL), HGRN_VWIDTH ** -0.5)
    inp['w_out'] = nrm((DEPTH, D_MODEL, D_MODEL), D_MODEL ** -0.5)
    return inp


def reference(x_prompt, x_sample, c, cache_k, cache_v, state_s5_re, state_s5_im, state_hgrn, c_ctx,
              norm_pre, norm_post, w_mod, b_mod, w_in, fourier_w,
              s5_lambda_re, s5_lambda_im, s5_log_step, s5_b_re, s5_b_im, s5_c_re, s5_c_im,
              s5_d, s5_glu_w, s5_glu_b, q_norm, k_norm, hgrn_lb_logits, hgrn_norm,
              w_proj_a, w_proj_b, w_proj_c, w_proj_d, w_out):
    lb_w = jax.nn.softmax(hgrn_lb_logits.astype(jnp.float32), axis=0)
    lower_bounds = jnp.cumsum(lb_w, axis=0) - lb_w[0]

    y_p, y_s = x_prompt, x_sample
    ks, vs, s5r, s5i, hg = [], [], [], [], []
    for l in range(DEPTH):
        p = {'norm_pre': norm_pre[l], 'norm_post': norm_post[l], 'w_in': w_in[l],
             'fourier_w': fourier_w[l],
             's5_lambda_re': s5_lambda_re[l], 's5_lambda_im': s5_lambda_im[l],
             's5_log_step': s5_log_step[l], 's5_b_re': s5_b_re[l], 's5_b_im': s5_b_im[l],
             's5_c_re': s5_c_re[l], 's5_c_im': s5_c_im[l], 's5_d': s5_d[l],
             's5_glu_w': s5_glu_w[l], 's5_glu_b': s5_glu_b[l],
             'q_norm': q_norm[l], 'k_norm': k_norm[l],
             'lower_bound': lower_bounds[l], 'hgrn_norm': hgrn_norm[l],
             'w_proj_a': w_proj_a[l], 'w_proj_b': w_proj_b[l], 'w_proj_c': w_proj_c[l],
             'w_proj_d': w_proj_d[l], 'w_out': w_out[l]}
        mod_ctx = jax.nn.silu(c_ctx) @ w_mod[l] + b_mod[l]
        sh, sc, gt = jnp.split(mod_ctx, 3)
        y_p, (k_l, v_l, sr_l, si_l, hg_l) = trunk_layer(y_p, sh, sc, gt, p, None)
        ks.append(k_l)
        vs.append(v_l)
        s5r.append(sr_l)
        s5i.append(si_l)
        hg.append(hg_l)
        mod = jax.nn.silu(c) @ w_mod[l] + b_mod[l]
        sh, sc, gt = (t[:, None, :] for t in jnp.split(mod, 3, axis=-1))
        ctx = (cache_k[:, l], cache_v[:, l], state_s5_re[:, l], state_s5_im[:, l], state_hgrn[:, l])
        y_s, _ = trunk_layer(y_s, sh, sc, gt, p, ctx)

    new_cache_k = jnp.stack(ks, axis=1)
    new_cache_v = jnp.stack(vs, axis=1)
    new_state_s5_re = jnp.stack(s5r, axis=1)
    new_state_s5_im = jnp.stack(s5i, axis=1)
    new_state_hgrn = jnp.stack(hg, axis=1)
    return (y_p, y_s, new_cache_k, new_cache_v, new_state_s5_re, new_state_s5_im, new_state_hgrn)
```

```python
import numpy as np
import ml_dtypes
from contextlib import ExitStack
import concourse.bass as bass
import concourse.mybir as mybir
from concourse.bass_utils import run_bass_kernel_spmd

F32 = mybir.dt.float32
BF16 = mybir.dt.bfloat16
AF = mybir.ActivationFunctionType
ALU = mybir.AluOpType
AX = mybir.AxisListType

D = 2048
DEPTH = 2
LS = 4096
LP = 256
NTOK = LS + 2 * LP
NT = NTOK // 128
INC = 15360
PAST = 512
NCH = LS // 8 + 2 * (LP // 8)
EPS = 1e-6
NR = 12
NW_TBS = None
RELAX_OWN = True
NW_CBS = None


class Tk:
    __slots__ = ("w", "r", "pend", "name")

    def __init__(self, name=""):
        self.w = {}
        self.r = {}
        self.pend = None
        self.name = name


class KB:
    def __init__(self, nc):
        self.nc = nc
        self.top = ExitStack()
        self.stk = self.top
        self.eng = {}
        for nm, e in (("pe", nc.tensor), ("act", nc.scalar), ("dve", nc.vector),
                      ("pool", nc.gpsimd), ("sp", nc.sync)):
            sem = self.top.enter_context(nc.semaphore("p_" + nm))
            self.eng[nm] = dict(e=e, sem=sem, n=0, seen={}, pend=[])
        self.rings = {}
        for q in ("sp", "pool", "act"):
            sems = [self.top.enter_context(nc.semaphore(f"d_{q}{i}")) for i in range(NR)]
            self.rings[q] = dict(sems=sems, cnt=[0] * NR, i=0)
        self.uid = 0
        self.ninst = 0
        self.debug = None
        self.dumped = set()

    def sb(self, shape, dt, name="t"):
        self.uid += 1
        t = self.stk.enter_context(self.nc.sbuf_tensor(f"{name}_{self.uid}", list(shape), dt))
        return t, Tk(name)

    def ps(self, shape, dt, name="p"):
        self.uid += 1
        t = self.stk.enter_context(self.nc.psum_tensor(f"{name}_{self.uid}", list(shape), dt))
        return t, Tk(name)

    def _waits(self, en, R, W, Wp):
        E = self.eng[en]
        need = {}

        def add(d, skip_own=False):
            for sem, (val, owner) in d.items():
                if owner == en and (en == "pe" or skip_own):
                    continue
                if need.get(sem, (0, None))[0] < val:
                    need[sem] = (val, owner)
        for t in R:
            assert t.pend in (None, en), (t.name, t.pend, en)
            add(t.w)
        for t in W:
            assert t.pend in (None, en), (t.name, t.pend, en)
            add(t.w, RELAX_OWN)
            add(t.r, RELAX_OWN)
        for t in Wp:
            add(t.r)
        for sem, (val, owner) in need.items():
            if E["seen"].get(sem, 0) >= val:
                continue
            E["e"].wait_ge(sem, val)
            self.ninst += 1
            E["seen"][sem] = val

    def op(self, en, fn, R=(), W=(), inc=True):
        self._waits(en, R, W, ())
        E = self.eng[en]
        ins = fn(E["e"])
        self.ninst += 1
        if inc:
            E["n"] += 1
            ins.then_inc(E["sem"], 1)
            tk = (E["n"], en)
            for (t, kind) in E["pend"]:
                if kind == "r":
                    t.r[E["sem"]] = tk
                else:
                    t.w[E["sem"]] = tk
                t.pend = None
            E["pend"] = []
            for t in R:
                t.r[E["sem"]] = tk
            for t in W:
                t.w = {E["sem"]: tk}
                t.r = {}
        else:
            for t in R:
                E["pend"].append((t, "r"))
                t.pend = en
            for t in W:
                t.w = {}
                t.r = {}
                E["pend"].append((t, "w"))
                t.pend = en
        return ins

    def dma(self, q, out, in_, R=(), W=(), Wp=(), slow=False):
        self._waits(q, R, W, Wp)
        E = self.eng[q]
        ring = self.rings[q]
        i = ring["i"]
        ring["i"] = (i + 1) % NR
        sem = ring["sems"][i]
        if ring["cnt"][i] > 0 and E["seen"].get(sem, 0) < ring["cnt"][i]:
            E["e"].wait_ge(sem, ring["cnt"][i])
            E["seen"][sem] = ring["cnt"][i]
            self.ninst += 1
        if slow:
            E["e"].dma_start(out=out, in_=in_, allow_slow_non_contiguous=True).then_inc(sem, 16)
        else:
            E["e"].dma_start(out=out, in_=in_).then_inc(sem, 16)
        self.ninst += 1
        ring["cnt"][i] += 16
        tk = (ring["cnt"][i], None)
        for t in R:
            t.r[sem] = tk
        for t in W:
            t.w = {sem: tk}
            t.r = {}
        for t in Wp:
            t.w[sem] = tk

    def dump(self, name, ap, tk, shape, dt=F32):
        if not self.debug or name not in self.debug or name in self.dumped:
            return
        self.dumped.add(name)
        d = self.nc.dram_tensor(name, list(shape), dt, kind="ExternalOutput").ap()
        self.dma("sp", d, ap, R=[tk])

    def barrier(self):
        for en, E in self.eng.items():
            assert not E["pend"], en
        for en, E in self.eng.items():
            for on, O in self.eng.items():
                if on != en and O["n"] > E["seen"].get(O["sem"], 0):
                    E["e"].wait_ge(O["sem"], O["n"])
                    E["seen"][O["sem"]] = O["n"]
                    self.ninst += 1
            for q, ring in self.rings.items():
                for sem, c in zip(ring["sems"], ring["cnt"]):
                    if c > E["seen"].get(sem, 0):
                        E["e"].wait_ge(sem, c)
                        E["seen"][sem] = c
                        self.ninst += 1


class Rot:
    def __init__(self, kb, n, shape, dt, name, psum=False):
        self.items = [(kb.ps if psum else kb.sb)(shape, dt, name) for _ in range(n)]
        self.i = 0

    def next(self):
        it = self.items[self.i]
        self.i = (self.i + 1) % len(self.items)
        return it


BLK = {0: ("fm", "ua"), 1: ("fm", "ga"), 2: ("fs", "ub"), 3: ("fs", "gb"), 4: ("tm", "q0"), 5: ("tm", "q1"),
       6: ("tm", "kv"), 7: ("fm", "gc0"), 8: ("fm", "gc1"), 9: ("fm", "qd"), 10: ("tm", "id"),
       11: ("fm", "zf"), 12: ("fm", "zb"), 13: ("fm", "gd")}
for _b in range(14, 30):
    BLK[_b] = ("fm", "m")


def tok_src(P, g):
    if g < 32:
        return P["xs_cur"][g * 128:(g + 1) * 128, :]
    return P["xp_cur"][(g - 32) * 128:(g - 31) * 128, :]


def build(debug=None, nlayers=DEPTH, stages=None):
    nc = bass.Bass("TRN2", target_bir_lowering=False)
    kb = KB(nc)
    kb.debug = debug
    P = {}

    def din(name, shape, dt=F32):
        P[name] = nc.dram_tensor(name, list(shape), dt, kind="ExternalInput").ap()
        return P[name]

    def dout(name, shape, dt=F32):
        P[name] = nc.dram_tensor(name, list(shape), dt, kind="ExternalOutput").ap()
        return P[name]

    def dscr(name, shape, dt):
        kind = "ExternalOutput" if (debug and name in debug) else "Internal"
        P[name] = nc.dram_tensor(name, list(shape), dt, kind=kind).ap()
        return P[name]

    din("x_s", [LS, D]); din("x_p", [2 * LP, D]); din("c", [1, D]); din("c_ctx", [D])
    din("cache_k", [DEPTH, PAST, 2, 128]); din("cache_v", [DEPTH, PAST, 2, 128])
    din("s5_h0_re", [DEPTH, 2, 32, 64]); din("s5_h0_im", [DEPTH, 2, 32, 64])
    din("hg_s0", [DEPTH, 2, 4, 128, 128])
    din("norm_pre", [DEPTH, D]); din("norm_post", [DEPTH, D])
    din("w_mod", [DEPTH, D, 3 * D]); din("b_mod", [DEPTH, 3 * D]); din("w_in", [DEPTH, D, INC])
    din("fourier_w", [DEPTH, 4, 128, 128])
    for n in ("s5_lambda_re", "s5_lambda_im"):
        din(n, [DEPTH, 2, 32, 64])
    din("s5_log_step", [DEPTH, 2, 32])
    for n in ("s5_b_re", "s5_b_im"):
        din(n, [DEPTH, 2, 32, 64, 16])
    for n in ("s5_c_re", "s5_c_im"):
        din(n, [DEPTH, 2, 32, 16, 64])
    din("s5_d", [DEPTH, 512]); din("s5_glu_w", [DEPTH, 512, 512]); din("s5_glu_b", [DEPTH, 512])
    din("q_norm", [DEPTH, 128]); din("k_norm", [DEPTH, 128]); din("hgrn_lb_logits", [DEPTH, 2, 512])
    din("hgrn_norm", [DEPTH, 128])
    din("w_proj_a", [DEPTH, 512, D]); din("w_proj_b", [DEPTH, 512, D]); din("w_proj_c", [DEPTH, 1024, D])
    din("w_proj_d", [DEPTH, 512, D]); din("w_out", [DEPTH, D, D])
    din("ident", [128, 128])
    din("ropeC", [LS, 640]); din("ropeS", [LS, 640]); din("hmask", [2, 64, 64]); din("tmask", [2, 128, 128])
    din("dft128", [2, 128, 128]); din("dftS", [2, LS, LS], BF16); din("dftP", [2, LP, LP], BF16)
    dout("y_s", [LS, D]); dout("y_p", [2 * LP, D])
    dout("nk", [2, DEPTH, LP, 2, 128]); dout("nv", [2, DEPTH, LP, 2, 128])
    dout("ns5re", [2, DEPTH, 2, 32, 64]); dout("ns5im", [2, DEPTH, 2, 32, 64])
    dout("nhg", [2, DEPTH, 2, 4, 128, 128])
    dscr("x1_s", [LS, D], F32); dscr("x1_p", [2 * LP, D], F32)
    dscr("gpD", [2, D], F32)
    dscr("uaT", [512, NTOK], BF16); dscr("gaT", [512, NTOK], BF16)
    dscr("UD", [512, 8, NCH], BF16); dscr("gbP", [512, 8, NCH], BF16)
    dscr("qc", [NTOK, 1024], F32); dscr("kvc", [NTOK, 512], F32)
    dscr("gcT", [1024, NTOK], BF16)
    dscr("qdT", [512, NTOK], F32); dscr("idm", [NTOK, 512], BF16)
    dscr("zfT", [512, NTOK], F32); dscr("zbT", [512, NTOK], F32); dscr("gdT", [512, NTOK], BF16)
    dscr("mT", [8192, NTOK], BF16)
    dscr("ygT", [2560, NTOK], BF16); dscr("mixD", [D, NTOK], BF16)
    dscr("wpB", [2560, D], BF16); dscr("woB", [D, D], BF16)
    dscr("s5par", [2, 34, 2048], F32); dscr("YD", [512, 8, NCH], F32)
    TK = {n: Tk(n) for n in P}

    ident, ident_k = kb.sb([128, 128], F32, "ident")
    kb.dma("sp", ident[:], P["ident"][:, :], W=[ident_k])
    scT, scT_k = kb.sb([128, 16, 2], BF16, "scT")
    modS, modS_k = kb.sb([128, 48, 2], F32, "modS")
    gs, gs_k = kb.sb([128, 16, 2], F32, "gs")

    with ExitStack() as st:
        kb.stk = st
        cT, cT_k = kb.sb([128, 16, 2], F32, "cT")
        kb.dma("sp", cT[:, :, 0], P["c"].rearrange("o (k p) -> p (o k)", p=128), W=[cT_k], slow=True)
        kb.dma("sp", cT[:, :, 1], P["c_ctx"].rearrange("(k p) -> p k", p=128), Wp=[cT_k], slow=True)
        kb.op("act", lambda e: e.activation(out=scT[:], in_=cT[:], func=AF.Silu), R=[cT_k], W=[scT_k])
        kb.barrier()
    kb.stk = kb.top

    def want(s):
        return stages is None or s in stages

    for l in range(nlayers):
        P["xs_cur"] = P["x_s"] if l == 0 else P["x1_s"]
        P["xp_cur"] = P["x_p"] if l == 0 else P["x1_p"]
        if want("mod"):
            stage_mod(kb, P, TK, l, scT, scT_k, modS, modS_k, gs, gs_k)
        if want("p"):
            r0 = 0
            for nm, nr in (("w_proj_a", 512), ("w_proj_b", 512), ("w_proj_c", 1024), ("w_proj_d", 512)):
                for rr in range(0, nr, 512):
                    kb.dma("pool", P["wpB"][r0 + rr:r0 + rr + 512, :], P[nm][l, rr:rr + 512, :], Wp=[TK["wpB"]])
                r0 += nr
            for rr in range(0, D, 512):
                kb.dma("pool", P["woB"][rr:rr + 512, :], P["w_out"][l, rr:rr + 512, :], Wp=[TK["woB"]])
        if want("nw"):
            stage_nw(kb, P, TK, l, ident, ident_k, modS, modS_k, gs, gs_k)
        if want("fourier"):
            stage_fourier(kb, P, TK, l)
        if want("attn"):
            stage_attn(kb, P, TK, l, ident, ident_k)
        if want("hgrn"):
            stage_hgrn(kb, P, TK, l, ident, ident_k)
        if want("s5"):
            stage_s5(kb, P, TK, l, ident, ident_k)
        if want("p"):
            stage_p(kb, P, TK, l, l == nlayers - 1)

    kb.barrier()
    kb.top.close()
    return nc, kb


def stage_mod(kb, P, TK, l, scT, scT_k, modS, modS_k, gs, gs_k):
    with ExitStack() as st:
        kb.stk = st
        bm, bm_k = kb.sb([128, 48], F32, "bm")
        npre, npre_k = kb.sb([128, 16], F32, "npre")
        npost, npost_k = kb.sb([128, 16], F32, "npost")
        gp, gp_k = kb.sb([128, 16, 2], F32, "gp")
        kb.dma("sp", bm[:], P["b_mod"][l].rearrange("(j p) -> p j", p=128), W=[bm_k], slow=True)
        kb.dma("sp", npre[:], P["norm_pre"][l].rearrange("(j p) -> p j", p=128), W=[npre_k], slow=True)
        kb.dma("sp", npost[:], P["norm_post"][l].rearrange("(j p) -> p j", p=128), W=[npost_k], slow=True)
        wrot = Rot(kb, 3, [128, 16, 512], BF16, "wm")
        psm, psm_k = kb.ps([128, 48, 2], F32, "psm")
        wsrc = P["w_mod"][l].rearrange("(k p) c -> p k c", p=128)
        for cb in range(12):
            w, w_k = wrot.next()
            kb.dma("pool", w[:], wsrc[:, :, cb * 512:(cb + 1) * 512], W=[w_k])
            for j in range(4):
                for k in range(16):
                    kb.op("pe", lambda e, w=w, j=j, k=k, cb=cb: e.matmul(
                        psm[:, cb * 4 + j, :], lhsT=w[:, k, j * 128:(j + 1) * 128], rhs=scT[:, k, :],
                        start=(k == 0), stop=(k == 15)),
                        R=[w_k, scT_k], W=[psm_k], inc=(k == 15))
        for r in range(2):
            kb.op("dve", lambda e, r=r: e.tensor_tensor(out=modS[:, :, r], in0=psm[:, :, r], in1=bm[:], op=ALU.add),
                  R=[psm_k, bm_k], W=[modS_k])
        for r in range(2):
            kb.op("dve", lambda e, r=r: e.scalar_tensor_tensor(
                out=gs[:, :, r], in0=modS[:, 16:32, r], scalar=1.0, in1=npre[:], op0=ALU.add, op1=ALU.mult),
                R=[modS_k, npre_k], W=[gs_k])
            kb.op("dve", lambda e, r=r: e.tensor_tensor(out=gp[:, :, r], in0=modS[:, 32:48, r], in1=npost[:], op=ALU.mult),
                  R=[modS_k, npost_k], W=[gp_k])
        for r in range(2):
            kb.dma("sp", P["gpD"][r].rearrange("(k p) -> p k", p=128), gp[:, :, r], R=[gp_k], Wp=[TK["gpD"]], slow=True)
        kb.barrier()
    kb.stk = kb.top


def stage_nw(kb, P, TK, l, ident, ident_k, modS, modS_k, gs, gs_k, TBS=((0, 2560), (2560, 2048))):
    nc = kb.nc
    TB = max(n for _, n in TBS)
    tpb_max = TB // 128
    with ExitStack() as st:
        kb.stk = st
        hT, hT_k = kb.sb([128, 16, TB], BF16, "hT")
        hT_tk = [Tk(f"hT{i}") for i in range(tpb_max)]
        xrot = Rot(kb, 2, [128, D], F32, "xt")
        xsrot = Rot(kb, 2, [128, D], F32, "xs")
        strot = Rot(kb, 2, [128, 4], F32, "stat")
        mhalf, mhalf_k = kb.sb([128, 1], F32, "mhalf")
        kb.op("dve", lambda e: e.memset(mhalf[:], -0.5), W=[mhalf_k])
        wrot = Rot(kb, 3, [128, 16, 512], BF16, "win")
        pT = Rot(kb, 2, [128, 4, 128], F32, "pT", psum=True)
        pM = Rot(kb, 4, [128, 512], F32, "pM", psum=True)
        so32 = Rot(kb, 3, [128, 512], F32, "so32")
        so16 = Rot(kb, 3, [128, 512], BF16, "so16")
        wsrc = P["w_in"][l].rearrange("(k p) c -> p k c", p=128)
        evi = [0]

        for tb in (NW_TBS if NW_TBS is not None else range(len(TBS))):
            tok0, TBn = TBS[tb]
            tpb = TBn // 128
            for tt in range(tpb):
                g = tok0 // 128 + tt
                r = 0 if g < 32 else 1
                xt, xt_k = xrot.next()
                xs, xs_k = xsrot.next()
                stt, stt_k = strot.next()
                kb.dma("sp", xt[:], tok_src(P, g), W=[xt_k])
                kb.op("act", lambda e, xs=xs, xt=xt, stt=stt: e.activation(
                    out=xs[:], in_=xt[:], func=AF.Square, accum_out=stt[:, 0:1]), R=[xt_k], W=[xs_k, stt_k])
                kb.op("dve", lambda e, stt=stt: e.tensor_scalar(
                    out=stt[:, 1:2], in0=stt[:, 0:1], scalar1=1.0 / D, scalar2=EPS, op0=ALU.mult, op1=ALU.add),
                    R=[stt_k], W=[stt_k])
                kb.op("pool", lambda e, stt=stt: e.tensor_tensor(
                    out=stt[:, 2:3], in0=stt[:, 1:2], in1=mhalf[:], op=ALU.pow), R=[stt_k, mhalf_k], W=[stt_k])
                kb.op("dve", lambda e, xs=xs, xt=xt, stt=stt: e.tensor_scalar(
                    out=xs[:], in0=xt[:], scalar1=stt[:, 2:3], scalar2=None, op0=ALU.mult),
                    R=[xt_k, stt_k], W=[xs_k])
                for q4 in range(4):
                    pt, pt_k = pT.next()
                    for i in range(4):
                        k = q4 * 4 + i
                        kb.op("pe", lambda e, pt=pt, i=i, k=k, xs=xs: e.transpose(
                            pt[:, i, :], xs[:, k * 128:(k + 1) * 128], ident[:]),
                            R=[xs_k, ident_k], W=[pt_k], inc=(i == 3))
                    for i in range(4):
                        k = q4 * 4 + i
                        kb.op("dve", lambda e, pt=pt, i=i, k=k, tt=tt, r=r: e.tensor_scalar(
                            out=hT[:, k, tt * 128:(tt + 1) * 128], in0=pt[:, i, :],
                            scalar1=gs[:, k, r:r + 1], scalar2=modS[:, k, r:r + 1], op0=ALU.mult, op1=ALU.add),
                            R=[pt_k, gs_k, modS_k], W=[hT_tk[tt]])
            for cb in (NW_CBS if NW_CBS is not None else range(30)):
                kind, nm = BLK[cb]
                w, w_k = wrot.next()
                kb.dma("pool", w[:], wsrc[:, :, cb * 512:(cb + 1) * 512], W=[w_k])
                if kind == "tm":
                    for tt in range(tpb):
                        ps, ps_k = pM.next()
                        for k in range(16):
                            kb.op("pe", lambda e, ps=ps, k=k, tt=tt, w=w: e.matmul(
                                ps[:], lhsT=hT[:, k, tt * 128:(tt + 1) * 128], rhs=w[:, k, :],
                                start=(k == 0), stop=(k == 15)),
                                R=[w_k, hT_tk[tt]], W=[ps_k], inc=(k == 15))
                        t0 = tok0 + tt * 128
                        if nm in ("q0", "q1"):
                            o, o_k = so32.next()
                            kb.op("dve", lambda e, o=o, ps=ps: e.tensor_copy(out=o[:], in_=ps[:]), R=[ps_k], W=[o_k])
                            c0 = 0 if nm == "q0" else 512
                            kb.dma("sp", P["qc"][t0:t0 + 128, c0:c0 + 512], o[:], R=[o_k], Wp=[TK["qc"]])
                        elif nm == "kv":
                            o, o_k = so32.next()
                            kb.op("dve", lambda e, o=o, ps=ps: e.tensor_copy(out=o[:], in_=ps[:]), R=[ps_k], W=[o_k])
                            kb.dma("sp", P["kvc"][t0:t0 + 128, :], o[:], R=[o_k], Wp=[TK["kvc"]])
                        else:
                            o2, o2_k = so16.next()
                            kb.op("dve", lambda e, o2=o2, ps=ps: e.tensor_copy(out=o2[:], in_=ps[:]), R=[ps_k], W=[o2_k])
                            kb.dma("sp", P["idm"][t0:t0 + 128, :], o2[:], R=[o2_k], Wp=[TK["idm"]])
                elif kind == "fm":
                    for j in range(4):
                        for t5 in range(TBn // 512):
                            ps, ps_k = pM.next()
                            for k in range(16):
                                kb.op("pe", lambda e, ps=ps, k=k, j=j, t5=t5, w=w: e.matmul(
                                    ps[:], lhsT=w[:, k, j * 128:(j + 1) * 128], rhs=hT[:, k, t5 * 512:(t5 + 1) * 512],
                                    start=(k == 0), stop=(k == 15)),
                                    R=[w_k] + hT_tk[t5 * 4:(t5 + 1) * 4], W=[ps_k], inc=(k == 15))
                            t0 = tok0 + t5 * 512
                            if nm in ("qd", "zf", "zb"):
                                o, o_k = so32.next()
                                kb.op("dve", lambda e, o=o, ps=ps: e.tensor_copy(out=o[:], in_=ps[:]), R=[ps_k], W=[o_k])
                                dst = {"qd": "qdT", "zf": "zfT", "zb": "zbT"}[nm]
                                kb.dma("sp", P[dst][j * 128:(j + 1) * 128, t0:t0 + 512], o[:], R=[o_k], Wp=[TK[dst]])
                            elif nm == "ua":
                                o2, o2_k = so16.next()
                                kb.op("dve", lambda e, o2=o2, ps=ps: e.tensor_copy(out=o2[:], in_=ps[:]), R=[ps_k], W=[o2_k])
                                kb.dma("sp", P["uaT"][j * 128:(j + 1) * 128, t0:t0 + 512], o2[:], R=[o2_k], Wp=[TK["uaT"]])
                            else:
                                o2, o2_k = so16.next()
                                fn = AF.Sigmoid if nm == "m" else AF.Silu
                                kb.op("act", lambda e, o2=o2, ps=ps, fn=fn: e.activation(out=o2[:], in_=ps[:], func=fn), R=[ps_k], W=[o2_k])
                                if nm == "m":
                                    row0 = (cb - 14) * 512 + j * 128
                                    dst = "mT"
                                else:
                                    dst = {"ga": "gaT", "gc0": "gcT", "gc1": "gcT", "gd": "gdT"}[nm]
                                    row0 = j * 128 + (512 if nm == "gc1" else 0)
                                kb.dma("sp", P[dst][row0:row0 + 128, t0:t0 + 512], o2[:], R=[o2_k], Wp=[TK[dst]])
                else:
                    for j in range(4):
                        for t5 in range(TBn // 512):
                            ps, ps_k = pM.next()
                            for k in range(16):
                                kb.op("pe", lambda e, ps=ps, k=k, j=j, t5=t5, w=w: e.matmul(
                                    ps[:], lhsT=w[:, k, j * 128:(j + 1) * 128], rhs=hT[:, k, t5 * 512:(t5 + 1) * 512],
                                    start=(k == 0), stop=(k == 15)),
                                    R=[w_k] + hT_tk[t5 * 4:(t5 + 1) * 4], W=[ps_k], inc=(k == 15))
                            c0 = (tok0 + t5 * 512) // 8
                            o2, o2_k = so16.next()
                            o2v = o2[:].rearrange("p (t c) -> p t c", t=8)
                            psv = ps[:].rearrange("p (c t) -> p t c", t=8)
                            if nm == "ub":
                                kb.op("dve", lambda e, o2v=o2v, psv=psv: e.tensor_copy(out=o2v, in_=psv), R=[ps_k], W=[o2_k])
                                dst = "UD"
                            else:
                                kb.op("act", lambda e, o2v=o2v, psv=psv: e.activation(out=o2v, in_=psv, func=AF.Silu), R=[ps_k], W=[o2_k])
                                dst = "gbP"
                            kb.dma("sp", P[dst][j * 128:(j + 1) * 128, :, c0:c0 + 64],
                                   o2[:].rearrange("p (t c) -> p t c", t=8), R=[o2_k], Wp=[TK[dst]])
        kb.barrier()
    kb.stk = kb.top


def stage_fourier(kb, P, TK, l):
    with ExitStack() as st:
        kb.stk = st
        dc, dc_k = kb.sb([128, 2, 128], F32, "dc")
        fw, fw_k = kb.sb([128, 4, 128], F32, "fw")
        w12, w12_k = kb.sb([128, 4, 256], BF16, "w12")
        kb.dma("sp", dc[:], P["dft128"].rearrange("s p c -> p s c"), W=[dc_k])
        kb.dma("sp", fw[:], P["fourier_w"][l].rearrange("g p c -> p g c"), W=[fw_k])
        pw = Rot(kb, 2, [128, 512], F32, "pw", psum=True)
        for g in range(4):
            ps, ps_k = pw.next()
            for s_ in range(2):
                kb.op("pe", lambda e, ps=ps, g=g, s_=s_: e.matmul(
                    ps[:, s_ * 128:(s_ + 1) * 128], lhsT=dc[:, s_, :], rhs=fw[:, g, :], start=True, stop=True),
                    R=[dc_k, fw_k], W=[ps_k], inc=(s_ == 1))
            kb.op("dve", lambda e, ps=ps, g=g: e.tensor_copy(out=w12[:, g, 0:128], in_=ps[:, 0:128]), R=[ps_k], W=[w12_k])
            kb.op("dve", lambda e, ps=ps, g=g: e.tensor_scalar(
                out=w12[:, g, 128:256], in0=ps[:, 128:256], scalar1=-1.0, scalar2=None, op0=ALU.mult), R=[ps_k], W=[w12_k])
        a12, a12_k = kb.sb([128, 32, 4, 256], BF16, "a12")
        urot = Rot(kb, 3, [128, 4, 128], BF16, "ua")
        crot = Rot(kb, 2, [128, 8, 512], BF16, "dC")
        srot = Rot(kb, 2, [128, 8, 512], BF16, "dS")
        grot = Rot(kb, 3, [128, 512], BF16, "gt")
        orot = Rot(kb, 3, [128, 512], BF16, "yo")
        pacc = Rot(kb, 4, [128, 512], F32, "pacc", psum=True)
        for (tok0, L, dname) in ((0, LS, "dftS"), (LS, LP, "dftP"), (LS + LP, LP, "dftP")):
            nti = L // 128
            for i in range(nti):
                u, u_k = urot.next()
                t0 = tok0 + i * 128
                kb.dma("sp", u[:], P["uaT"][:, t0:t0 + 128].rearrange("(g c) t -> c g t", g=4), R=[TK["uaT"]], W=[u_k])
                ps, ps_k = pw.next()
                for g in range(4):
                    kb.op("pe", lambda e, ps=ps, g=g, u=u: e.matmul(
                        ps[:, 0:256] if False else ps[:, 0:256], lhsT=u[:, g, :], rhs=w12[:, g, :], start=True, stop=True),
                        R=[u_k, w12_k], W=[ps_k])
                    kb.op("dve", lambda e, ps=ps, g=g, i=i: e.tensor_copy(out=a12[:, i, g, :], in_=ps[:, 0:256]),
                          R=[ps_k], W=[a12_k])
            nb = min(512, L)
            igs = max(1, nti // 8)
            per = min(8, nti)
            for nblk in range(L // nb):
                accs = [pacc.next() for _ in range(4)]
                for ig in range(igs):
                    C, C_k = crot.next()
                    S, S_k = srot.next()
                    r0 = ig * per * 128
                    kb.dma("sp", C[:, 0:per, 0:nb], P[dname][0, r0:r0 + per * 128, nblk * nb:(nblk + 1) * nb].rearrange("(i p) n -> p i n", p=128), W=[C_k])
                    kb.dma("sp", S[:, 0:per, 0:nb], P[dname][1, r0:r0 + per * 128, nblk * nb:(nblk + 1) * nb].rearrange("(i p) n -> p i n", p=128), W=[S_k])
                    for g in range(4):
                        ps, ps_k = accs[g]
                        for i in range(per):
                            ti = ig * per + i
                            first = (ig == 0 and i == 0)
                            last = (ig == igs - 1 and i == per - 1)
                            kb.op("pe", lambda e, ps=ps, g=g, ti=ti, i=i, C=C, first=first: e.matmul(
                                ps[:, 0:nb], lhsT=a12[:, ti, g, 0:128], rhs=C[:, i, 0:nb], start=first, stop=False),
                                R=[a12_k, C_k], W=[ps_k], inc=False)
                            kb.op("pe", lambda e, ps=ps, g=g, ti=ti, i=i, S=S, last=last: e.matmul(
                                ps[:, 0:nb], lhsT=a12[:, ti, g, 128:256], rhs=S[:, i, 0:nb], start=False, stop=last),
                                R=[a12_k, S_k], W=[ps_k], inc=(i == per - 1))
                n0 = tok0 + nblk * nb
                for g in range(4):
                    ps, ps_k = accs[g]
                    gt, gt_k = grot.next()
                    o, o_k = orot.next()
                    kb.dma("sp", gt[:, 0:nb], P["gaT"][g * 128:(g + 1) * 128, n0:n0 + nb], R=[TK["gaT"]], W=[gt_k])
                    kb.op("dve", lambda e, ps=ps, gt=gt, o=o: e.tensor_tensor(out=o[:, 0:nb], in0=ps[:, 0:nb], in1=gt[:, 0:nb], op=ALU.mult),
                          R=[ps_k, gt_k], W=[o_k])
                    kb.dma("sp", P["ygT"][g * 128:(g + 1) * 128, n0:n0 + nb], o[:, 0:nb], R=[o_k], Wp=[TK["ygT"]])
        kb.barrier()
    kb.stk = kb.top


def stage_attn(kb, P, TK, l, ident, ident_k):
    SCALE = 128.0 ** -0.5
    with ExitStack() as st:
        kb.stk = st
        identb, identb_k = kb.sb([128, 128], BF16, "identb")
        kb.op("dve", lambda e: e.tensor_copy(out=identb[:], in_=ident[:]), R=[ident_k], W=[identb_k])
        G, G_k = kb.sb([128, 10, 128], F32, "G")
        kb.dma("sp", G[:, 0, :], P["q_norm"][l].partition_broadcast(128), W=[G_k])
        kb.dma("sp", G[:, 8, :], P["k_norm"][l].partition_broadcast(128), Wp=[G_k])
        for h in list(range(1, 8)) + [9]:
            kb.op("dve", lambda e, h=h: e.tensor_copy(out=G[:, h, :], in_=G[:, 0 if h < 8 else 8, :]), R=[G_k], W=[G_k])
        mh10, mh10_k = kb.sb([128, 10], F32, "mh10")
        kb.op("dve", lambda e: e.memset(mh10[:], -0.5), W=[mh10_k])
        QT, QT_k = kb.sb([128, 8, LS], BF16, "QT")
        KT, KT_k = kb.sb([128, 2, LS + PAST], BF16, "KT")
        VA, VA_k = kb.sb([128, 36, 2, 130], BF16, "VA")
        kb.op("dve", lambda e: e.memset(VA[:, :, :, 128:130], 1.0), W=[VA_k])
        qkrot = Rot(kb, 2, [128, 10, 128], F32, "qk")
        vrot = Rot(kb, 2, [128, 2, 128], F32, "vt")
        tmprot = Rot(kb, 2, [128, 10, 128], F32, "tmp")
        qnrot = Rot(kb, 2, [128, 10, 128], F32, "qn")
        strot = Rot(kb, 2, [128, 32], F32, "st")
        csrot = Rot(kb, 2, [128, 2, 640], F32, "cs")
        r4rot = Rot(kb, 2, [128, 4, 640], F32, "r4")
        qbrot = Rot(kb, 2, [128, 10, 128], BF16, "qb")
        ptq = Rot(kb, 1, [128, 8, 128], BF16, "ptq", psum=True)
        ptk = Rot(kb, 1, [128, 8, 128], BF16, "ptk", psum=True)
        pst = Rot(kb, 3, [128, 512], F32, "pst", psum=True)
        pacc = Rot(kb, 3, [128, 2, 130], F32, "pacc", psum=True)
        ptrot = Rot(kb, 4, [128, 512], BF16, "PT")
        recrot = Rot(kb, 4, [128, 1], F32, "rec")
        obrot = Rot(kb, 4, [128, 128], BF16, "ob")
        gtrot = Rot(kb, 2, [128, 512], BF16, "gt")
        yorot = Rot(kb, 2, [128, 512], BF16, "yo")

        def prep_tile(t0, tl, slot, rope_row, pb):
            qk, qk_k = qkrot.next(); vt, vt_k = vrot.next(); tmp, tmp_k = tmprot.next()
            qn, qn_k = qnrot.next(); stt, stt_k = strot.next(); qb, qb_k = qbrot.next()
            kb.dma("sp", qk[:, 0:8, :], P["qc"][t0:t0 + 128, :].rearrange("t (h d) -> t h d", h=8), R=[TK["qc"]], W=[qk_k])
            kb.dma("sp", qk[:, 8:10, :], P["kvc"][t0:t0 + 128, 0:256].rearrange("t (h d) -> t h d", h=2), R=[TK["kvc"]], Wp=[qk_k])
            kb.dma("sp", vt[:], P["kvc"][t0:t0 + 128, 256:512].rearrange("t (h d) -> t h d", h=2), R=[TK["kvc"]], W=[vt_k])
            kb.op("dve", lambda e: e.tensor_tensor(out=tmp[:], in0=qk[:], in1=qk[:], op=ALU.mult), R=[qk_k], W=[tmp_k])
            kb.op("dve", lambda e: e.tensor_reduce(out=stt[:, 0:10], in_=tmp[:], axis=AX.X, op=ALU.add), R=[tmp_k], W=[stt_k])
            kb.op("dve", lambda e: e.tensor_scalar(out=stt[:, 10:20], in0=stt[:, 0:10], scalar1=1.0 / 128, scalar2=EPS,
                                                   op0=ALU.mult, op1=ALU.add), R=[stt_k], W=[stt_k])
            kb.op("pool", lambda e: e.tensor_tensor(out=stt[:, 20:30], in0=stt[:, 10:20], in1=mh10[:], op=ALU.pow),
                  R=[stt_k, mh10_k], W=[stt_k])
            kb.op("dve", lambda e: e.tensor_tensor(out=qn[:], in0=qk[:], in1=stt[:, 20:30].unsqueeze(2).to_broadcast([128, 10, 128]), op=ALU.mult),
                  R=[qk_k, stt_k], W=[qn_k])
            kb.op("dve", lambda e: e.tensor_tensor(out=qn[:], in0=qn[:], in1=G[:], op=ALU.mult), R=[qn_k, G_k], W=[qn_k])
            if pb is not None:
                b, tp = pb
                kb.dma("sp", P["nk"][b, l, tp:tp + 128, :, :], qn[:, 8:10, :], R=[qn_k], Wp=[TK["nk"]])
                kb.dma("sp", P["nv"][b, l, tp:tp + 128, :, :], vt[:], R=[vt_k], Wp=[TK["nv"]])
                kb.op("pool", lambda e: e.tensor_copy(out=qb[:], in_=qn[:]), R=[qn_k], W=[qb_k])
            else:
                cs, cs_k = csrot.next(); r4, r4_k = r4rot.next()
                kb.dma("sp", cs[:, 0, :], P["ropeC"][rope_row:rope_row + 128, :], W=[cs_k])
                kb.dma("sp", cs[:, 1, :], P["ropeS"][rope_row:rope_row + 128, :], Wp=[cs_k])
                xv = qn[:].rearrange("p h (a b c) -> p h a b c", a=2, b=2, c=32)
                ov = qb[:].rearrange("p h (a b c) -> p h a b c", a=2, b=2, c=32)
                cv = cs[:, 0, :].rearrange("p (h a c) -> p h a c", h=10, a=2)
                sv = cs[:, 1, :].rearrange("p (h a c) -> p h a c", h=10, a=2)
                rv = [r4[:, i, :].rearrange("p (h a c) -> p h a c", h=10, a=2) for i in range(4)]
                x1, x2 = xv[:, :, :, 0, :], xv[:, :, :, 1, :]
                kb.op("dve", lambda e: e.tensor_tensor(out=rv[0], in0=x1, in1=cv, op=ALU.mult), R=[qn_k, cs_k], W=[r4_k])
                kb.op("dve", lambda e: e.tensor_tensor(out=rv[1], in0=x2, in1=sv, op=ALU.mult), R=[qn_k, cs_k], W=[r4_k])
                kb.op("dve", lambda e: e.tensor_tensor(out=ov[:, :, :, 0, :], in0=rv[0], in1=rv[1], op=ALU.subtract), R=[r4_k], W=[qb_k])
                kb.op("dve", lambda e: e.tensor_tensor(out=rv[2], in0=x2, in1=cv, op=ALU.mult), R=[qn_k, cs_k], W=[r4_k])
                kb.op("dve", lambda e: e.tensor_tensor(out=rv[3], in0=x1, in1=sv, op=ALU.mult), R=[qn_k, cs_k], W=[r4_k])
                kb.op("dve", lambda e: e.tensor_tensor(out=ov[:, :, :, 1, :], in0=rv[2], in1=rv[3], op=ALU.add), R=[r4_k], W=[qb_k])
            pq, pq_k = ptq.next(); pk, pk_k = ptk.next()
            for h in range(8):
                kb.op("pe", lambda e, h=h: e.transpose(pq[:, h, :], qb[:, h, :], identb[:]), R=[qb_k, identb_k], W=[pq_k], inc=(h == 7))
            for h in range(2):
                kb.op("pe", lambda e, h=h: e.transpose(pk[:, h, :], qb[:, 8 + h, :], identb[:]), R=[qb_k, identb_k], W=[pk_k], inc=(h == 1))
            kb.op("act", lambda e: e.copy(out=QT[:, :, tl:tl + 128], in_=pq[:]), R=[pq_k], W=[QT_k])
            kb.op("dve", lambda e: e.tensor_copy(out=KT[:, :, tl:tl + 128], in_=pk[:, 0:2, :]), R=[pk_k], W=[KT_k])
            kb.op("pool", lambda e: e.tensor_copy(out=VA[:, slot, :, 0:128], in_=vt[:]), R=[vt_k], W=[VA_k])

        def ctx_tile(i):
            qk, qk_k = qkrot.next(); vt, vt_k = vrot.next(); qb, qb_k = qbrot.next()
            kb.dma("sp", qk[:, 0:2, :], P["cache_k"][l, i * 128:(i + 1) * 128, :, :], W=[qk_k])
            kb.dma("sp", vt[:], P["cache_v"][l, i * 128:(i + 1) * 128, :, :], W=[vt_k])
            kb.op("pool", lambda e: e.tensor_copy(out=qb[:, 0:2, :], in_=qk[:, 0:2, :]), R=[qk_k], W=[qb_k])
            pk, pk_k = ptk.next()
            for h in range(2):
                kb.op("pe", lambda e, h=h: e.transpose(pk[:, h, :], qb[:, h, :], identb[:]), R=[qb_k, identb_k], W=[pk_k], inc=(h == 1))
            kb.op("dve", lambda e: e.tensor_copy(out=KT[:, :, LS + i * 128:LS + (i + 1) * 128], in_=pk[:, 0:2, :]), R=[pk_k], W=[KT_k])
            kb.op("pool", lambda e: e.tensor_copy(out=VA[:, 32 + i, :, 0:128], in_=vt[:]), R=[vt_k], W=[VA_k])

        def core(tok0, Lq, nkt):
            nq = min(512, Lq)
            nqi = nq // 128
            for h in range(8):
                kvh = h // 4
                for qb_ in range(Lq // nq):
                    accs = [pacc.next() for _ in range((nqi + 1) // 2)]

                    def score(kt):
                        ps, ps_k = pst.next()
                        kb.op("pe", lambda e, ps=ps, kt=kt: e.matmul(
                            ps[:, 0:nq], lhsT=KT[:, kvh, kt * 128:(kt + 1) * 128], rhs=QT[:, h, qb_ * nq:(qb_ + 1) * nq],
                            start=True, stop=True), R=[KT_k, QT_k], W=[ps_k])
                        pt, pt_k = ptrot.next()
                        kb.op("act", lambda e, ps=ps, pt=pt: e.activation(out=pt[:, 0:nq], in_=ps[:, 0:nq], func=AF.Exp, scale=SCALE),
                              R=[ps_k], W=[pt_k])
                        return pt, pt_k

                    def pv(kt, pt, pt_k):
                        for qi in range(nqi):
                            acc, acc_k = accs[qi // 2]
                            kb.op("pe", lambda e, acc=acc, qi=qi, pt=pt, kt=kt: e.matmul(
                                acc[:, qi % 2, :], lhsT=pt[:, qi * 128:(qi + 1) * 128], rhs=VA[:, kt, kvh, :],
                                start=(kt == 0 and qi % 2 == 0), stop=(kt == nkt - 1)), R=[pt_k, VA_k], W=[acc_k], inc=(qi == nqi - 1))

                    LOOK = 2
                    pend = []
                    for kt in range(nkt + LOOK):
                        if kt < nkt:
                            pend.append((kt,) + score(kt))
                        if kt >= LOOK:
                            pv(*pend.pop(0))
                    po, po_k = ptq.next()
                    for qi in range(nqi):
                        acc, acc_k = accs[qi // 2]
                        rec, rec_k = recrot.next(); ob, ob_k = obrot.next()
                        kb.op("dve", lambda e, acc=acc, qi=qi, rec=rec: e.reciprocal(out=rec[:], in_=acc[:, qi % 2, 128:129]), R=[acc_k], W=[rec_k])
                        kb.op("dve", lambda e, acc=acc, qi=qi, rec=rec, ob=ob: e.tensor_scalar(
                            out=ob[:], in0=acc[:, qi % 2, 0:128], scalar1=rec[:, 0:1], scalar2=None, op0=ALU.mult), R=[acc_k, rec_k], W=[ob_k])
                        kb.op("pe", lambda e, qi=qi, ob=ob: e.transpose(po[:, qi, :], ob[:], identb[:]), R=[ob_k, identb_k], W=[po_k])
                    gt, gt_k = gtrot.next(); yo, yo_k = yorot.next()
                    n0 = tok0 + qb_ * nq
                    kb.dma("sp", gt[:, 0:nq], P["gcT"][h * 128:(h + 1) * 128, n0:n0 + nq], R=[TK["gcT"]], W=[gt_k])
                    kb.op("dve", lambda e, gt=gt, yo=yo: e.tensor_tensor(
                        out=yo[:, 0:nq], in0=po[:, 0:nqi, :].rearrange("p a b -> p (a b)"), in1=gt[:, 0:nq], op=ALU.mult),
                        R=[po_k, gt_k], W=[yo_k])
                    kb.dma("sp", P["ygT"][1024 + h * 128:1024 + (h + 1) * 128, n0:n0 + nq], yo[:, 0:nq], R=[yo_k], Wp=[TK["ygT"]])

        for i in range(32):
            prep_tile(i * 128, i * 128, i, i * 128, None)
        for i in range(4):
            ctx_tile(i)
        core(0, LS, 36)
        for b in range(2):
            for i in range(2):
                prep_tile(LS + b * LP + i * 128, i * 128, i, None, (b, i * 128))
            core(LS + b * LP, LP, 2)
        kb.barrier()
    kb.stk = kb.top


def stage_hgrn(kb, P, TK, l, ident, ident_k):
    with ExitStack() as st:
        kb.stk = st
        identb, identb_k = kb.sb([128, 128], BF16, "identb")
        kb.op("dve", lambda e: e.tensor_copy(out=identb[:], in_=ident[:]), R=[ident_k], W=[identb_k])
        mask, mask_k = kb.sb([64, 2, 64], F32, "mask")
        kb.dma("sp", mask[:], P["hmask"].rearrange("d s t -> s d t"), W=[mask_k])
        Gh, Gh_k = kb.sb([64, 128], F32, "Gh")
        kb.dma("sp", Gh[:], P["hgrn_norm"][l].partition_broadcast(64), W=[Gh_k])
        lg, lg_k = kb.sb([128, 2, 2, 4], F32, "lg")
        kb.dma("sp", lg[:], P["hgrn_lb_logits"].rearrange("l d (h p) -> p l d h", p=128), W=[lg_k], slow=True)
        lb, lb_k = kb.sb([128, 2, 4], F32, "lb")
        oml, oml_k = kb.sb([128, 2, 4], F32, "oml")
        if l == 0:
            kb.op("dve", lambda e: e.memset(lb[:], 0.0), W=[lb_k])
        else:
            kb.op("dve", lambda e: e.tensor_tensor(out=lb[:], in0=lg[:, 1, :, :], in1=lg[:, 0, :, :], op=ALU.subtract), R=[lg_k], W=[lb_k])
            kb.op("act", lambda e: e.activation(out=lb[:], in_=lb[:], func=AF.Sigmoid), R=[lb_k], W=[lb_k])
        kb.op("dve", lambda e: e.tensor_scalar(out=oml[:], in0=lb[:], scalar1=-1.0, scalar2=1.0, op0=ALU.mult, op1=ALU.add), R=[lb_k], W=[oml_k])
        mh, mh_k = kb.sb([64, 64], F32, "mh")
        kb.op("dve", lambda e: e.memset(mh[:], -0.5), W=[mh_k])
        one1, one1_k = kb.sb([128, 1], F32, "one1")
        kb.op("dve", lambda e: e.memset(one1[:], 1.0), W=[one1_k])

        qt, qt_k = kb.sb([128, LS], F32, "qt")
        A, A_k = kb.sb([128, LS], F32, "A")
        B, B_k = kb.sb([128, LS], F32, "B")
        C2, C2_k = kb.sb([128, LS], F32, "C2")
        qT, qT_k = kb.sb([128, 2, LS], BF16, "qT")
        kT, kT_k = kb.sb([128, 2, LS], BF16, "kT")
        kH, kH_k = kb.sb([128, 2, LS], BF16, "kH")
        V, V_k = kb.sb([64, LS // 64, 128], BF16, "V")
        onb = A[:].bitcast(BF16)[0:64, :].rearrange("p (c e) -> p c e", e=128)
        onb_k = A_k
        O, O_k = kb.sb([64, LS // 64, 128], F32, "O")
        O2, O2_k = kb.sb([64, LS // 64, 128], F32, "O2")
        stt, stt_k = kb.sb([128, 2, 3, LS // 64], F32, "stt")
        ost, ost_k = kb.sb([64, 3, LS // 64], F32, "ost")
        S = [[kb.sb([128, 128], F32, "S") for _ in range(2)] for _ in range(2)]
        Sbf = [[kb.sb([128, 128], BF16, "Sbf") for _ in range(2)] for _ in range(2)]
        tmpr = None
        ATr = [Rot(kb, 3, [64, 64], BF16, "AT") for _ in range(2)]
        for d_ in range(2):
            for (AT_, ATk_) in ATr[d_].items:
                kb.op("dve", lambda e, AT_=AT_: e.memset(AT_[:], 0.0), W=[ATk_])
        ktr = Rot(kb, 3, [64, 128], BF16, "ktok")
        ps_s = [kb.ps([128, 512], F32, "ps_s") for _ in range(2)]
        for (p_, pk_) in ps_s:
            kb.op("dve", lambda e, p_=p_: e.memset(p_[:], 0.0), W=[pk_])
        ps_t = Rot(kb, 2, [128, 1024], BF16, "ps_t", psum=True)
        ps_o = Rot(kb, 2, [128, 512], F32, "ps_o", psum=True)
        ps_kv = Rot(kb, 2, [128, 512], F32, "ps_kv", psum=True)
        gtr = Rot(kb, 2, [128, 512], BF16, "gt")
        yor = Rot(kb, 2, [128, 512], BF16, "yo")
        kvslots = [(ps_kv.items[i // 4][0][:, (i % 4) * 128:(i % 4 + 1) * 128], Tk("kv%d" % i)) for i in range(8)]

        def seq_head(tok0, L, h, segs):
            ncq = L // 64
            r0 = h * 128
            kb.dma("sp", qt[:, 0:L], P["qdT"][r0:r0 + 128, tok0:tok0 + L], R=[TK["qdT"]], W=[qt_k])
            kb.dma("sp", V[:, 0:ncq, :], P["idm"][tok0:tok0 + L, r0:r0 + 128].rearrange("(c s) e -> s c e", s=64), R=[TK["idm"]], W=[V_k])
            for d in range(2):
                zsrc = "zfT" if d == 0 else "zbT"
                kb.dma("sp", A[:, 0:L], P[zsrc][r0:r0 + 128, tok0:tok0 + L], R=[TK[zsrc]], W=[A_k])
                kb.op("act", lambda e: e.activation(out=A[:, 0:L], in_=A[:, 0:L], func=AF.Sigmoid), R=[A_k], W=[A_k])
                kb.op("act", lambda e, d=d: e.activation(out=A[:, 0:L], in_=A[:, 0:L], func=AF.Identity, scale=oml[:, d, h:h + 1], bias=lb[:, d, h:h + 1]),
                      R=[A_k, oml_k, lb_k], W=[A_k])
                kb.op("dve", lambda e: e.tensor_scalar(out=B[:, 0:L], in0=A[:, 0:L], scalar1=1e-30, scalar2=None, op0=ALU.max), R=[A_k], W=[B_k])
                kb.op("act", lambda e: e.activation(out=B[:, 0:L], in_=B[:, 0:L], func=AF.Ln), R=[B_k], W=[B_k])
                kb.op("act", lambda e: e.activation(out=A[:, 0:L], in_=A[:, 0:L], func=AF.Identity, scale=-1.0, bias=one1[:, 0:1]), R=[A_k, one1_k], W=[A_k])
                src, src_k, dst, dst_k = B, B_k, C2, C2_k
                for sh in (1, 2, 4, 8, 16, 32):
                    sv = src[:, 0:L].rearrange("p (c s) -> p c s", s=64)
                    dv = dst[:, 0:L].rearrange("p (c s) -> p c s", s=64)
                    if d == 0:
                        kb.op("dve", lambda e, sv=sv, dv=dv, sh=sh: e.tensor_tensor(out=dv[:, :, sh:], in0=sv[:, :, sh:], in1=sv[:, :, :64 - sh], op=ALU.add), R=[src_k], W=[dst_k])
                        kb.op("pool", lambda e, sv=sv, dv=dv, sh=sh: e.tensor_copy(out=dv[:, :, :sh], in_=sv[:, :, :sh]), R=[src_k], W=[dst_k])
                    else:
                        kb.op("dve", lambda e, sv=sv, dv=dv, sh=sh: e.tensor_tensor(out=dv[:, :, :64 - sh], in0=sv[:, :, :64 - sh], in1=sv[:, :, sh:], op=ALU.add), R=[src_k], W=[dst_k])
                        kb.op("pool", lambda e, sv=sv, dv=dv, sh=sh: e.tensor_copy(out=dv[:, :, 64 - sh:], in_=sv[:, :, 64 - sh:]), R=[src_k], W=[dst_k])
                    src, src_k, dst, dst_k = dst, dst_k, src, src_k
                cv = B[:, 0:L].rearrange("p (c s) -> p c s", s=64)
                mv = cv[:, :, 32]
                lv = cv[:, :, 63] if d == 0 else cv[:, :, 0]
                kb.op("act", lambda e, d=d, lv=lv: e.activation(out=stt[:, d, 0, 0:ncq], in_=lv, func=AF.Exp), R=[B_k], W=[stt_k])
                kb.op("act", lambda e, d=d, mv=mv: e.activation(out=stt[:, d, 1, 0:ncq], in_=mv, func=AF.Exp), R=[B_k], W=[stt_k])
                kb.op("dve", lambda e, d=d, lv=lv, mv=mv: e.tensor_tensor(out=stt[:, d, 2, 0:ncq], in0=lv, in1=mv, op=ALU.subtract), R=[B_k], W=[stt_k])
                kb.op("act", lambda e, d=d: e.activation(out=stt[:, d, 2, 0:ncq], in_=stt[:, d, 2, 0:ncq], func=AF.Exp), R=[stt_k], W=[stt_k])
                lvb = lv.unsqueeze(2).to_broadcast([128, ncq, 64])
                kb.op("dve", lambda e, lvb=lvb: e.tensor_tensor(out=C2[:, 0:L].rearrange("p (c s) -> p c s", s=64), in0=lvb, in1=cv, op=ALU.subtract), R=[B_k], W=[C2_k])
                kb.op("act", lambda e: e.activation(out=C2[:, 0:L], in_=C2[:, 0:L], func=AF.Exp), R=[C2_k], W=[C2_k])
                kb.op("dve", lambda e, d=d: e.tensor_tensor(out=kH[:, d, 0:L], in0=A[:, 0:L], in1=C2[:, 0:L], op=ALU.mult), R=[A_k, C2_k], W=[kH_k])
                mvb = mv.unsqueeze(2).to_broadcast([128, ncq, 64])
                kb.op("dve", lambda e, mvb=mvb: e.tensor_tensor(out=C2[:, 0:L].rearrange("p (c s) -> p c s", s=64), in0=cv, in1=mvb, op=ALU.subtract), R=[B_k, C2_k], W=[C2_k])
                kb.op("act", lambda e: e.activation(out=B[:, 0:L], in_=C2[:, 0:L], func=AF.Exp), R=[C2_k], W=[B_k])
                kb.op("dve", lambda e, d=d: e.tensor_tensor(out=qT[:, d, 0:L], in0=qt[:, 0:L], in1=B[:, 0:L], op=ALU.mult), R=[qt_k, B_k], W=[qT_k])
                kb.op("act", lambda e: e.activation(out=B[:, 0:L], in_=C2[:, 0:L], func=AF.Exp, scale=-1.0), R=[C2_k], W=[B_k])
                kb.op("dve", lambda e, d=d: e.tensor_tensor(out=kT[:, d, 0:L], in0=A[:, 0:L], in1=B[:, 0:L], op=ALU.mult), R=[A_k, B_k], W=[kT_k])
            def FF(step, d, c):
                c0_ = c * 64
                cs = slice(c0_, c0_ + 64)
                pss, pss_k = ps_s[d]
                AT, AT_k = ATr[d].next()
                if d == 0:
                    kb.op("pe", lambda e: e.matmul(pss[0:32, 0:64], lhsT=kT[:, 0, c0_:c0_ + 32], rhs=qT[:, 0, c0_:c0_ + 64], start=True, stop=True),
                          R=[kT_k, qT_k], W=[pss_k], inc=False)
                    kb.op("pe", lambda e: e.matmul(pss[32:64, 32:64], lhsT=kT[:, 0, c0_ + 32:c0_ + 64], rhs=qT[:, 0, c0_ + 32:c0_ + 64], start=True, stop=True,
                                                   tile_position=(0, 32)), R=[kT_k, qT_k], W=[pss_k])
                else:
                    kb.op("pe", lambda e: e.matmul(pss[32:64, 0:64], lhsT=kT[:, 1, c0_ + 32:c0_ + 64], rhs=qT[:, 1, c0_:c0_ + 64], start=True, stop=True,
                                                   tile_position=(0, 32)), R=[kT_k, qT_k], W=[pss_k], inc=False)
                    kb.op("pe", lambda e: e.matmul(pss[0:32, 0:32], lhsT=kT[:, 1, c0_:c0_ + 32], rhs=qT[:, 1, c0_:c0_ + 32], start=True, stop=True),
                          R=[kT_k, qT_k], W=[pss_k])
                kb.op("dve", lambda e: e.tensor_tensor(out=AT[:], in0=pss[0:64, 0:64], in1=mask[:, d, :], op=ALU.mult), R=[pss_k, mask_k], W=[AT_k])
                pst, pst_k = ps_t.next()
                kb.op("pe", lambda e: e.transpose(pst[0:64, 0:128], kH[:, d, cs], identb[:]), R=[kH_k, identb_k], W=[pst_k])
                kt, kt_k = ktr.next()
                kb.op("act", lambda e: e.copy(out=kt[:], in_=pst[0:64, 0:128]), R=[pst_k], W=[kt_k])
                pkv, pkv_k = ps_kv.next()
                kb.op("pe", lambda e: e.matmul(pkv[:, 0:128], lhsT=kt[:], rhs=V[:, c, :], start=True, stop=True), R=[kt_k, V_k], W=[pkv_k])
                Sp, Sp_k = S[d][step % 2]
                Sn, Sn_k = S[d][(step + 1) % 2]
                Sb, Sb_k = Sbf[d][step % 2]
                kb.op("act", lambda e: e.mul(out=Sb[:], in_=Sp[:], mul=stt[:, d, 1, c:c + 1]), R=[Sp_k, stt_k], W=[Sb_k])
                kb.op("dve", lambda e: e.scalar_tensor_tensor(out=Sn[:], in0=Sp[:], scalar=stt[:, d, 0, c:c + 1], in1=pkv[:, 0:128], op0=ALU.mult, op1=ALU.add),
                      R=[pkv_k, stt_k, Sp_k], W=[Sn_k])
                return (c, AT, AT_k, Sb, Sb_k)

            def ST(step, d, ff):
                c, AT, AT_k, Sb, Sb_k = ff
                cs = slice(c * 64, (c + 1) * 64)
                pso, pso_k = ps_o.next()
                kb.op("pe", lambda e: e.matmul(pso[0:64, 0:128], lhsT=qT[:, d, cs], rhs=Sb[:], start=True, stop=False),
                      R=[qT_k, Sb_k], W=[pso_k], inc=False)
                kb.op("pe", lambda e: e.matmul(pso[0:64, 0:128], lhsT=AT[:], rhs=V[:, c, :], start=False, stop=True),
                      R=[AT_k, V_k], W=[pso_k])
                if d == 0:
                    kb.op("act", lambda e: e.copy(out=O[:, c, :], in_=pso[0:64, 0:128]), R=[pso_k], W=[O_k])
                else:
                    kb.op("act", lambda e: e.copy(out=O2[:, c, :], in_=pso[0:64, 0:128]), R=[pso_k], W=[O2_k])

            for (cbase, cnum, bidx) in segs:
                for d in range(2):
                    Sd, Sd_k = S[d][0]
                    if bidx is None:
                        kb.dma("sp", Sd[:], P["hg_s0"][l, d, h, :, :], W=[Sd_k])
                    else:
                        kb.op("dve", lambda e, Sd=Sd: e.memset(Sd[:], 0.0), W=[Sd_k])
                ffs = [{}, {}]
                for step in range(cnum + 1):
                    if step < cnum:
                        for d in range(2):
                            c = cbase + step if d == 0 else cbase + cnum - 1 - step
                            ffs[d][step] = FF(step, d, c)
                    if step >= 1:
                        for d in range(2):
                            ST(step - 1, d, ffs[d].pop(step - 1))
                if bidx is not None:
                    for d in range(2):
                        kb.dma("sp", P["nhg"][bidx, l, d, h, :, :], S[d][cnum % 2][0][:], R=[S[d][cnum % 2][1]], Wp=[TK["nhg"]])
            kb.op("dve", lambda e: e.tensor_tensor(out=O[:, 0:ncq, :], in0=O[:, 0:ncq, :], in1=O2[:, 0:ncq, :], op=ALU.add), R=[O_k, O2_k], W=[O_k])
            kb.op("dve", lambda e: e.tensor_tensor(out=O2[:, 0:ncq, :], in0=O[:, 0:ncq, :], in1=O[:, 0:ncq, :], op=ALU.mult), R=[O_k], W=[O2_k])
            kb.op("dve", lambda e: e.tensor_reduce(out=ost[:, 0, 0:ncq], in_=O2[:, 0:ncq, :], axis=AX.X, op=ALU.add), R=[O2_k], W=[ost_k])
            kb.op("dve", lambda e: e.tensor_scalar(out=ost[:, 1, 0:ncq], in0=ost[:, 0, 0:ncq], scalar1=1.0 / 128, scalar2=EPS, op0=ALU.mult, op1=ALU.add), R=[ost_k], W=[ost_k])
            kb.op("pool", lambda e: e.tensor_tensor(out=ost[:, 2, 0:ncq], in0=ost[:, 1, 0:ncq], in1=mh[:, 0:ncq], op=ALU.pow), R=[ost_k, mh_k], W=[ost_k])
            for c in range(ncq):
                kb.op("dve", lambda e, c=c: e.scalar_tensor_tensor(out=onb[:, c, :], in0=O[:, c, :], scalar=ost[:, 2, c:c + 1], in1=Gh[:], op0=ALU.mult, op1=ALU.mult),
                      R=[O_k, ost_k, Gh_k], W=[onb_k])
            nblk = max(1, L // 512)
            cpb = ncq // nblk
            for blk in range(nblk):
                pst, pst_k = ps_t.next()
                for i in range(cpb):
                    c = blk * cpb + i
                    kb.op("pe", lambda e, c=c, i=i, pst=pst: e.transpose(pst[:, i * 64:(i + 1) * 64], onb[:, c, :], identb[0:64, 0:64]),
                          R=[onb_k, identb_k], W=[pst_k], inc=(i == cpb - 1))
                n0 = tok0 + blk * 512
                nn = cpb * 64
                gt, gt_k = gtr.next(); yo, yo_k = yor.next()
                kb.dma("sp", gt[:, 0:nn], P["gdT"][r0:r0 + 128, n0:n0 + nn], R=[TK["gdT"]], W=[gt_k])
                kb.op("dve", lambda e, pst=pst, gt=gt, yo=yo, nn=nn: e.tensor_tensor(out=yo[:, 0:nn], in0=pst[:, 0:nn], in1=gt[:, 0:nn], op=ALU.mult),
                      R=[pst_k, gt_k], W=[yo_k])
                kb.dma("sp", P["ygT"][2048 + r0:2048 + r0 + 128, n0:n0 + nn], yo[:, 0:nn], R=[yo_k], Wp=[TK["ygT"]])

        for h in range(4):
            seq_head(0, LS, h, [(0, LS // 64, None)])
            seq_head(LS, 2 * LP, h, [(0, LP // 64, 0), (LP // 64, LP // 64, 1)])
        kb.barrier()
    kb.stk = kb.top


def stage_p(kb, P, TK, l, last):
    KR = ((0, 4), (4, 8), (8, 16), (16, 20))
    with ExitStack() as st:
        kb.stk = st
        wp, wp_k = kb.sb([128, 20, D], BF16, "wp")
        for kk in range(0, 20, 4):
            kb.dma("sp", wp[:, kk:kk + 4, :], P["wpB"][kk * 128:(kk + 4) * 128, :].rearrange("(k p) c -> p k c", p=128), R=[TK["wpB"]], Wp=[wp_k])
        ygr = Rot(kb, 2, [128, 20, 512], BF16, "yg")
        mixr = Rot(kb, 2, [128, 16, 512], BF16, "mixT")
        mrot = Rot(kb, 6, [128, 512], BF16, "mt")
        accrot = Rot(kb, 2, [128, 512], F32, "acc")
        tmprot = Rot(kb, 3, [128, 512], F32, "ptmp")
        pp = Rot(kb, 4, [128, 512], F32, "pp", psum=True)
        for tb in range(NTOK // 512):
            t0 = tb * 512
            yg, yg_k = ygr.next()
            mixT, mixT_k = mixr.next()
            kb.dma("sp", yg[:], P["ygT"][:, t0:t0 + 512].rearrange("(k p) t -> p k t", p=128), R=[TK["ygT"]], W=[yg_k])
            for ct in range(16):
                acc, acc_k = accrot.next()
                for j, (ka, kb_) in enumerate(KR):
                    ps, ps_k = pp.next()
                    for k in range(ka, kb_):
                        kb.op("pe", lambda e, ps=ps, k=k, ct=ct, ka=ka, kb_=kb_, yg=yg: e.matmul(ps[:], lhsT=wp[:, k, ct * 128:(ct + 1) * 128], rhs=yg[:, k, :],
                                                                                        start=(k == ka), stop=(k == kb_ - 1)), R=[wp_k, yg_k], W=[ps_k], inc=(k == kb_ - 1))
                    mt, mt_k = mrot.next()
                    row0 = j * D + ct * 128
                    kb.dma("sp", mt[:], P["mT"][row0:row0 + 128, t0:t0 + 512], R=[TK["mT"]], W=[mt_k])
                    if j == 0:
                        kb.op("dve", lambda e, ps=ps, mt=mt, acc=acc: e.tensor_tensor(out=acc[:], in0=ps[:], in1=mt[:], op=ALU.mult), R=[ps_k, mt_k], W=[acc_k])
                    else:
                        tmp, tmp_k = tmprot.next()
                        kb.op("dve", lambda e, ps=ps, mt=mt, tmp=tmp: e.tensor_tensor(out=tmp[:], in0=ps[:], in1=mt[:], op=ALU.mult), R=[ps_k, mt_k], W=[tmp_k])
                        kb.op("dve", lambda e, tmp=tmp, acc=acc: e.tensor_tensor(out=acc[:], in0=acc[:], in1=tmp[:], op=ALU.add), R=[tmp_k, acc_k], W=[acc_k])
                kb.op("act", lambda e, acc=acc, ct=ct, mixT=mixT: e.copy(out=mixT[:, ct, :], in_=acc[:]), R=[acc_k], W=[mixT_k])
            kb.dma("sp", P["mixD"][:, t0:t0 + 512].rearrange("(k p) t -> p k t", p=128), mixT[:], R=[mixT_k], Wp=[TK["mixD"]])
        kb.barrier()
    kb.stk = kb.top
    with ExitStack() as st:
        kb.stk = st
        wo, wo_k = kb.sb([128, 16, D], BF16, "wo")
        for kk in range(0, 16, 4):
            kb.dma("sp", wo[:, kk:kk + 4, :], P["woB"][kk * 128:(kk + 4) * 128, :].rearrange("(k p) c -> p k c", p=128), R=[TK["woB"]], Wp=[wo_k])
        gprow, gprow_k = kb.sb([128, D], F32, "gprow")
        mhalf, mhalf_k = kb.sb([128, 1], F32, "mhalf")
        kb.op("dve", lambda e: e.memset(mhalf[:], -0.5), W=[mhalf_k])
        mixr = Rot(kb, 2, [128, 16, 512], BF16, "mixT2")
        orot = Rot(kb, 3, [128, D], F32, "oT")
        xrot = Rot(kb, 3, [128, D], F32, "xp")
        strot = Rot(kb, 3, [128, 4], F32, "pst")
        po = Rot(kb, 4, [128, 512], F32, "po", psum=True)
        for tb in range(NTOK // 512):
            r = 0 if tb < 8 else 1
            if tb == 0 or tb == 8:
                kb.dma("sp", gprow[:], P["gpD"][r].partition_broadcast(128), R=[TK["gpD"]], W=[gprow_k])
            t0 = tb * 512
            mixT, mixT_k = mixr.next()
            kb.dma("sp", mixT[:], P["mixD"][:, t0:t0 + 512].rearrange("(k p) t -> p k t", p=128), R=[TK["mixD"]], W=[mixT_k])
            for tt in range(4):
                g = tb * 4 + tt
                o, o_k = orot.next()
                xt, xt_k = xrot.next()
                stt, stt_k = strot.next()
                kb.dma("sp", xt[:], tok_src(P, g), W=[xt_k])
                for cb in range(4):
                    ps, ps_k = po.next()
                    for k in range(16):
                        kb.op("pe", lambda e, ps=ps, k=k, tt=tt, cb=cb, mixT=mixT: e.matmul(ps[:], lhsT=mixT[:, k, tt * 128:(tt + 1) * 128], rhs=wo[:, k, cb * 512:(cb + 1) * 512],
                                                                                       start=(k == 0), stop=(k == 15)), R=[mixT_k, wo_k], W=[ps_k], inc=(k == 15))
                    if cb % 2 == 0:
                        kb.op("act", lambda e, o=o, ps=ps, cb=cb: e.copy(out=o[:, cb * 512:(cb + 1) * 512], in_=ps[:]), R=[ps_k], W=[o_k])
                    else:
                        kb.op("dve", lambda e, o=o, ps=ps, cb=cb: e.tensor_copy(out=o[:, cb * 512:(cb + 1) * 512], in_=ps[:]), R=[ps_k], W=[o_k])
                jk, jk_k = xrot.next()
                kb.op("act", lambda e, jk=jk, o=o, stt=stt: e.activation(out=jk[:], in_=o[:], func=AF.Square, accum_out=stt[:, 0:1]), R=[o_k], W=[jk_k, stt_k])
                kb.op("dve", lambda e, stt=stt: e.tensor_scalar(out=stt[:, 1:2], in0=stt[:, 0:1], scalar1=1.0 / D, scalar2=EPS, op0=ALU.mult, op1=ALU.add), R=[stt_k], W=[stt_k])
                kb.op("pool", lambda e, stt=stt: e.tensor_tensor(out=stt[:, 2:3], in0=stt[:, 1:2], in1=mhalf[:], op=ALU.pow), R=[stt_k, mhalf_k], W=[stt_k])
                kb.op("dve", lambda e, o=o, stt=stt: e.scalar_tensor_tensor(out=o[:], in0=o[:], scalar=stt[:, 2:3], in1=gprow[:], op0=ALU.mult, op1=ALU.mult),
                      R=[o_k, stt_k, gprow_k], W=[o_k])
                kb.op("dve", lambda e, o=o, xt=xt: e.tensor_tensor(out=xt[:], in0=o[:], in1=xt[:], op=ALU.add), R=[o_k, xt_k], W=[xt_k])
                if g < 32:
                    dst = (P["y_s"] if last else P["x1_s"])[g * 128:(g + 1) * 128, :]
                    dk = TK["y_s" if last else "x1_s"]
                else:
                    dst = (P["y_p"] if last else P["x1_p"])[(g - 32) * 128:(g - 31) * 128, :]
                    dk = TK["y_p" if last else "x1_p"]
                kb.dma("sp", dst, xt[:], R=[xt_k], Wp=[dk])
        kb.barrier()
    kb.stk = kb.top


def sincos(kb, x, x_k, shape, sin_o, cos_o, o_k, tmps):
    MAGIC = 12582912.0
    TWO_PI = 6.283185307179586
    (t1, t1_k), (t2, t2_k), (t3, t3_k) = tmps
    for (off, out) in ((0.0, sin_o), (1.5707963267948966, cos_o)):
        kb.op("dve", lambda e, off=off: e.tensor_scalar(out=t3, in0=x, scalar1=off, scalar2=None, op0=ALU.add), R=[x_k], W=[t3_k])
        kb.op("dve", lambda e: e.tensor_scalar(out=t1, in0=t3, scalar1=1.0 / TWO_PI, scalar2=MAGIC, op0=ALU.mult, op1=ALU.add), R=[t3_k], W=[t1_k])
        kb.op("dve", lambda e: e.tensor_scalar(out=t2, in0=t1, scalar1=-MAGIC, scalar2=None, op0=ALU.add), R=[t1_k], W=[t2_k])
        kb.op("dve", lambda e: e.scalar_tensor_tensor(out=t1, in0=t2, scalar=-TWO_PI, in1=t3, op0=ALU.mult, op1=ALU.add), R=[t2_k, t3_k], W=[t1_k])
        kb.op("dve", lambda e: e.tensor_scalar(out=t1, in0=t1, scalar1=-3.1415925, scalar2=3.1415925, op0=ALU.max, op1=ALU.min), R=[t1_k], W=[t1_k])
        kb.op("act", lambda e, out=out: e.activation(out=out, in_=t1, func=AF.Sin), R=[t1_k], W=[o_k])


def cmul(kb, o_re, o_im, o_k, a_re, a_im, a_k, b_re, b_im, b_k, t1, t1_k, t2, t2_k, eng="dve"):
    kb.op(eng, lambda e: e.tensor_tensor(out=t1, in0=a_re, in1=b_re, op=ALU.mult), R=[a_k, b_k], W=[t1_k])
    kb.op(eng, lambda e: e.tensor_tensor(out=t2, in0=a_im, in1=b_im, op=ALU.mult), R=[a_k, b_k], W=[t2_k])
    kb.op(eng, lambda e: e.tensor_tensor(out=o_re, in0=t1, in1=t2, op=ALU.subtract), R=[t1_k, t2_k], W=[o_k])
    kb.op(eng, lambda e: e.tensor_tensor(out=t1, in0=a_re, in1=b_im, op=ALU.mult), R=[a_k, b_k], W=[t1_k])
    kb.op(eng, lambda e: e.tensor_tensor(out=t2, in0=a_im, in1=b_re, op=ALU.mult), R=[a_k, b_k], W=[t2_k])
    kb.op(eng, lambda e: e.tensor_tensor(out=o_im, in0=t1, in1=t2, op=ALU.add), R=[t1_k, t2_k], W=[o_k])


def s5_lambar(kb, lre, lim, dtt, k_in, np_, shape, pool):
    T = {}
    for nm in ("are", "th", "mag", "img", "sn", "cs", "lbr", "lbi", "t1", "t2", "t3"):
        T[nm] = kb.sb(shape, F32, "s5" + nm)
    a = lambda nm: T[nm][0][:]
    k = lambda nm: T[nm][1]
    kb.op("dve", lambda e: e.tensor_tensor(out=a("are"), in0=lre, in1=dtt, op=ALU.mult), R=[k_in], W=[k("are")])
    kb.op("dve", lambda e: e.tensor_tensor(out=a("th"), in0=lim, in1=dtt, op=ALU.mult), R=[k_in], W=[k("th")])
    kb.op("act", lambda e: e.activation(out=a("mag"), in_=a("are"), func=AF.Exp), R=[k("are")], W=[k("mag")])
    kb.op("act", lambda e: e.activation(out=a("img"), in_=a("are"), func=AF.Exp, scale=-1.0), R=[k("are")], W=[k("img")])
    sincos(kb, a("th"), k("th"), shape, a("sn"), a("cs"), k("sn"), [(a("t1"), k("t1")), (a("t2"), k("t2")), (a("t3"), k("t3"))])
    T["cs"] = (T["cs"][0], T["sn"][1])
    kb.op("dve", lambda e: e.tensor_tensor(out=a("lbr"), in0=a("mag"), in1=a("cs"), op=ALU.mult), R=[k("mag"), k("sn")], W=[k("lbr")])
    kb.op("dve", lambda e: e.tensor_tensor(out=a("lbi"), in0=a("mag"), in1=a("sn"), op=ALU.mult), R=[k("mag"), k("sn")], W=[k("lbi")])
    return T


def stage_s5(kb, P, TK, l, ident, ident_k):
    GP = 4
    with ExitStack() as st:
        kb.stk = st
        W1t, W1t_k = kb.sb([128, 2, 2, 32, 64], BF16, "W1t")
        Wct, Wct_k = kb.sb([128, 2, 2, 16, 128], BF16, "Wct")
        Toep, Toep_k = kb.sb([128, 32, 128], BF16, "Toep")
        apw, apw_k = kb.sb([128, 2, 2, 10, 16], F32, "apw")
        napi, napi_k = kb.sb([128, 2, 10, 16], F32, "napi")
        inj, inj_k = kb.sb([128, 2, 2, 16], F32, "inj")
        h0t, h0t_k = kb.sb([128, 2, 2, 16], F32, "h0t")
        h0b, h0b_k = kb.sb([128, 2, 2, 16], BF16, "h0b")
        fin, fin_k = kb.sb([128, 2, 2, 2, 16], F32, "fin")
        with ExitStack() as st2:
            kb.stk = st2
            Toep32, Toep32_k = kb.sb([128, 32, 128], F32, "Toep32")
            tmk, tmk_k = kb.sb([128, 2, 128], F32, "tmk")
            kb.dma("sp", tmk[:], P["tmask"].rearrange("d s t -> s d t"), W=[tmk_k])
            Dcol, Dcol_k = kb.sb([128, 32], F32, "Dcol")
            for t in range(8):
                kb.dma("sp", Dcol[16 * t:16 * (t + 1), :], P["s5_d"][l].rearrange("(g q) -> q g", q=16), Wp=[Dcol_k], slow=True)
            pw = Rot(kb, 2, [128, 512], F32, "pw", psum=True)
            ptp = Rot(kb, 2, [128, 512], F32, "ptp", psum=True)
            pT4 = Rot(kb, 2, [64, 4, 128], F32, "pT4", psum=True)
            ptf = Rot(kb, 2, [128, 512], F32, "ptf", psum=True)
            for d in range(2):
                with ExitStack() as st3:
                    kb.stk = st3
                    with ExitStack() as st4:
                        kb.stk = st4
                        cp, cp_k = kb.sb([32, 4, 64], F32, "cp")
                        kb.dma("sp", cp[:, 0, :], P["s5_lambda_re"][l, d], W=[cp_k])
                        kb.dma("sp", cp[:, 1, :], P["s5_lambda_im"][l, d], Wp=[cp_k])
                        ls, ls_k = kb.sb([32, 2], F32, "ls")
                        kb.dma("sp", ls[:, 0:1], P["s5_log_step"][l, d].rearrange("(g o) -> g o", o=1), W=[ls_k], slow=True)
                        kb.op("act", lambda e: e.activation(out=ls[:, 1:2], in_=ls[:, 0:1], func=AF.Exp), R=[ls_k], W=[ls_k])
                        dtt, dtt_k = kb.sb([32, 64], F32, "dtt")
                        kb.op("dve", lambda e: e.memset(dtt[:], 1.0), W=[dtt_k])
                        kb.op("dve", lambda e: e.tensor_scalar(out=dtt[:], in0=dtt[:], scalar1=ls[:, 1:2], scalar2=None, op0=ALU.mult), R=[ls_k, dtt_k], W=[dtt_k])
                        kall = Tk("kall")
                        kb.op("dve", lambda e: e.tensor_copy(out=cp[:, 2, :], in_=dtt[:]), R=[dtt_k, cp_k], W=[cp_k])
                        LB = s5_lambar(kb, cp[:, 0, :], cp[:, 1, :], cp[:, 2, :], cp_k, 32, [32, 64], None)
                        g_ = lambda nm: LB[nm][0][:]
                        gk = lambda nm: LB[nm][1]
                        fr, fr_k = kb.sb([32, 6, 64], F32, "fr")
                        lre, lim = cp[:, 0, :], cp[:, 1, :]
                        kb.op("dve", lambda e: e.tensor_scalar(out=fr[:, 0, :], in0=g_("lbr"), scalar1=-1.0, scalar2=None, op0=ALU.add), R=[gk("lbr")], W=[fr_k])
                        kb.op("dve", lambda e: e.tensor_tensor(out=fr[:, 1, :], in0=lre, in1=lre, op=ALU.mult), R=[cp_k, fr_k], W=[fr_k])
                        kb.op("dve", lambda e: e.tensor_tensor(out=fr[:, 2, :], in0=lim, in1=lim, op=ALU.mult), R=[cp_k, fr_k], W=[fr_k])
                        kb.op("dve", lambda e: e.tensor_tensor(out=fr[:, 1, :], in0=fr[:, 1, :], in1=fr[:, 2, :], op=ALU.add), R=[fr_k], W=[fr_k])
                        kb.op("dve", lambda e: e.reciprocal(out=fr[:, 1, :], in_=fr[:, 1, :]), R=[fr_k], W=[fr_k])
                        kb.op("dve", lambda e: e.tensor_tensor(out=fr[:, 2, :], in0=fr[:, 0, :], in1=lre, op=ALU.mult), R=[fr_k, cp_k], W=[fr_k])
                        kb.op("dve", lambda e: e.tensor_tensor(out=fr[:, 3, :], in0=g_("lbi"), in1=lim, op=ALU.mult), R=[gk("lbi"), cp_k, fr_k], W=[fr_k])
                        kb.op("dve", lambda e: e.tensor_tensor(out=fr[:, 2, :], in0=fr[:, 2, :], in1=fr[:, 3, :], op=ALU.add), R=[fr_k], W=[fr_k])
                        kb.op("dve", lambda e: e.tensor_tensor(out=fr[:, 4, :], in0=fr[:, 2, :], in1=fr[:, 1, :], op=ALU.mult), R=[fr_k], W=[fr_k])
                        kb.op("dve", lambda e: e.tensor_tensor(out=fr[:, 2, :], in0=g_("lbi"), in1=lre, op=ALU.mult), R=[gk("lbi"), cp_k, fr_k], W=[fr_k])
                        kb.op("dve", lambda e: e.tensor_tensor(out=fr[:, 3, :], in0=fr[:, 0, :], in1=lim, op=ALU.mult), R=[fr_k, cp_k], W=[fr_k])
                        kb.op("dve", lambda e: e.tensor_tensor(out=fr[:, 2, :], in0=fr[:, 2, :], in1=fr[:, 3, :], op=ALU.subtract), R=[fr_k], W=[fr_k])
                        kb.op("dve", lambda e: e.tensor_tensor(out=fr[:, 5, :], in0=fr[:, 2, :], in1=fr[:, 1, :], op=ALU.mult), R=[fr_k], W=[fr_k])
                        par = lambda idx: P["s5par"][d, idx].rearrange("(g n) -> g n", n=64)
                        kb.dma("sp", par(32), fr[:, 4, :], R=[fr_k], Wp=[TK["s5par"]])
                        kb.dma("sp", par(33), fr[:, 5, :], R=[fr_k], Wp=[TK["s5par"]])
                        ivr, ivr_k = kb.sb([32, 2, 64], F32, "ivr")
                        kb.op("dve", lambda e: e.tensor_tensor(out=ivr[:, 0, :], in0=g_("img"), in1=g_("cs"), op=ALU.mult), R=[gk("img"), gk("sn")], W=[ivr_k])
                        kb.op("dve", lambda e: e.scalar_tensor_tensor(out=ivr[:, 1, :], in0=g_("img"), scalar=-1.0, in1=g_("sn"), op0=ALU.mult, op1=ALU.mult), R=[gk("img"), gk("sn"), ivr_k], W=[ivr_k])
                        pwr = Rot(kb, 3, [32, 2, 64], F32, "pwr")
                        ct1, ct1_k = kb.sb([32, 64], F32, "ct1")
                        ct2, ct2_k = kb.sb([32, 64], F32, "ct2")
                        cur, cur_k = kb.sb([32, 2, 64], F32, "one")
                        kb.op("dve", lambda e, cur=cur: e.memset(cur[:, 0, :], 1.0), W=[cur_k])
                        kb.op("dve", lambda e, cur=cur: e.memset(cur[:, 1, :], 0.0), R=[cur_k], W=[cur_k])
                        kb.dma("sp", par(0), cur[:, 0, :], R=[cur_k], Wp=[TK["s5par"]])
                        kb.dma("sp", par(1), cur[:, 1, :], R=[cur_k], Wp=[TK["s5par"]])
                        for sign in (0, 1):
                            c_, c_k = cur, cur_k
                            for k in range(1, 9 if sign == 0 else 8):
                                n_, n_k = pwr.next()
                                if sign == 0:
                                    cmul(kb, n_[:, 0, :], n_[:, 1, :], n_k, c_[:, 0, :], c_[:, 1, :], c_k, g_("lbr"), g_("lbi"), gk("lbi"), ct1[:], ct1_k, ct2[:], ct2_k)
                                    base = 2 * k
                                else:
                                    cmul(kb, n_[:, 0, :], n_[:, 1, :], n_k, c_[:, 0, :], c_[:, 1, :], c_k, ivr[:, 0, :], ivr[:, 1, :], ivr_k, ct1[:], ct1_k, ct2[:], ct2_k)
                                    base = 18 + 2 * (k - 1)
                                kb.dma("sp", par(base), n_[:, 0, :], R=[n_k], Wp=[TK["s5par"]])
                                kb.dma("sp", par(base + 1), n_[:, 1, :], R=[n_k], Wp=[TK["s5par"]])
                                c_, c_k = n_, n_k
                        kb.barrier()
                    kb.stk = st3
                    X = [kb.sb([128, 2048], F32, "X") for _ in range(14)]
                    POS, NEG, Fr, BT, CT, FB, Rr = (X[0], X[1]), (X[2], X[3]), (X[4], X[5]), (X[6], X[7]), (X[8], X[9]), (X[10], X[11]), (X[12], X[13])
                    bn, bn_k = kb.sb([128, 16, 16], F32, "bn")
                    bnr, bnr_k = kb.sb([128, 16, 8, 16], F32, "bnr")
                    tA, tA_k = kb.sb([128, 2048], F32, "tA")
                    tB, tB_k = kb.sb([128, 2048], F32, "tB")
                    for comp in range(2):
                        for sl in range(8):
                            kb.dma("sp", POS[comp][0][16 * sl:16 * (sl + 1), :], P["s5par"][d, 2 * sl + comp].partition_broadcast(16), R=[TK["s5par"]], Wp=[POS[comp][1]])
                            nidx = (0 + comp) if sl == 0 else (18 + 2 * (sl - 1) + comp)
                            kb.dma("sp", NEG[comp][0][16 * sl:16 * (sl + 1), :], P["s5par"][d, nidx].partition_broadcast(16), R=[TK["s5par"]], Wp=[NEG[comp][1]])
                        kb.dma("sp", Fr[comp][0][:], P["s5par"][d, 32 + comp].partition_broadcast(128), R=[TK["s5par"]], W=[Fr[comp][1]])
                        csrc = P["s5_c_re" if comp == 0 else "s5_c_im"][l, d].rearrange("g p n -> p g n")
                        for sl in range(8):
                            kb.dma("sp", CT[comp][0][16 * sl:16 * (sl + 1), :].rearrange("p (g n) -> p g n", n=64), csrc, Wp=[CT[comp][1]])
                        kb.dma("sp", bn[:], P["s5_b_re" if comp == 0 else "s5_b_im"][l, d].rearrange("(gp g2) n q -> (g2 n) gp q", g2=2), W=[bn_k])
                        for t in range(8):
                            kb.op("dve" if t % 2 == 0 else "pool", lambda e, t=t: e.tensor_copy(out=bnr[:, :, t, :], in_=bn[:]), R=[bn_k], W=[bnr_k])
                        for gq in range(4):
                            ps, ps_k = pw.next()
                            for i in range(4):
                                gp = gq * 4 + i
                                kb.op("pe", lambda e, ps=ps, i=i, gp=gp: e.matmul(ps[:, i * 128:(i + 1) * 128], lhsT=bnr[:, gp, :, :].rearrange("p t q -> p (t q)"), rhs=ident[:],
                                                                              start=True, stop=True), R=[bnr_k, ident_k], W=[ps_k], inc=(i == 3))
                            kb.op("act", lambda e, ps=ps, gq=gq, comp=comp: e.copy(out=BT[comp][0][:, gq * 512:(gq + 1) * 512], in_=ps[:]), R=[ps_k], W=[BT[comp][1]])
                    A_ = lambda pr, c: pr[c][0][:]
                    K_ = lambda pr: pr[0][1]
                    def both(pr):
                        return [pr[0][1], pr[1][1]]
                    def CM(o, a, b_):
                        ok = Tk("o")
                        kb.op("dve", lambda e: e.tensor_tensor(out=tA[:], in0=A_(a, 0), in1=A_(b_, 0), op=ALU.mult), R=both(a) + both(b_), W=[tA_k])
                        kb.op("dve", lambda e: e.tensor_tensor(out=tB[:], in0=A_(a, 1), in1=A_(b_, 1), op=ALU.mult), R=both(a) + both(b_), W=[tB_k])
                        kb.op("dve", lambda e: e.tensor_tensor(out=A_(o, 0), in0=tA[:], in1=tB[:], op=ALU.subtract), R=[tA_k, tB_k], W=[o[0][1]])
                        kb.op("dve", lambda e: e.tensor_tensor(out=tA[:], in0=A_(a, 0), in1=A_(b_, 1), op=ALU.mult), R=both(a) + both(b_), W=[tA_k])
                        kb.op("dve", lambda e: e.tensor_tensor(out=tB[:], in0=A_(a, 1), in1=A_(b_, 0), op=ALU.mult), R=both(a) + both(b_), W=[tB_k])
                        kb.op("dve", lambda e: e.tensor_tensor(out=A_(o, 1), in0=tA[:], in1=tB[:], op=ALU.add), R=[tA_k, tB_k], W=[o[1][1]])
                    CM(FB, Fr, BT)
                    if d == 0:
                        CM(BT, NEG, FB)
                        CM(Rr, POS, CT)
                        LAM = NEG
                        for comp in range(2):
                            kb.dma("sp", LAM[comp][0][:], P["s5par"][d, 2 + comp].partition_broadcast(128), R=[TK["s5par"]], W=[LAM[comp][1]])
                        CM(CT, Rr, LAM)
                        for comp in range(2):
                            kb.dma("sp", Fr[comp][0][:], P["s5par"][d, 14 + comp].partition_broadcast(128), R=[TK["s5par"]], W=[Fr[comp][1]])
                        CM(FB, BT, Fr)
                        W1src, Lsrc, Rsrc, Wcsrc = FB, BT, Rr, CT
                    else:
                        CM(BT, POS, FB)
                        CM(Rr, NEG, CT)
                        LAM = POS
                        for comp in range(2):
                            kb.dma("sp", LAM[comp][0][:], P["s5par"][d, 16 + comp].partition_broadcast(128), R=[TK["s5par"]], W=[LAM[comp][1]])
                        CM(CT, Rr, LAM)
                        W1src, Lsrc, Rsrc, Wcsrc = BT, BT, Rr, CT
                    for comp in range(2):
                        kb.op("act", lambda e, comp=comp: e.copy(out=W1t[:, d, comp, :, :].rearrange("p g n -> p (g n)"), in_=A_(W1src, comp)), R=both(W1src), W=[W1t_k])
                    for gp in range(16):
                        for comp in range(2):
                            ps, ps_k = ptp.next()
                            for g2 in range(2):
                                g = 2 * gp + g2
                                kb.op("pe", lambda e, ps=ps, g=g, g2=g2, comp=comp: e.matmul(ps[64 * g2:64 * g2 + 64, 0:128], lhsT=A_(Wcsrc, comp)[:, g * 64:(g + 1) * 64], rhs=ident[:],
                                                                                                start=True, stop=True, tile_position=(0, 64 * g2)), R=both(Wcsrc) + [ident_k], W=[ps_k], inc=(g2 == 1))
                            kb.op("dve", lambda e, ps=ps, gp=gp, comp=comp: e.tensor_scalar(out=Wct[:, d, comp, gp, :], in0=ps[:, 0:128], scalar1=(1.0 if comp == 0 else -1.0), scalar2=None, op0=ALU.mult),
                                  R=[ps_k], W=[Wct_k])
                    lrr = Rot(kb, 2, [64, 4, 128], F32, "lr")
                    for g in range(32):
                        p4, p4_k = pT4.next()
                        for i, (src, comp) in enumerate(((Lsrc, 0), (Lsrc, 1), (Rsrc, 0), (Rsrc, 1))):
                            kb.op("pe", lambda e, p4=p4, i=i, src=src, comp=comp, g=g: e.transpose(p4[:, i, :], A_(src, comp)[:, g * 64:(g + 1) * 64], ident[:]),
                                  R=both(src) + [ident_k], W=[p4_k], inc=(i == 3))
                        lr, lr_k = lrr.next()
                        kb.op("act", lambda e, lr=lr, p4=p4: e.copy(out=lr[:, 0:3, :], in_=p4[:, 0:3, :]), R=[p4_k], W=[lr_k])
                        kb.op("dve", lambda e, lr=lr, p4=p4: e.tensor_scalar(out=lr[:, 3, :], in0=p4[:, 3, :], scalar1=-1.0, scalar2=None, op0=ALU.mult), R=[p4_k], W=[lr_k])
                        pt, pt_k = ptf.next()
                        kb.op("pe", lambda e, pt=pt, lr=lr: e.matmul(pt[:, 0:128], lhsT=lr[:, 0, :], rhs=lr[:, 2, :], start=True, stop=False), R=[lr_k], W=[pt_k], inc=False)
                        kb.op("pe", lambda e, pt=pt, lr=lr: e.matmul(pt[:, 0:128], lhsT=lr[:, 1, :], rhs=lr[:, 3, :], start=False, stop=True), R=[lr_k], W=[pt_k])
                        if d == 0:
                            kb.op("dve", lambda e, pt=pt, g=g: e.tensor_tensor(out=Toep32[:, g, :], in0=pt[:, 0:128], in1=tmk[:, 0, :], op=ALU.mult), R=[pt_k, tmk_k], W=[Toep32_k])
                        else:
                            tt_, tt_k = lrr.next() if False else (None, None)
                            kb.op("dve", lambda e, pt=pt, g=g: e.tensor_tensor(out=tA[:, 0:128], in0=pt[:, 0:128], in1=tmk[:, 1, :], op=ALU.mult), R=[pt_k, tmk_k], W=[tA_k])
                            kb.op("dve", lambda e, g=g: e.tensor_tensor(out=Toep32[:, g, :], in0=Toep32[:, g, :], in1=tA[:, 0:128], op=ALU.add), R=[tA_k, Toep32_k], W=[Toep32_k])
                            kb.op("dve", lambda e, g=g: e.scalar_tensor_tensor(out=Toep[:, g, :], in0=ident[:], scalar=Dcol[:, g:g + 1], in1=Toep32[:, g, :], op0=ALU.mult, op1=ALU.add),
                                  R=[ident_k, Dcol_k, Toep32_k], W=[Toep_k])
                    sp_, sp_k = kb.sb([128, 4, 16], F32, "sp_")
                    kb.dma("sp", sp_[:, 0, :], P["s5_lambda_re"][l, d].rearrange("(gp g2) n -> (g2 n) gp", g2=2), W=[sp_k], slow=True)
                    kb.dma("sp", sp_[:, 1, :], P["s5_lambda_im"][l, d].rearrange("(gp g2) n -> (g2 n) gp", g2=2), Wp=[sp_k], slow=True)
                    for g2 in range(2):
                        kb.dma("sp", sp_[64 * g2:64 * g2 + 64, 2, :], P["s5_log_step"][l, d, g2::2].partition_broadcast(64), Wp=[sp_k], slow=True)
                    kb.op("act", lambda e: e.activation(out=sp_[:, 2, :], in_=sp_[:, 2, :], func=AF.Exp), R=[sp_k], W=[sp_k])
                    LS_ = s5_lambar(kb, sp_[:, 0, :], sp_[:, 1, :], sp_[:, 2, :], sp_k, 128, [128, 16], None)
                    s1, s1_k = kb.sb([128, 16], F32, "s1")
                    s2, s2_k = kb.sb([128, 16], F32, "s2")
                    sq = [kb.sb([128, 2, 16], F32, "sq") for _ in range(2)]
                    c_re, c_im, c_k = LS_["lbr"][0][:], LS_["lbi"][0][:], Tk("ck")
                    cks = [LS_["lbr"][1], LS_["lbi"][1]]
                    for it in range(12):
                        o, o_k = sq[it % 2]
                        kb.op("dve", lambda e, c_re=c_re: e.tensor_tensor(out=s1[:], in0=c_re, in1=c_re, op=ALU.mult), R=cks, W=[s1_k])
                        kb.op("dve", lambda e, c_im=c_im: e.tensor_tensor(out=s2[:], in0=c_im, in1=c_im, op=ALU.mult), R=cks, W=[s2_k])
                        kb.op("dve", lambda e, o=o: e.tensor_tensor(out=o[:, 0, :], in0=s1[:], in1=s2[:], op=ALU.subtract), R=[s1_k, s2_k], W=[o_k])
                        kb.op("dve", lambda e, o=o, c_re=c_re, c_im=c_im: e.scalar_tensor_tensor(out=o[:, 1, :], in0=c_re, scalar=2.0, in1=c_im, op0=ALU.mult, op1=ALU.mult), R=cks + [o_k], W=[o_k])
                        c_re, c_im, cks = o[:, 0, :], o[:, 1, :], [o_k]
                        if it >= 2:
                            k = it - 2
                            kb.op("dve", lambda e, o=o, k=k: e.tensor_copy(out=apw[:, d, :, k, :], in_=o[:]), R=[o_k], W=[apw_k])
                            kb.op("dve", lambda e, o=o, k=k: e.tensor_scalar(out=napi[:, d, k, :], in0=o[:, 1, :], scalar1=-1.0, scalar2=None, op0=ALU.mult), R=[o_k], W=[napi_k])
                    kb.dma("sp", h0t[:, d, 0, :], P["s5_h0_re"][l, d].rearrange("(gp g2) n -> (g2 n) gp", g2=2), W=[h0t_k] if d == 0 else [], Wp=[] if d == 0 else [h0t_k], slow=True)
                    kb.dma("sp", h0t[:, d, 1, :], P["s5_h0_im"][l, d].rearrange("(gp g2) n -> (g2 n) gp", g2=2), Wp=[h0t_k], slow=True)
                    cmul(kb, inj[:, d, 0, :], inj[:, d, 1, :], inj_k, apw[:, d, 0, 0, :], apw[:, d, 1, 0, :], apw_k, h0t[:, d, 0, :], h0t[:, d, 1, :], h0t_k, s1[:], s1_k, s2[:], s2_k)
                    if d == 0:
                        kb.dump("d_pos", POS[0][0][:], POS[0][1], [128, 2048])
                        kb.dump("d_w1src", W1src[0][0][:], W1src[0][1], [128, 2048])
                    kb.barrier()
                kb.stk = st2
            kb.dump("d_apw", apw[:], apw_k, [128, 2, 2, 10, 16])
            kb.dump("d_inj", inj[:], inj_k, [128, 2, 2, 16])
            kb.op("dve", lambda e: e.tensor_copy(out=h0b[:], in_=h0t[:]), R=[h0t_k], W=[h0b_k])
            kb.barrier()
        kb.stk = st
        U, U_k = kb.sb([128, 32, NCH], BF16, "U")
        for g in range(32):
            for t in range(8):
                kb.dma("sp", U[16 * t:16 * (t + 1), g, :], P["UD"][g * 16:(g + 1) * 16, t, :], R=[TK["UD"]], Wp=[U_k])
        Hrot = [[kb.sb([128, GP, NCH], F32, "H") for _ in range(2)] for _ in range(2)]
        Hp, Hp_k = kb.sb([128, 2, 2, GP, NCH], BF16, "Hp")
        ph = Rot(kb, 2, [128, 1024], F32, "ph", psum=True)
        py = Rot(kb, 2, [128, 1024], F32, "py", psum=True)
        yrot = Rot(kb, 2, [128, NCH], F32, "yv")
        xsr = Rot(kb, 2, [128, NCH], F32, "xs5")
        g1, g1_k = kb.sb([128, NCH], F32, "g1")
        g2t, g2t_k = kb.sb([128, NCH], F32, "g2t")
        SEG = ((0, 512), (512, 32), (544, 32))
        for sub in range(16 // GP):
            for d in range(2):
                cur = 0
                for gl in range(GP):
                    gp = sub * GP + gl
                    for comp in range(2):
                        ps, ps_k = ph.next()
                        for (c0, cn) in ((0, 512), (512, 64)):
                            for g2 in range(2):
                                g = 2 * gp + g2
                                kb.op("pe", lambda e, ps=ps, g=g, g2=g2, comp=comp, c0=c0, cn=cn: e.matmul(
                                    ps[64 * g2:64 * g2 + 64, c0:c0 + cn], lhsT=W1t[:, d, comp, g, :], rhs=U[:, g, c0:c0 + cn], start=True, stop=True, tile_position=(0, 64 * g2)),
                                    R=[W1t_k, U_k], W=[ps_k], inc=(g2 == 1 and c0 == 512))
                        H, H_k = Hrot[cur][comp]
                        kb.op("act", lambda e, H=H, ps=ps, gl=gl: e.copy(out=H[:, gl, :], in_=ps[:, 0:NCH]), R=[ps_k], W=[H_k])
                        ci = 0 if d == 0 else 511
                        kb.op("dve", lambda e, H=H, gl=gl, gp=gp, comp=comp, ci=ci: e.tensor_tensor(out=H[:, gl, ci:ci + 1], in0=H[:, gl, ci:ci + 1], in1=inj[:, d, comp, gp:gp + 1], op=ALU.add),
                              R=[H_k, inj_k], W=[H_k])
                if sub == 0 and d == 0:
                    kb.dump("d_hc", Hrot[cur][0][0][:], Hrot[cur][0][1], [128, GP, NCH])
                for k in range(9):
                    sh = 1 << k
                    o_re, o_re_k = Hrot[cur][0]; o_im, o_im_k = Hrot[cur][1]
                    n_re, n_re_k = Hrot[1 - cur][0]; n_im, n_im_k = Hrot[1 - cur][1]
                    items = []
                    for gl in range(GP):
                        gp = sub * GP + gl
                        are = apw[:, d, 0, k, gp:gp + 1]; aim = apw[:, d, 1, k, gp:gp + 1]; naim = napi[:, d, k, gp:gp + 1]
                        for si, (c0, cn) in enumerate(SEG):
                            if si == 2:
                                continue
                            if si == 0:
                                V = (lambda gl, c0: (lambda t, a, b: t[:, gl, c0 + a:c0 + b]))(gl, c0)
                                n = cn
                            else:
                                V = (lambda gl: (lambda t, a, b: t[:, gl, 512:576].rearrange("p (s c) -> p s c", s=2)[:, :, a:b]))(gl)
                                n = 32
                            items.append((V, n, are, aim, naim))
                    for (V, n, are, aim, naim) in items:
                        if sh >= n:
                            kb.op("pool", lambda e, V=V, n=n: e.tensor_copy(out=V(n_re, 0, n), in_=V(o_re, 0, n)), R=[o_re_k], W=[n_re_k])
                            kb.op("pool", lambda e, V=V, n=n: e.tensor_copy(out=V(n_im, 0, n), in_=V(o_im, 0, n)), R=[o_im_k], W=[n_im_k])
                    act = []
                    for (V, n, are, aim, naim) in items:
                        if sh >= n:
                            continue
                        if d == 0:
                            dst, srcs, same, keep = (sh, n), (0, n - sh), (sh, n), (0, sh)
                        else:
                            dst, srcs, same, keep = (0, n - sh), (sh, n), (0, n - sh), (n - sh, n)
                        act.append((V, are, aim, naim, dst, srcs, same, keep))
                    for (V, are, aim, naim, dst, srcs, same, keep) in act:
                        kb.op("dve", lambda e, V=V, dst=dst, srcs=srcs, same=same, are=are: e.scalar_tensor_tensor(
                            out=V(n_re, *dst), in0=V(o_re, *srcs), scalar=are, in1=V(o_re, *same), op0=ALU.mult, op1=ALU.add), R=[o_re_k, apw_k], W=[n_re_k])
                        kb.op("dve", lambda e, V=V, dst=dst, srcs=srcs, same=same, are=are: e.scalar_tensor_tensor(
                            out=V(n_im, *dst), in0=V(o_im, *srcs), scalar=are, in1=V(o_im, *same), op0=ALU.mult, op1=ALU.add), R=[o_im_k, apw_k], W=[n_im_k])
                    for (V, are, aim, naim, dst, srcs, same, keep) in act:
                        kb.op("dve", lambda e, V=V, dst=dst, srcs=srcs, naim=naim: e.scalar_tensor_tensor(
                            out=V(n_re, *dst), in0=V(o_im, *srcs), scalar=naim, in1=V(n_re, *dst), op0=ALU.mult, op1=ALU.add), R=[o_im_k, napi_k, n_re_k], W=[n_re_k])
                        kb.op("dve", lambda e, V=V, dst=dst, srcs=srcs, aim=aim: e.scalar_tensor_tensor(
                            out=V(n_im, *dst), in0=V(o_re, *srcs), scalar=aim, in1=V(n_im, *dst), op0=ALU.mult, op1=ALU.add), R=[o_re_k, apw_k, n_im_k], W=[n_im_k])
                    for (V, are, aim, naim, dst, srcs, same, keep) in act:
                        kb.op("pool", lambda e, V=V, keep=keep: e.tensor_copy(out=V(n_re, *keep), in_=V(o_re, *keep)), R=[o_re_k], W=[n_re_k])
                        kb.op("pool", lambda e, V=V, keep=keep: e.tensor_copy(out=V(n_im, *keep), in_=V(o_im, *keep)), R=[o_im_k], W=[n_im_k])
                    cur = 1 - cur
                for comp in range(2):
                    H, H_k = Hrot[cur][comp]
                    for (c0, cn) in SEG:
                        if d == 0:
                            kb.op("act", lambda e, H=H, comp=comp, c0=c0, cn=cn: e.copy(out=Hp[:, d, comp, :, c0 + 1:c0 + cn], in_=H[:, :, c0:c0 + cn - 1]), R=[H_k], W=[Hp_k])
                            edge = c0
                        else:
                            kb.op("act", lambda e, H=H, comp=comp, c0=c0, cn=cn: e.copy(out=Hp[:, d, comp, :, c0:c0 + cn - 1], in_=H[:, :, c0 + 1:c0 + cn]), R=[H_k], W=[Hp_k])
                            edge = c0 + cn - 1
                        if c0 == 0:
                            kb.op("dve", lambda e, comp=comp, edge=edge: e.tensor_copy(out=Hp[:, d, comp, :, edge], in_=h0b[:, d, comp, sub * GP:(sub + 1) * GP]), R=[h0b_k, Hp_k], W=[Hp_k])
                        else:
                            kb.op("dve", lambda e, comp=comp, edge=edge: e.memset(Hp[:, d, comp, :, edge:edge + 1], 0.0), R=[Hp_k], W=[Hp_k])
                            b = 0 if c0 == 512 else 1
                            fc = (c0 + cn - 1) if d == 0 else c0
                            kb.op("dve", lambda e, H=H, comp=comp, b=b, fc=fc: e.tensor_copy(out=fin[:, b, d, comp, sub * GP:(sub + 1) * GP], in_=H[:, :, fc]), R=[H_k, fin_k], W=[fin_k])
            for gl in range(GP):
                gp = sub * GP + gl
                for g2 in range(2):
                    g = 2 * gp + g2
                    ps, ps_k = py.next()
                    for (c0, cn) in ((0, 512), (512, 64)):
                        kb.op("pe", lambda e, ps=ps, g=g, c0=c0, cn=cn: e.matmul(ps[:, c0:c0 + cn], lhsT=Toep[:, g, :], rhs=U[:, g, c0:c0 + cn], start=True, stop=False),
                              R=[Toep_k, U_k], W=[ps_k], inc=False)
                        for d in range(2):
                            for comp in range(2):
                                last = (d == 1 and comp == 1)
                                kb.op("pe", lambda e, ps=ps, g2=g2, gp=gp, gl=gl, d=d, comp=comp, c0=c0, cn=cn, last=last: e.matmul(
                                    ps[:, c0:c0 + cn], lhsT=Wct[64 * g2:64 * g2 + 64, d, comp, gp, :], rhs=Hp[64 * g2:64 * g2 + 64, d, comp, gl, c0:c0 + cn], start=False, stop=last),
                                    R=[Wct_k, Hp_k], W=[ps_k], inc=(last and c0 == 512))
                    yv, yv_k = yrot.next()
                    xs_, xs_k = xsr.next()
                    kb.op("act", lambda e, xs_=xs_, ps=ps: e.copy(out=xs_[:], in_=ps[:, 0:NCH]), R=[ps_k], W=[xs_k])
                    x = xs_[:]
                    ps_k = xs_k
                    kb.op("dve", lambda e, x=x: e.tensor_tensor(out=g1[:], in0=x, in1=x, op=ALU.mult), R=[ps_k], W=[g1_k])
                    kb.op("dve", lambda e: e.tensor_scalar(out=g1[:], in0=g1[:], scalar1=0.044715, scalar2=1.0, op0=ALU.mult, op1=ALU.add), R=[g1_k], W=[g1_k])
                    kb.op("dve", lambda e, x=x: e.tensor_tensor(out=g1[:], in0=g1[:], in1=x, op=ALU.mult), R=[g1_k, ps_k], W=[g1_k])
                    kb.op("act", lambda e: e.activation(out=g2t[:], in_=g1[:], func=AF.Tanh, scale=0.7978845608028654), R=[g1_k], W=[g2t_k])
                    kb.op("dve", lambda e: e.tensor_scalar(out=g2t[:], in0=g2t[:], scalar1=1.0, scalar2=0.5, op0=ALU.add, op1=ALU.mult), R=[g2t_k], W=[g2t_k])
                    kb.op("dve", lambda e, x=x, yv=yv: e.tensor_tensor(out=yv[:], in0=g2t[:], in1=x, op=ALU.mult), R=[g2t_k, ps_k], W=[yv_k])
                    for t in range(8):
                        kb.dma("sp", P["YD"][g * 16:(g + 1) * 16, t, :], yv[16 * t:16 * (t + 1), :], R=[yv_k], Wp=[TK["YD"]])
        for b in range(2):
            for d in range(2):
                kb.dma("sp", P["ns5re"][b, l, d].rearrange("(gp g2) n -> (g2 n) gp", g2=2), fin[:, b, d, 0, :], R=[fin_k], Wp=[TK["ns5re"]], slow=True)
                kb.dma("sp", P["ns5im"][b, l, d].rearrange("(gp g2) n -> (g2 n) gp", g2=2), fin[:, b, d, 1, :], R=[fin_k], Wp=[TK["ns5im"]], slow=True)
        kb.barrier()
    kb.stk = kb.top
    with ExitStack() as st:
        kb.stk = st
        YT, YT_k = kb.sb([128, 4, 8, NCH], F32, "YT")
        YB, YB_k = kb.sb([128, 4, 8, NCH], BF16, "YB")
        gw, gw_k = kb.sb([128, 4, 512], BF16, "gw")
        gb, gb_k = kb.sb([128, 4], F32, "gb")
        kb.dma("pool", gw[:], P["s5_glu_w"][l].rearrange("(k p) c -> p k c", p=128), W=[gw_k])
        kb.dma("sp", gb[:], P["s5_glu_b"][l].rearrange("(j p) -> p j", p=128), W=[gb_k], slow=True)
        for j in range(4):
            kb.dma("sp", YT[:, j, :, :], P["YD"][j * 128:(j + 1) * 128, :, :], R=[TK["YD"]], Wp=[YT_k])
        for j in range(4):
            if j % 2 == 0:
                kb.op("dve", lambda e, j=j: e.tensor_copy(out=YB[:, j, :, :], in_=YT[:, j, :, :]), R=[YT_k], W=[YB_k] if j == 0 else [], inc=True)
            else:
                kb.op("act", lambda e, j=j: e.copy(out=YB[:, j, :, :], in_=YT[:, j, :, :]), R=[YT_k], W=[], inc=True)
        kb.barrier()
        pg = Rot(kb, 3, [128, 1024], F32, "pg", psum=True)
        gtr = Rot(kb, 2, [128, 8, NCH], BF16, "gbt")
        sgr = Rot(kb, 3, [128, NCH], F32, "sg")
        Z, Z_k = kb.sb([128, NTOK], BF16, "Z")
        for jo in range(4):
            gt, gt_k = gtr.next()
            kb.dma("sp", gt[:], P["gbP"][jo * 128:(jo + 1) * 128, :, :], R=[TK["gbP"]], W=[gt_k])
            for t in range(8):
                ps, ps_k = pg.next()
                for (c0, cn) in ((0, 512), (512, 64)):
                    for kc in range(4):
                        kb.op("pe", lambda e, ps=ps, kc=kc, jo=jo, t=t, c0=c0, cn=cn: e.matmul(ps[:, c0:c0 + cn], lhsT=gw[:, kc, jo * 128:(jo + 1) * 128], rhs=YB[:, kc, t, c0:c0 + cn],
                                                                                 start=(kc == 0), stop=(kc == 3)), R=[gw_k, YB_k], W=[ps_k], inc=(kc == 3 and c0 == 512))
                sg, sg_k = sgr.next()
                kb.op("act", lambda e, sg=sg, ps=ps, jo=jo: e.activation(out=sg[:], in_=ps[:, 0:NCH], func=AF.Sigmoid, bias=gb[:, jo:jo + 1]), R=[ps_k, gb_k], W=[sg_k])
                kb.op("dve", lambda e, sg=sg, jo=jo, t=t: e.tensor_tensor(out=sg[:], in0=sg[:], in1=YT[:, jo, t, :], op=ALU.mult), R=[sg_k, YT_k], W=[sg_k])
                kb.op("dve", lambda e, sg=sg, gt=gt, t=t: e.tensor_tensor(out=Z[:, t::8], in0=sg[:], in1=gt[:, t, :], op=ALU.mult), R=[sg_k, gt_k], W=[Z_k])
            kb.dma("sp", P["ygT"][512 + jo * 128:512 + (jo + 1) * 128, :], Z[:], R=[Z_k], Wp=[TK["ygT"]])
        kb.barrier()
    kb.stk = kb.top


def rope_consts():
    t = np.arange(LS)
    row = (t // 64).astype(np.float32)
    col = (t % 64).astype(np.float32)
    inv = (np.float32(10000.0) ** (-(np.arange(0, 64, 2, dtype=np.float32)) / np.float32(64))).astype(np.float32)
    ang = np.stack([row[:, None] * inv[None, :], col[:, None] * inv[None, :]], axis=1).astype(np.float32)
    c = np.cos(ang.astype(np.float64)).astype(np.float32)
    s_ = np.sin(ang.astype(np.float64)).astype(np.float32)
    c = np.broadcast_to(c[:, None], (LS, 10, 2, 32)).reshape(LS, 640)
    s_ = np.broadcast_to(s_[:, None], (LS, 10, 2, 32)).reshape(LS, 640)
    return np.ascontiguousarray(c), np.ascontiguousarray(s_)


def dft_consts():
    def cs(L):
        j = np.arange(L, dtype=np.int64)
        jk = (j[:, None] * j[None, :]) % L
        ang = 2.0 * np.pi * jk.astype(np.float64) / L
        return np.stack([np.cos(ang), np.sin(ang)]) / np.sqrt(L)
    return (cs(128).astype(np.float32), cs(LS).astype(ml_dtypes.bfloat16), cs(LP).astype(ml_dtypes.bfloat16))


def make_in_maps(inputs, n_cores=8):
    ident = np.eye(128, dtype=np.float32)
    f = lambda a: np.ascontiguousarray(np.asarray(a, dtype=np.float32))
    shared = {k: f(inputs[k]) for k in (
        "c_ctx", "norm_pre", "norm_post", "w_mod", "b_mod", "w_in", "fourier_w", "s5_lambda_re", "s5_lambda_im",
        "s5_log_step", "s5_b_re", "s5_b_im", "s5_c_re", "s5_c_im", "s5_d", "s5_glu_w", "s5_glu_b", "q_norm",
        "k_norm", "hgrn_lb_logits", "hgrn_norm", "w_proj_a", "w_proj_b", "w_proj_c", "w_proj_d", "w_out")}
    shared["ident"] = ident
    shared["dft128"], shared["dftS"], shared["dftP"] = dft_consts()
    shared["ropeC"], shared["ropeS"] = rope_consts()
    tri = np.triu(np.ones((64, 64), np.float32))
    shared["hmask"] = np.ascontiguousarray(np.stack([tri, tri.T]))
    sidx = np.arange(128) // 16
    tf = (sidx[:, None] <= sidx[None, :]).astype(np.float32)
    shared["tmask"] = np.ascontiguousarray(np.stack([tf, tf.T]))
    maps = []
    for i in range(n_cores):
        m = dict(shared)
        m["x_s"] = f(inputs["x_sample"][i])
        m["x_p"] = f(inputs["x_prompt"][2 * i:2 * i + 2]).reshape(2 * LP, D)
        m["c"] = f(inputs["c"][i:i + 1])
        m["cache_k"] = f(inputs["cache_k"][i]); m["cache_v"] = f(inputs["cache_v"][i])
        m["s5_h0_re"] = f(inputs["state_s5_re"][i]); m["s5_h0_im"] = f(inputs["state_s5_im"][i])
        m["hg_s0"] = f(inputs["state_hgrn"][i])
        maps.append(m)
    return maps


def kernel(**inputs):
    nc, kb = build()
    maps = make_in_maps(inputs)
    res = run_bass_kernel_spmd(nc, maps, core_ids=list(range(8))).results
    y_p = np.concatenate([r["y_p"].reshape(2, LP, D) for r in res], axis=0)
    y_s = np.stack([r["y_s"] for r in res], axis=0)
    nk = np.concatenate([r["nk"] for r in res], axis=0)
    nv = np.concatenate([r["nv"] for r in res], axis=0)
    s5r = np.concatenate([r["ns5re"] for r in res], axis=0)
    s5i = np.concatenate([r["ns5im"] for r in res], axis=0)
    hg = np.concatenate([r["nhg"] for r in res], axis=0)
    return (y_p.astype(np.float32), y_s.astype(np.float32), nk.astype(np.float32), nv.astype(np.float32),
            s5r.astype(np.float32), s5i.astype(np.float32), hg.astype(np.float32))
```

```python
import numpy as np
import ml_dtypes
from contextlib import ExitStack
import concourse.bass as bass
import concourse.mybir as mybir
from concourse.bass_utils import run_bass_kernel_spmd

F32 = mybir.dt.float32
BF16 = mybir.dt.bfloat16
AF = mybir.ActivationFunctionType
ALU = mybir.AluOpType
AX = mybir.AxisListType

D = 2048
DEPTH = 2
LS = 4096
LP = 256
NTOK = LS + 2 * LP
NT = NTOK // 128
INC = 15360
PAST = 512
NCH = LS // 8 + 2 * (LP // 8)
EPS = 1e-6
NR = 12
NW_TBS = None
RELAX_OWN = True
NW_CBS = None


class Tk:
    __slots__ = ("w", "r", "pend", "name")

    def __init__(self, name=""):
        self.w = {}
        self.r = {}
        self.pend = None
        self.name = name


class KB:
    def __init__(self, nc):
        self.nc = nc
        self.top = ExitStack()
        self.stk = self.top
        self.eng = {}
        for nm, e in (("pe", nc.tensor), ("act", nc.scalar), ("dve", nc.vector),
                      ("pool", nc.gpsimd), ("sp", nc.sync)):
            sem = self.top.enter_context(nc.semaphore("p_" + nm))
            self.eng[nm] = dict(e=e, sem=sem, n=0, seen={}, pend=[])
        self.rings = {}
        for q in ("sp", "pool", "act"):
            sems = [self.top.enter_context(nc.semaphore(f"d_{q}{i}")) for i in range(NR)]
            self.rings[q] = dict(sems=sems, cnt=[0] * NR, i=0)
        self.uid = 0
        self.ninst = 0
        self.debug = None
        self.dumped = set()

    def sb(self, shape, dt, name="t"):
        self.uid += 1
        t = self.stk.enter_context(self.nc.sbuf_tensor(f"{name}_{self.uid}", list(shape), dt))
        return t, Tk(name)

    def ps(self, shape, dt, name="p"):
        self.uid += 1
        t = self.stk.enter_context(self.nc.psum_tensor(f"{name}_{self.uid}", list(shape), dt))
        return t, Tk(name)

    def _waits(self, en, R, W, Wp):
        E = self.eng[en]
        need = {}

        def add(d, skip_own=False):
            for sem, (val, owner) in d.items():
                if owner == en and (en == "pe" or skip_own):
                    continue
                if need.get(sem, (0, None))[0] < val:
                    need[sem] = (val, owner)
        for t in R:
            assert t.pend in (None, en), (t.name, t.pend, en)
            add(t.w)
        for t in W:
            assert t.pend in (None, en), (t.name, t.pend, en)
            add(t.w, RELAX_OWN)
            add(t.r, RELAX_OWN)
        for t in Wp:
            add(t.r)
        for sem, (val, owner) in need.items():
            if E["seen"].get(sem, 0) >= val:
                continue
            E["e"].wait_ge(sem, val)
            self.ninst += 1
            E["seen"][sem] = val

    def op(self, en, fn, R=(), W=(), inc=True):
        self._waits(en, R, W, ())
        E = self.eng[en]
        ins = fn(E["e"])
        self.ninst += 1
        if inc:
            E["n"] += 1
            ins.then_inc(E["sem"], 1)
            tk = (E["n"], en)
            for (t, kind) in E["pend"]:
                if kind == "r":
                    t.r[E["sem"]] = tk
                else:
                    t.w[E["sem"]] = tk
                t.pend = None
            E["pend"] = []
            for t in R:
                t.r[E["sem"]] = tk
            for t in W:
                t.w = {E["sem"]: tk}
                t.r = {}
        else:
            for t in R:
                E["pend"].append((t, "r"))
                t.pend = en
            for t in W:
                t.w = {}
                t.r = {}
                E["pend"].append((t, "w"))
                t.pend = en
        return ins

    def dma(self, q, out, in_, R=(), W=(), Wp=(), slow=False):
        self._waits(q, R, W, Wp)
        E = self.eng[q]
        ring = self.rings[q]
        i = ring["i"]
        ring["i"] = (i + 1) % NR
        sem = ring["sems"][i]
        if ring["cnt"][i] > 0 and E["seen"].get(sem, 0) < ring["cnt"][i]:
            E["e"].wait_ge(sem, ring["cnt"][i])
            E["seen"][sem] = ring["cnt"][i]
            self.ninst += 1
        if slow:
            E["e"].dma_start(out=out, in_=in_, allow_slow_non_contiguous=True).then_inc(sem, 16)
        else:
            E["e"].dma_start(out=out, in_=in_).then_inc(sem, 16)
        self.ninst += 1
        ring["cnt"][i] += 16
        tk = (ring["cnt"][i], None)
        for t in R:
            t.r[sem] = tk
        for t in W:
            t.w = {sem: tk}
            t.r = {}
        for t in Wp:
            t.w[sem] = tk

    def dump(self, name, ap, tk, shape, dt=F32):
        if not self.debug or name not in self.debug or name in self.dumped:
            return
        self.dumped.add(name)
        d = self.nc.dram_tensor(name, list(shape), dt, kind="ExternalOutput").ap()
        self.dma("sp", d, ap, R=[tk])

    def barrier(self):
        for en, E in self.eng.items():
            assert not E["pend"], en
        for en, E in self.eng.items():
            for on, O in self.eng.items():
                if on != en and O["n"] > E["seen"].get(O["sem"], 0):
                    E["e"].wait_ge(O["sem"], O["n"])
                    E["seen"][O["sem"]] = O["n"]
                    self.ninst += 1
            for q, ring in self.rings.items():
                for sem, c in zip(ring["sems"], ring["cnt"]):
                    if c > E["seen"].get(sem, 0):
                        E["e"].wait_ge(sem, c)
                        E["seen"][sem] = c
                        self.ninst += 1


class Rot:
    def __init__(self, kb, n, shape, dt, name, psum=False):
        self.items = [(kb.ps if psum else kb.sb)(shape, dt, name) for _ in range(n)]
        self.i = 0

    def next(self):
        it = self.items[self.i]
        self.i = (self.i + 1) % len(self.items)
        return it


BLK = {0: ("fm", "ua"), 1: ("fm", "ga"), 2: ("fs", "ub"), 3: ("fs", "gb"), 4: ("tm", "q0"), 5: ("tm", "q1"),
       6: ("tm", "kv"), 7: ("fm", "gc0"), 8: ("fm", "gc1"), 9: ("fm", "qd"), 10: ("tm", "id"),
       11: ("fm", "zf"), 12: ("fm", "zb"), 13: ("fm", "gd")}
for _b in range(14, 30):
    BLK[_b] = ("fm", "m")


def tok_src(P, g):
    if g < 32:
        return P["xs_cur"][g * 128:(g + 1) * 128, :]
    return P["xp_cur"][(g - 32) * 128:(g - 31) * 128, :]


def build(debug=None, nlayers=DEPTH, stages=None):
    nc = bass.Bass("TRN2", target_bir_lowering=False)
    kb = KB(nc)
    kb.debug = debug
    P = {}

    def din(name, shape, dt=F32):
        P[name] = nc.dram_tensor(name, list(shape), dt, kind="ExternalInput").ap()
        return P[name]

    def dout(name, shape, dt=F32):
        P[name] = nc.dram_tensor(name, list(shape), dt, kind="ExternalOutput").ap()
        return P[name]

    def dscr(name, shape, dt):
        kind = "ExternalOutput" if (debug and name in debug) else "Internal"
        P[name] = nc.dram_tensor(name, list(shape), dt, kind=kind).ap()
        return P[name]

    din("x_s", [LS, D]); din("x_p", [2 * LP, D]); din("c", [1, D]); din("c_ctx", [D])
    din("cache_k", [DEPTH, PAST, 2, 128]); din("cache_v", [DEPTH, PAST, 2, 128])
    din("s5_h0_re", [DEPTH, 2, 32, 64]); din("s5_h0_im", [DEPTH, 2, 32, 64])
    din("hg_s0", [DEPTH, 2, 4, 128, 128])
    din("norm_pre", [DEPTH, D]); din("norm_post", [DEPTH, D])
    din("w_mod", [DEPTH, D, 3 * D]); din("b_mod", [DEPTH, 3 * D]); din("w_in", [DEPTH, D, INC])
    din("fourier_w", [DEPTH, 4, 128, 128])
    for n in ("s5_lambda_re", "s5_lambda_im"):
        din(n, [DEPTH, 2, 32, 64])
    din("s5_log_step", [DEPTH, 2, 32])
    for n in ("s5_b_re", "s5_b_im"):
        din(n, [DEPTH, 2, 32, 64, 16])
    for n in ("s5_c_re", "s5_c_im"):
        din(n, [DEPTH, 2, 32, 16, 64])
    din("s5_d", [DEPTH, 512]); din("s5_glu_w", [DEPTH, 512, 512]); din("s5_glu_b", [DEPTH, 512])
    din("q_norm", [DEPTH, 128]); din("k_norm", [DEPTH, 128]); din("hgrn_lb_logits", [DEPTH, 2, 512])
    din("hgrn_norm", [DEPTH, 128])
    din("w_proj_a", [DEPTH, 512, D]); din("w_proj_b", [DEPTH, 512, D]); din("w_proj_c", [DEPTH, 1024, D])
    din("w_proj_d", [DEPTH, 512, D]); din("w_out", [DEPTH, D, D])
    din("ident", [128, 128])
    din("ropeC", [LS, 640]); din("ropeS", [LS, 640]); din("hmask", [2, 64, 64]); din("tmask", [2, 128, 128])
    din("dft128", [2, 128, 128]); din("dftS", [2, LS, LS], BF16); din("dftP", [2, LP, LP], BF16)
    dout("y_s", [LS, D]); dout("y_p", [2 * LP, D])
    dout("nk", [2, DEPTH, LP, 2, 128]); dout("nv", [2, DEPTH, LP, 2, 128])
    dout("ns5re", [2, DEPTH, 2, 32, 64]); dout("ns5im", [2, DEPTH, 2, 32, 64])
    dout("nhg", [2, DEPTH, 2, 4, 128, 128])
    dscr("x1_s", [LS, D], F32); dscr("x1_p", [2 * LP, D], F32)
    dscr("gpD", [2, D], F32)
    dscr("uaT", [512, NTOK], BF16); dscr("gaT", [512, NTOK], BF16)
    dscr("UD", [512, 8, NCH], BF16); dscr("gbP", [512, 8, NCH], BF16)
    dscr("qc", [NTOK, 1024], F32); dscr("kvc", [NTOK, 512], F32)
    dscr("gcT", [1024, NTOK], BF16)
    dscr("qdT", [512, NTOK], F32); dscr("idm", [NTOK, 512], BF16)
    dscr("zfT", [512, NTOK], F32); dscr("zbT", [512, NTOK], F32); dscr("gdT", [512, NTOK], BF16)
    dscr("mT", [8192, NTOK], BF16)
    dscr("ygT", [2560, NTOK], BF16); dscr("mixD", [D, NTOK], BF16)
    dscr("wpB", [2560, D], BF16); dscr("woB", [D, D], BF16); dscr("wmB", [D, 3 * D], BF16)
    dscr("s5par", [2, 34, 2048], F32); dscr("YD", [512, 8, NCH], F32)
    TK = {n: Tk(n) for n in P}

    ident, ident_k = kb.sb([128, 128], F32, "ident")
    kb.dma("sp", ident[:], P["ident"][:, :], W=[ident_k])
    scT, scT_k = kb.sb([128, 16, 2], BF16, "scT")
    modS, modS_k = kb.sb([128, 48, 2], F32, "modS")
    gs, gs_k = kb.sb([128, 16, 2], F32, "gs")

    with ExitStack() as st:
        kb.stk = st
        cT, cT_k = kb.sb([128, 16, 2], F32, "cT")
        kb.dma("sp", cT[:, :, 0], P["c"].rearrange("o (k p) -> p (o k)", p=128), W=[cT_k], slow=True)
        kb.dma("sp", cT[:, :, 1], P["c_ctx"].rearrange("(k p) -> p k", p=128), Wp=[cT_k], slow=True)
        kb.op("act", lambda e: e.activation(out=scT[:], in_=cT[:], func=AF.Silu), R=[cT_k], W=[scT_k])
        kb.barrier()
    kb.stk = kb.top

    def want(s):
        return stages is None or s in stages

    for l in range(nlayers):
        P["xs_cur"] = P["x_s"] if l == 0 else P["x1_s"]
        P["xp_cur"] = P["x_p"] if l == 0 else P["x1_p"]
        if want("mod"):
            stage_mod(kb, P, TK, l, scT, scT_k, modS, modS_k, gs, gs_k)
        if want("p"):
            r0 = 0
            for nm, nr in (("w_proj_a", 512), ("w_proj_b", 512), ("w_proj_c", 1024), ("w_proj_d", 512)):
                for rr in range(0, nr, 512):
                    kb.dma("pool", P["wpB"][r0 + rr:r0 + rr + 512, :], P[nm][l, rr:rr + 512, :], Wp=[TK["wpB"]])
                r0 += nr
            for rr in range(0, D, 512):
                kb.dma("pool", P["woB"][rr:rr + 512, :], P["w_out"][l, rr:rr + 512, :], Wp=[TK["woB"]])
        if want("nw"):
            stage_nw(kb, P, TK, l, ident, ident_k, modS, modS_k, gs, gs_k)
        if want("fourier"):
            stage_fourier(kb, P, TK, l)
        if want("attn"):
            if l == 0 and nlayers > 1 and want("mod"):
                for rr in range(0, D, 256):
                    kb.dma("pool", P["wmB"][rr:rr + 256, :], P["w_mod"][1, rr:rr + 256, :], Wp=[TK["wmB"]])
            stage_attn(kb, P, TK, l, ident, ident_k)
        if want("hgrn"):
            stage_hgrn(kb, P, TK, l, ident, ident_k)
        if want("s5"):
            stage_s5(kb, P, TK, l, ident, ident_k)
        if want("p"):
            stage_p(kb, P, TK, l, l == nlayers - 1)

    kb.barrier()
    kb.top.close()
    return nc, kb


def stage_mod(kb, P, TK, l, scT, scT_k, modS, modS_k, gs, gs_k):
    with ExitStack() as st:
        kb.stk = st
        bm, bm_k = kb.sb([128, 48], F32, "bm")
        npre, npre_k = kb.sb([128, 16], F32, "npre")
        npost, npost_k = kb.sb([128, 16], F32, "npost")
        gp, gp_k = kb.sb([128, 16, 2], F32, "gp")
        kb.dma("sp", bm[:], P["b_mod"][l].rearrange("(j p) -> p j", p=128), W=[bm_k], slow=True)
        kb.dma("sp", npre[:], P["norm_pre"][l].rearrange("(j p) -> p j", p=128), W=[npre_k], slow=True)
        kb.dma("sp", npost[:], P["norm_post"][l].rearrange("(j p) -> p j", p=128), W=[npost_k], slow=True)
        wrot = Rot(kb, 3, [128, 16, 512], BF16, "wm")
        psm, psm_k = kb.ps([128, 48, 2], F32, "psm")
        pre = (l == 1)
        wsrc = (P["wmB"] if pre else P["w_mod"][l]).rearrange("(k p) c -> p k c", p=128)
        for cb in range(12):
            w, w_k = wrot.next()
            if pre:
                kb.dma("sp", w[:], wsrc[:, :, cb * 512:(cb + 1) * 512], R=[TK["wmB"]], W=[w_k])
            else:
                kb.dma("pool", w[:], wsrc[:, :, cb * 512:(cb + 1) * 512], W=[w_k])
            for j in range(4):
                for k in range(16):
                    kb.op("pe", lambda e, w=w, j=j, k=k, cb=cb: e.matmul(
                        psm[:, cb * 4 + j, :], lhsT=w[:, k, j * 128:(j + 1) * 128], rhs=scT[:, k, :],
                        start=(k == 0), stop=(k == 15)),
                        R=[w_k, scT_k], W=[psm_k], inc=(k == 15))
        for r in range(2):
            kb.op("dve", lambda e, r=r: e.tensor_tensor(out=modS[:, :, r], in0=psm[:, :, r], in1=bm[:], op=ALU.add),
                  R=[psm_k, bm_k], W=[modS_k])
        for r in range(2):
            kb.op("dve", lambda e, r=r: e.scalar_tensor_tensor(
                out=gs[:, :, r], in0=modS[:, 16:32, r], scalar=1.0, in1=npre[:], op0=ALU.add, op1=ALU.mult),
                R=[modS_k, npre_k], W=[gs_k])
            kb.op("dve", lambda e, r=r: e.tensor_tensor(out=gp[:, :, r], in0=modS[:, 32:48, r], in1=npost[:], op=ALU.mult),
                  R=[modS_k, npost_k], W=[gp_k])
        for r in range(2):
            kb.dma("sp", P["gpD"][r].rearrange("(k p) -> p k", p=128), gp[:, :, r], R=[gp_k], Wp=[TK["gpD"]], slow=True)
        kb.barrier()
    kb.stk = kb.top


def stage_nw(kb, P, TK, l, ident, ident_k, modS, modS_k, gs, gs_k, TBS=((0, 2560), (2560, 2048))):
    nc = kb.nc
    TB = max(n for _, n in TBS)
    tpb_max = TB // 128
    with ExitStack() as st:
        kb.stk = st
        hT, hT_k = kb.sb([128, 16, TB], BF16, "hT")
        hT_tk = [Tk(f"hT{i}") for i in range(tpb_max)]
        xrot = Rot(kb, 2, [128, D], F32, "xt")
        xsrot = Rot(kb, 2, [128, D], F32, "xs")
        strot = Rot(kb, 2, [128, 4], F32, "stat")
        mhalf, mhalf_k = kb.sb([128, 1], F32, "mhalf")
        kb.op("dve", lambda e: e.memset(mhalf[:], -0.5), W=[mhalf_k])
        wrot = Rot(kb, 3, [128, 16, 512], BF16, "win")
        pT = Rot(kb, 2, [128, 4, 128], F32, "pT", psum=True)
        pM = Rot(kb, 4, [128, 512], F32, "pM", psum=True)
        so32 = Rot(kb, 3, [128, 512], F32, "so32")
        so16 = Rot(kb, 3, [128, 512], BF16, "so16")
        wsrc = P["w_in"][l].rearrange("(k p) c -> p k c", p=128)
        evi = [0]

        for tb in (NW_TBS if NW_TBS is not None else range(len(TBS))):
            tok0, TBn = TBS[tb]
            tpb = TBn // 128
            for tt in range(tpb):
                g = tok0 // 128 + tt
                r = 0 if g < 32 else 1
                xt, xt_k = xrot.next()
                xs, xs_k = xsrot.next()
                stt, stt_k = strot.next()
                kb.dma("sp", xt[:], tok_src(P, g), W=[xt_k])
                kb.op("act", lambda e, xs=xs, xt=xt, stt=stt: e.activation(
                    out=xs[:], in_=xt[:], func=AF.Square, accum_out=stt[:, 0:1]), R=[xt_k], W=[xs_k, stt_k])
                kb.op("dve", lambda e, stt=stt: e.tensor_scalar(
                    out=stt[:, 1:2], in0=stt[:, 0:1], scalar1=1.0 / D, scalar2=EPS, op0=ALU.mult, op1=ALU.add),
                    R=[stt_k], W=[stt_k])
                kb.op("pool", lambda e, stt=stt: e.tensor_tensor(
                    out=stt[:, 2:3], in0=stt[:, 1:2], in1=mhalf[:], op=ALU.pow), R=[stt_k, mhalf_k], W=[stt_k])
                kb.op("dve", lambda e, xs=xs, xt=xt, stt=stt: e.tensor_scalar(
                    out=xs[:], in0=xt[:], scalar1=stt[:, 2:3], scalar2=None, op0=ALU.mult),
                    R=[xt_k, stt_k], W=[xs_k])
                for q4 in range(4):
                    pt, pt_k = pT.next()
                    for i in range(4):
                        k = q4 * 4 + i
                        kb.op("pe", lambda e, pt=pt, i=i, k=k, xs=xs: e.transpose(
                            pt[:, i, :], xs[:, k * 128:(k + 1) * 128], ident[:]),
                            R=[xs_k, ident_k], W=[pt_k], inc=(i == 3))
                    for i in range(4):
                        k = q4 * 4 + i
                        kb.op("dve", lambda e, pt=pt, i=i, k=k, tt=tt, r=r: e.tensor_scalar(
                            out=hT[:, k, tt * 128:(tt + 1) * 128], in0=pt[:, i, :],
                            scalar1=gs[:, k, r:r + 1], scalar2=modS[:, k, r:r + 1], op0=ALU.mult, op1=ALU.add),
                            R=[pt_k, gs_k, modS_k], W=[hT_tk[tt]])
            for cb in (NW_CBS if NW_CBS is not None else range(30)):
                kind, nm = BLK[cb]
                w, w_k = wrot.next()
                kb.dma("pool", w[:], wsrc[:, :, cb * 512:(cb + 1) * 512], W=[w_k])
                if kind == "tm":
                    for tt in range(tpb):
                        ps, ps_k = pM.next()
                        for k in range(16):
                            kb.op("pe", lambda e, ps=ps, k=k, tt=tt, w=w: e.matmul(
                                ps[:], lhsT=hT[:, k, tt * 128:(tt + 1) * 128], rhs=w[:, k, :],
                                start=(k == 0), stop=(k == 15)),
                                R=[w_k, hT_tk[tt]], W=[ps_k], inc=(k == 15))
                        t0 = tok0 + tt * 128
                        if nm in ("q0", "q1"):
                            o, o_k = so32.next()
                            kb.op("dve", lambda e, o=o, ps=ps: e.tensor_copy(out=o[:], in_=ps[:]), R=[ps_k], W=[o_k])
                            c0 = 0 if nm == "q0" else 512
                            kb.dma("sp", P["qc"][t0:t0 + 128, c0:c0 + 512], o[:], R=[o_k], Wp=[TK["qc"]])
                        elif nm == "kv":
                            o, o_k = so32.next()
                            kb.op("dve", lambda e, o=o, ps=ps: e.tensor_copy(out=o[:], in_=ps[:]), R=[ps_k], W=[o_k])
                            kb.dma("sp", P["kvc"][t0:t0 + 128, :], o[:], R=[o_k], Wp=[TK["kvc"]])
                        else:
                            o2, o2_k = so16.next()
                            kb.op("dve", lambda e, o2=o2, ps=ps: e.tensor_copy(out=o2[:], in_=ps[:]), R=[ps_k], W=[o2_k])
                            kb.dma("sp", P["idm"][t0:t0 + 128, :], o2[:], R=[o2_k], Wp=[TK["idm"]])
                elif kind == "fm":
                    for j in range(4):
                        for t5 in range(TBn // 512):
                            ps, ps_k = pM.next()
                            for k in range(16):
                                kb.op("pe", lambda e, ps=ps, k=k, j=j, t5=t5, w=w: e.matmul(
                                    ps[:], lhsT=w[:, k, j * 128:(j + 1) * 128], rhs=hT[:, k, t5 * 512:(t5 + 1) * 512],
                                    start=(k == 0), stop=(k == 15)),
                                    R=[w_k] + hT_tk[t5 * 4:(t5 + 1) * 4], W=[ps_k], inc=(k == 15))
                            t0 = tok0 + t5 * 512
                            if nm in ("qd", "zf", "zb"):
                                o, o_k = so32.next()
                                kb.op("dve", lambda e, o=o, ps=ps: e.tensor_copy(out=o[:], in_=ps[:]), R=[ps_k], W=[o_k])
                                dst = {"qd": "qdT", "zf": "zfT", "zb": "zbT"}[nm]
                                kb.dma("sp", P[dst][j * 128:(j + 1) * 128, t0:t0 + 512], o[:], R=[o_k], Wp=[TK[dst]])
                            elif nm == "ua":
                                o2, o2_k = so16.next()
                                kb.op("dve", lambda e, o2=o2, ps=ps: e.tensor_copy(out=o2[:], in_=ps[:]), R=[ps_k], W=[o2_k])
                                kb.dma("sp", P["uaT"][j * 128:(j + 1) * 128, t0:t0 + 512], o2[:], R=[o2_k], Wp=[TK["uaT"]])
                            else:
                                o2, o2_k = so16.next()
                                fn = AF.Sigmoid if nm == "m" else AF.Silu
                                kb.op("act", lambda e, o2=o2, ps=ps, fn=fn: e.activation(out=o2[:], in_=ps[:], func=fn), R=[ps_k], W=[o2_k])
                                if nm == "m":
                                    row0 = (cb - 14) * 512 + j * 128
                                    dst = "mT"
                                else:
                                    dst = {"ga": "gaT", "gc0": "gcT", "gc1": "gcT", "gd": "gdT"}[nm]
                                    row0 = j * 128 + (512 if nm == "gc1" else 0)
                                kb.dma("sp", P[dst][row0:row0 + 128, t0:t0 + 512], o2[:], R=[o2_k], Wp=[TK[dst]])
                else:
                    for j in range(4):
                        for t5 in range(TBn // 512):
                            ps, ps_k = pM.next()
                            for k in range(16):
                                kb.op("pe", lambda e, ps=ps, k=k, j=j, t5=t5, w=w: e.matmul(
                                    ps[:], lhsT=w[:, k, j * 128:(j + 1) * 128], rhs=hT[:, k, t5 * 512:(t5 + 1) * 512],
                                    start=(k == 0), stop=(k == 15)),
                                    R=[w_k] + hT_tk[t5 * 4:(t5 + 1) * 4], W=[ps_k], inc=(k == 15))
                            c0 = (tok0 + t5 * 512) // 8
                            o2, o2_k = so16.next()
                            o2v = o2[:].rearrange("p (t c) -> p t c", t=8)
                            psv = ps[:].rearrange("p (c t) -> p t c", t=8)
                            if nm == "ub":
                                kb.op("dve", lambda e, o2v=o2v, psv=psv: e.tensor_copy(out=o2v, in_=psv), R=[ps_k], W=[o2_k])
                                dst = "UD"
                            else:
                                kb.op("act", lambda e, o2v=o2v, psv=psv: e.activation(out=o2v, in_=psv, func=AF.Silu), R=[ps_k], W=[o2_k])
                                dst = "gbP"
                            kb.dma("sp", P[dst][j * 128:(j + 1) * 128, :, c0:c0 + 64],
                                   o2[:].rearrange("p (t c) -> p t c", t=8), R=[o2_k], Wp=[TK[dst]])
        kb.barrier()
    kb.stk = kb.top


def stage_fourier(kb, P, TK, l):
    with ExitStack() as st:
        kb.stk = st
        dc, dc_k = kb.sb([128, 2, 128], F32, "dc")
        fw, fw_k = kb.sb([128, 4, 128], F32, "fw")
        w12, w12_k = kb.sb([128, 4, 256], BF16, "w12")
        kb.dma("sp", dc[:], P["dft128"].rearrange("s p c -> p s c"), W=[dc_k])
        kb.dma("sp", fw[:], P["fourier_w"][l].rearrange("g p c -> p g c"), W=[fw_k])
        pw = Rot(kb, 2, [128, 512], F32, "pw", psum=True)
        for g in range(4):
            ps, ps_k = pw.next()
            for s_ in range(2):
                kb.op("pe", lambda e, ps=ps, g=g, s_=s_: e.matmul(
                    ps[:, s_ * 128:(s_ + 1) * 128], lhsT=dc[:, s_, :], rhs=fw[:, g, :], start=True, stop=True),
                    R=[dc_k, fw_k], W=[ps_k], inc=(s_ == 1))
            kb.op("dve", lambda e, ps=ps, g=g: e.tensor_copy(out=w12[:, g, 0:128], in_=ps[:, 0:128]), R=[ps_k], W=[w12_k])
            kb.op("dve", lambda e, ps=ps, g=g: e.tensor_scalar(
                out=w12[:, g, 128:256], in0=ps[:, 128:256], scalar1=-1.0, scalar2=None, op0=ALU.mult), R=[ps_k], W=[w12_k])
        a12, a12_k = kb.sb([128, 32, 4, 256], BF16, "a12")
        urot = Rot(kb, 3, [128, 4, 128], BF16, "ua")
        crot = Rot(kb, 2, [128, 8, 512], BF16, "dC")
        srot = Rot(kb, 2, [128, 8, 512], BF16, "dS")
        grot = Rot(kb, 3, [128, 512], BF16, "gt")
        orot = Rot(kb, 3, [128, 512], BF16, "yo")
        pacc = Rot(kb, 4, [128, 512], F32, "pacc", psum=True)
        for (tok0, L, dname) in ((0, LS, "dftS"), (LS, LP, "dftP"), (LS + LP, LP, "dftP")):
            nti = L // 128
            for i in range(nti):
                u, u_k = urot.next()
                t0 = tok0 + i * 128
                kb.dma("sp", u[:], P["uaT"][:, t0:t0 + 128].rearrange("(g c) t -> c g t", g=4), R=[TK["uaT"]], W=[u_k])
                ps, ps_k = pw.next()
                for g in range(4):
                    kb.op("pe", lambda e, ps=ps, g=g, u=u: e.matmul(
                        ps[:, 0:256] if False else ps[:, 0:256], lhsT=u[:, g, :], rhs=w12[:, g, :], start=True, stop=True),
                        R=[u_k, w12_k], W=[ps_k])
                    kb.op("dve", lambda e, ps=ps, g=g, i=i: e.tensor_copy(out=a12[:, i, g, :], in_=ps[:, 0:256]),
                          R=[ps_k], W=[a12_k])
            nb = min(512, L)
            igs = max(1, nti // 8)
            per = min(8, nti)
            for nblk in range(L // nb):
                accs = [pacc.next() for _ in range(4)]
                for ig in range(igs):
                    C, C_k = crot.next()
                    S, S_k = srot.next()
                    r0 = ig * per * 128
                    kb.dma("sp", C[:, 0:per, 0:nb], P[dname][0, r0:r0 + per * 128, nblk * nb:(nblk + 1) * nb].rearrange("(i p) n -> p i n", p=128), W=[C_k])
                    kb.dma("sp", S[:, 0:per, 0:nb], P[dname][1, r0:r0 + per * 128, nblk * nb:(nblk + 1) * nb].rearrange("(i p) n -> p i n", p=128), W=[S_k])
                    for g in range(4):
                        ps, ps_k = accs[g]
                        for i in range(per):
                            ti = ig * per + i
                            first = (ig == 0 and i == 0)
                            last = (ig == igs - 1 and i == per - 1)
                            kb.op("pe", lambda e, ps=ps, g=g, ti=ti, i=i, C=C, first=first: e.matmul(
                                ps[:, 0:nb], lhsT=a12[:, ti, g, 0:128], rhs=C[:, i, 0:nb], start=first, stop=False),
                                R=[a12_k, C_k], W=[ps_k], inc=False)
                            kb.op("pe", lambda e, ps=ps, g=g, ti=ti, i=i, S=S, last=last: e.matmul(
                                ps[:, 0:nb], lhsT=a12[:, ti, g, 128:256], rhs=S[:, i, 0:nb], start=False, stop=last),
                                R=[a12_k, S_k], W=[ps_k], inc=(i == per - 1))
                n0 = tok0 + nblk * nb
                for g in range(4):
                    ps, ps_k = accs[g]
                    gt, gt_k = grot.next()
                    o, o_k = orot.next()
                    kb.dma("sp", gt[:, 0:nb], P["gaT"][g * 128:(g + 1) * 128, n0:n0 + nb], R=[TK["gaT"]], W=[gt_k])
                    kb.op("dve", lambda e, ps=ps, gt=gt, o=o: e.tensor_tensor(out=o[:, 0:nb], in0=ps[:, 0:nb], in1=gt[:, 0:nb], op=ALU.mult),
                          R=[ps_k, gt_k], W=[o_k])
                    kb.dma("sp", P["ygT"][g * 128:(g + 1) * 128, n0:n0 + nb], o[:, 0:nb], R=[o_k], Wp=[TK["ygT"]])
        kb.barrier()
    kb.stk = kb.top


def stage_attn(kb, P, TK, l, ident, ident_k):
    SCALE = 128.0 ** -0.5
    with ExitStack() as st:
        kb.stk = st
        identb, identb_k = kb.sb([128, 128], BF16, "identb")
        kb.op("dve", lambda e: e.tensor_copy(out=identb[:], in_=ident[:]), R=[ident_k], W=[identb_k])
        G, G_k = kb.sb([128, 10, 128], F32, "G")
        kb.dma("sp", G[:, 0, :], P["q_norm"][l].partition_broadcast(128), W=[G_k])
        kb.dma("sp", G[:, 8, :], P["k_norm"][l].partition_broadcast(128), Wp=[G_k])
        for h in list(range(1, 8)) + [9]:
            kb.op("dve", lambda e, h=h: e.tensor_copy(out=G[:, h, :], in_=G[:, 0 if h < 8 else 8, :]), R=[G_k], W=[G_k])
        mh10, mh10_k = kb.sb([128, 10], F32, "mh10")
        kb.op("dve", lambda e: e.memset(mh10[:], -0.5), W=[mh10_k])
        QT, QT_k = kb.sb([128, 8, LS], BF16, "QT")
        KT, KT_k = kb.sb([128, 2, LS + PAST], BF16, "KT")
        VA, VA_k = kb.sb([128, 36, 2, 130], BF16, "VA")
        kb.op("dve", lambda e: e.memset(VA[:, :, :, 128:130], 1.0), W=[VA_k])
        qkrot = Rot(kb, 2, [128, 10, 128], F32, "qk")
        vrot = Rot(kb, 2, [128, 2, 128], F32, "vt")
        tmprot = Rot(kb, 2, [128, 10, 128], F32, "tmp")
        qnrot = Rot(kb, 2, [128, 10, 128], F32, "qn")
        strot = Rot(kb, 2, [128, 32], F32, "st")
        csrot = Rot(kb, 2, [128, 2, 640], F32, "cs")
        r4rot = Rot(kb, 2, [128, 4, 640], F32, "r4")
        qbrot = Rot(kb, 2, [128, 10, 128], BF16, "qb")
        ptq = Rot(kb, 1, [128, 8, 128], BF16, "ptq", psum=True)
        ptk = Rot(kb, 1, [128, 8, 128], BF16, "ptk", psum=True)
        pst = Rot(kb, 3, [128, 512], F32, "pst", psum=True)
        pacc = Rot(kb, 3, [128, 2, 130], F32, "pacc", psum=True)
        ptrot = Rot(kb, 4, [128, 512], BF16, "PT")
        recrot = Rot(kb, 4, [128, 1], F32, "rec")
        obrot = Rot(kb, 4, [128, 128], BF16, "ob")
        gtrot = Rot(kb, 2, [128, 512], BF16, "gt")
        yorot = Rot(kb, 2, [128, 512], BF16, "yo")

        def prep_tile(t0, tl, slot, rope_row, pb):
            qk, qk_k = qkrot.next(); vt, vt_k = vrot.next(); tmp, tmp_k = tmprot.next()
            qn, qn_k = qnrot.next(); stt, stt_k = strot.next(); qb, qb_k = qbrot.next()
            kb.dma("sp", qk[:, 0:8, :], P["qc"][t0:t0 + 128, :].rearrange("t (h d) -> t h d", h=8), R=[TK["qc"]], W=[qk_k])
            kb.dma("sp", qk[:, 8:10, :], P["kvc"][t0:t0 + 128, 0:256].rearrange("t (h d) -> t h d", h=2), R=[TK["kvc"]], Wp=[qk_k])
            kb.dma("sp", vt[:], P["kvc"][t0:t0 + 128, 256:512].rearrange("t (h d) -> t h d", h=2), R=[TK["kvc"]], W=[vt_k])
            kb.op("dve", lambda e: e.tensor_tensor(out=tmp[:], in0=qk[:], in1=qk[:], op=ALU.mult), R=[qk_k], W=[tmp_k])
            kb.op("dve", lambda e: e.tensor_reduce(out=stt[:, 0:10], in_=tmp[:], axis=AX.X, op=ALU.add), R=[tmp_k], W=[stt_k])
            kb.op("dve", lambda e: e.tensor_scalar(out=stt[:, 10:20], in0=stt[:, 0:10], scalar1=1.0 / 128, scalar2=EPS,
                                                   op0=ALU.mult, op1=ALU.add), R=[stt_k], W=[stt_k])
            kb.op("pool", lambda e: e.tensor_tensor(out=stt[:, 20:30], in0=stt[:, 10:20], in1=mh10[:], op=ALU.pow),
                  R=[stt_k, mh10_k], W=[stt_k])
            for h in range(10):
                kb.op("dve", lambda e, h=h: e.scalar_tensor_tensor(
                    out=qn[:, h, :], in0=qk[:, h, :], scalar=stt[:, 20 + h:21 + h], in1=G[:, h, :], op0=ALU.mult, op1=ALU.mult),
                    R=[qk_k, stt_k, G_k], W=[qn_k])
            if pb is not None:
                b, tp = pb
                kb.dma("sp", P["nk"][b, l, tp:tp + 128, :, :], qn[:, 8:10, :], R=[qn_k], Wp=[TK["nk"]])
                kb.dma("sp", P["nv"][b, l, tp:tp + 128, :, :], vt[:], R=[vt_k], Wp=[TK["nv"]])
                kb.op("pool", lambda e: e.tensor_copy(out=qb[:], in_=qn[:]), R=[qn_k], W=[qb_k])
            else:
                cs, cs_k = csrot.next(); r4, r4_k = r4rot.next()
                kb.dma("sp", cs[:, 0, :], P["ropeC"][rope_row:rope_row + 128, :], W=[cs_k])
                kb.dma("sp", cs[:, 1, :], P["ropeS"][rope_row:rope_row + 128, :], Wp=[cs_k])
                xv = qn[:].rearrange("p h (a b c) -> p h a b c", a=2, b=2, c=32)
                ov = qb[:].rearrange("p h (a b c) -> p h a b c", a=2, b=2, c=32)
                cv = cs[:, 0, :].rearrange("p (h a c) -> p h a c", h=10, a=2)
                sv = cs[:, 1, :].rearrange("p (h a c) -> p h a c", h=10, a=2)
                rv = [r4[:, i, :].rearrange("p (h a c) -> p h a c", h=10, a=2) for i in range(4)]
                x1, x2 = xv[:, :, :, 0, :], xv[:, :, :, 1, :]
                kb.op("dve", lambda e: e.tensor_tensor(out=rv[0], in0=x1, in1=cv, op=ALU.mult), R=[qn_k, cs_k], W=[r4_k])
                kb.op("dve", lambda e: e.tensor_tensor(out=rv[1], in0=x2, in1=sv, op=ALU.mult), R=[qn_k, cs_k], W=[r4_k])
                kb.op("dve", lambda e: e.tensor_tensor(out=ov[:, :, :, 0, :], in0=rv[0], in1=rv[1], op=ALU.subtract), R=[r4_k], W=[qb_k])
                kb.op("dve", lambda e: e.tensor_tensor(out=rv[2], in0=x2, in1=cv, op=ALU.mult), R=[qn_k, cs_k], W=[r4_k])
                kb.op("dve", lambda e: e.tensor_tensor(out=rv[3], in0=x1, in1=sv, op=ALU.mult), R=[qn_k, cs_k], W=[r4_k])
                kb.op("dve", lambda e: e.tensor_tensor(out=ov[:, :, :, 1, :], in0=rv[2], in1=rv[3], op=ALU.add), R=[r4_k], W=[qb_k])
            pq, pq_k = ptq.next(); pk, pk_k = ptk.next()
            for h in range(8):
                kb.op("pe", lambda e, h=h: e.transpose(pq[:, h, :], qb[:, h, :], identb[:]), R=[qb_k, identb_k], W=[pq_k], inc=(h == 7))
            for h in range(2):
                kb.op("pe", lambda e, h=h: e.transpose(pk[:, h, :], qb[:, 8 + h, :], identb[:]), R=[qb_k, identb_k], W=[pk_k], inc=(h == 1))
            kb.op("act", lambda e: e.copy(out=QT[:, :, tl:tl + 128], in_=pq[:]), R=[pq_k], W=[QT_k])
            kb.op("dve", lambda e: e.tensor_copy(out=KT[:, :, tl:tl + 128], in_=pk[:, 0:2, :]), R=[pk_k], W=[KT_k])
            kb.op("pool", lambda e: e.tensor_copy(out=VA[:, slot, :, 0:128], in_=vt[:]), R=[vt_k], W=[VA_k])

        def ctx_tile(i):
            qk, qk_k = qkrot.next(); vt, vt_k = vrot.next(); qb, qb_k = qbrot.next()
            kb.dma("sp", qk[:, 0:2, :], P["cache_k"][l, i * 128:(i + 1) * 128, :, :], W=[qk_k])
            kb.dma("sp", vt[:], P["cache_v"][l, i * 128:(i + 1) * 128, :, :], W=[vt_k])
            kb.op("pool", lambda e: e.tensor_copy(out=qb[:, 0:2, :], in_=qk[:, 0:2, :]), R=[qk_k], W=[qb_k])
            pk, pk_k = ptk.next()
            for h in range(2):
                kb.op("pe", lambda e, h=h: e.transpose(pk[:, h, :], qb[:, h, :], identb[:]), R=[qb_k, identb_k], W=[pk_k], inc=(h == 1))
            kb.op("dve", lambda e: e.tensor_copy(out=KT[:, :, LS + i * 128:LS + (i + 1) * 128], in_=pk[:, 0:2, :]), R=[pk_k], W=[KT_k])
            kb.op("pool", lambda e: e.tensor_copy(out=VA[:, 32 + i, :, 0:128], in_=vt[:]), R=[vt_k], W=[VA_k])

        def core(tok0, Lq, nkt):
            nq = min(512, Lq)
            nqi = nq // 128
            for h in range(8):
                kvh = h // 4
                for qb_ in range(Lq // nq):
                    accs = [pacc.next() for _ in range((nqi + 1) // 2)]

                    def score(kt):
                        ps, ps_k = pst.next()
                        kb.op("pe", lambda e, ps=ps, kt=kt: e.matmul(
                            ps[:, 0:nq], lhsT=KT[:, kvh, kt * 128:(kt + 1) * 128], rhs=QT[:, h, qb_ * nq:(qb_ + 1) * nq],
                            start=True, stop=True), R=[KT_k, QT_k], W=[ps_k])
                        pt, pt_k = ptrot.next()
                        kb.op("act", lambda e, ps=ps, pt=pt: e.activation(out=pt[:, 0:nq], in_=ps[:, 0:nq], func=AF.Exp, scale=SCALE),
                              R=[ps_k], W=[pt_k])
                        return pt, pt_k

                    def pv(kt, pt, pt_k):
                        for qi in range(nqi):
                            acc, acc_k = accs[qi // 2]
                            kb.op("pe", lambda e, acc=acc, qi=qi, pt=pt, kt=kt: e.matmul(
                                acc[:, qi % 2, :], lhsT=pt[:, qi * 128:(qi + 1) * 128], rhs=VA[:, kt, kvh, :],
                                start=(kt == 0 and qi % 2 == 0), stop=(kt == nkt - 1)), R=[pt_k, VA_k], W=[acc_k], inc=(qi == nqi - 1))

                    LOOK = 2
                    pend = []
                    for kt in range(nkt + LOOK):
                        if kt < nkt:
                            pend.append((kt,) + score(kt))
                        if kt >= LOOK:
                            pv(*pend.pop(0))
                    po, po_k = ptq.next()
                    for qi in range(nqi):
                        acc, acc_k = accs[qi // 2]
                        rec, rec_k = recrot.next(); ob, ob_k = obrot.next()
                        kb.op("dve", lambda e, acc=acc, qi=qi, rec=rec: e.reciprocal(out=rec[:], in_=acc[:, qi % 2, 128:129]), R=[acc_k], W=[rec_k])
                        kb.op("dve", lambda e, acc=acc, qi=qi, rec=rec, ob=ob: e.tensor_scalar(
                            out=ob[:], in0=acc[:, qi % 2, 0:128], scalar1=rec[:, 0:1], scalar2=None, op0=ALU.mult), R=[acc_k, rec_k], W=[ob_k])
                        kb.op("pe", lambda e, qi=qi, ob=ob: e.transpose(po[:, qi, :], ob[:], identb[:]), R=[ob_k, identb_k], W=[po_k])
                    gt, gt_k = gtrot.next(); yo, yo_k = yorot.next()
                    n0 = tok0 + qb_ * nq
                    kb.dma("sp", gt[:, 0:nq], P["gcT"][h * 128:(h + 1) * 128, n0:n0 + nq], R=[TK["gcT"]], W=[gt_k])
                    kb.op("dve", lambda e, gt=gt, yo=yo: e.tensor_tensor(
                        out=yo[:, 0:nq], in0=po[:, 0:nqi, :].rearrange("p a b -> p (a b)"), in1=gt[:, 0:nq], op=ALU.mult),
                        R=[po_k, gt_k], W=[yo_k])
                    kb.dma("sp", P["ygT"][1024 + h * 128:1024 + (h + 1) * 128, n0:n0 + nq], yo[:, 0:nq], R=[yo_k], Wp=[TK["ygT"]])

        for i in range(32):
            prep_tile(i * 128, i * 128, i, i * 128, None)
        for i in range(4):
            ctx_tile(i)
        core(0, LS, 36)
        for b in range(2):
            for i in range(2):
                prep_tile(LS + b * LP + i * 128, i * 128, i, None, (b, i * 128))
            core(LS + b * LP, LP, 2)
        kb.barrier()
    kb.stk = kb.top


def stage_hgrn(kb, P, TK, l, ident, ident_k):
    with ExitStack() as st:
        kb.stk = st
        identb, identb_k = kb.sb([128, 128], BF16, "identb")
        kb.op("dve", lambda e: e.tensor_copy(out=identb[:], in_=ident[:]), R=[ident_k], W=[identb_k])
        mask, mask_k = kb.sb([64, 2, 64], F32, "mask")
        kb.dma("sp", mask[:], P["hmask"].rearrange("d s t -> s d t"), W=[mask_k])
        Gh, Gh_k = kb.sb([64, 128], F32, "Gh")
        kb.dma("sp", Gh[:], P["hgrn_norm"][l].partition_broadcast(64), W=[Gh_k])
        lg, lg_k = kb.sb([128, 2, 2, 4], F32, "lg")
        kb.dma("sp", lg[:], P["hgrn_lb_logits"].rearrange("l d (h p) -> p l d h", p=128), W=[lg_k], slow=True)
        lb, lb_k = kb.sb([128, 2, 4], F32, "lb")
        oml, oml_k = kb.sb([128, 2, 4], F32, "oml")
        if l == 0:
            kb.op("dve", lambda e: e.memset(lb[:], 0.0), W=[lb_k])
        else:
            kb.op("dve", lambda e: e.tensor_tensor(out=lb[:], in0=lg[:, 1, :, :], in1=lg[:, 0, :, :], op=ALU.subtract), R=[lg_k], W=[lb_k])
            kb.op("act", lambda e: e.activation(out=lb[:], in_=lb[:], func=AF.Sigmoid), R=[lb_k], W=[lb_k])
        kb.op("dve", lambda e: e.tensor_scalar(out=oml[:], in0=lb[:], scalar1=-1.0, scalar2=1.0, op0=ALU.mult, op1=ALU.add), R=[lb_k], W=[oml_k])
        mh, mh_k = kb.sb([64, 64], F32, "mh")
        kb.op("dve", lambda e: e.memset(mh[:], -0.5), W=[mh_k])

        qt, qt_k = kb.sb([128, LS], F32, "qt")
        A, A_k = kb.sb([128, LS], F32, "A")
        B, B_k = kb.sb([128, LS], F32, "B")
        C2, C2_k = kb.sb([128, LS], F32, "C2")
        qT, qT_k = kb.sb([128, 2, LS], BF16, "qT")
        kT, kT_k = kb.sb([128, 2, LS], BF16, "kT")
        kH, kH_k = kb.sb([128, 2, LS], BF16, "kH")
        V, V_k = kb.sb([64, LS // 64, 128], BF16, "V")
        onb = A[:].bitcast(BF16)[0:64, :].rearrange("p (c e) -> p c e", e=128)
        onb_k = A_k
        O, O_k = kb.sb([64, LS // 64, 128], F32, "O")
        O2, O2_k = kb.sb([64, LS // 64, 128], F32, "O2")
        stt, stt_k = kb.sb([128, 2, 3, LS // 64], F32, "stt")
        ost, ost_k = kb.sb([64, 3, LS // 64], F32, "ost")
        S = [[kb.sb([128, 128], F32, "S") for _ in range(2)] for _ in range(2)]
        Sbf = [[kb.sb([128, 128], BF16, "Sbf") for _ in range(2)] for _ in range(2)]
        tmpr = None
        ATr = [Rot(kb, 3, [64, 64], BF16, "AT") for _ in range(2)]
        for d_ in range(2):
            for (AT_, ATk_) in ATr[d_].items:
                kb.op("dve", lambda e, AT_=AT_: e.memset(AT_[:], 0.0), W=[ATk_])
        ktr = Rot(kb, 3, [64, 128], BF16, "ktok")
        ps_s = [kb.ps([128, 512], F32, "ps_s") for _ in range(2)]
        for (p_, pk_) in ps_s:
            kb.op("dve", lambda e, p_=p_: e.memset(p_[:], 0.0), W=[pk_])
        ps_t = Rot(kb, 2, [128, 1024], BF16, "ps_t", psum=True)
        ps_o = Rot(kb, 2, [128, 512], F32, "ps_o", psum=True)
        ps_kv = Rot(kb, 2, [128, 512], F32, "ps_kv", psum=True)
        gtr = Rot(kb, 2, [128, 512], BF16, "gt")
        yor = Rot(kb, 2, [128, 512], BF16, "yo")
        kvslots = [(ps_kv.items[i // 4][0][:, (i % 4) * 128:(i % 4 + 1) * 128], Tk("kv%d" % i)) for i in range(8)]

        def seq_head(tok0, L, h, segs):
            ncq = L // 64
            r0 = h * 128
            kb.dma("sp", qt[:, 0:L], P["qdT"][r0:r0 + 128, tok0:tok0 + L], R=[TK["qdT"]], W=[qt_k])
            kb.dma("sp", V[:, 0:ncq, :], P["idm"][tok0:tok0 + L, r0:r0 + 128].rearrange("(c s) e -> s c e", s=64), R=[TK["idm"]], W=[V_k])
            for d in range(2):
                zsrc = "zfT" if d == 0 else "zbT"
                kb.dma("sp", A[:, 0:L], P[zsrc][r0:r0 + 128, tok0:tok0 + L], R=[TK[zsrc]], W=[A_k])
                kb.op("act", lambda e: e.activation(out=A[:, 0:L], in_=A[:, 0:L], func=AF.Sigmoid), R=[A_k], W=[A_k])
                kb.op("dve", lambda e, d=d: e.tensor_scalar(out=A[:, 0:L], in0=A[:, 0:L], scalar1=oml[:, d, h:h + 1], scalar2=lb[:, d, h:h + 1],
                                                       op0=ALU.mult, op1=ALU.add), R=[A_k, oml_k, lb_k], W=[A_k])
                kb.op("dve", lambda e: e.tensor_scalar(out=B[:, 0:L], in0=A[:, 0:L], scalar1=1e-30, scalar2=None, op0=ALU.max), R=[A_k], W=[B_k])
                kb.op("act", lambda e: e.activation(out=B[:, 0:L], in_=B[:, 0:L], func=AF.Ln), R=[B_k], W=[B_k])
                kb.op("dve", lambda e: e.tensor_scalar(out=A[:, 0:L], in0=A[:, 0:L], scalar1=-1.0, scalar2=1.0, op0=ALU.mult, op1=ALU.add), R=[A_k], W=[A_k])
                src, src_k, dst, dst_k = B, B_k, C2, C2_k
                for sh in (1, 2, 4, 8, 16, 32):
                    sv = src[:, 0:L].rearrange("p (c s) -> p c s", s=64)
                    dv = dst[:, 0:L].rearrange("p (c s) -> p c s", s=64)
                    if d == 0:
                        kb.op("dve", lambda e, sv=sv, dv=dv, sh=sh: e.tensor_tensor(out=dv[:, :, sh:], in0=sv[:, :, sh:], in1=sv[:, :, :64 - sh], op=ALU.add), R=[src_k], W=[dst_k])
                        kb.op("pool", lambda e, sv=sv, dv=dv, sh=sh: e.tensor_copy(out=dv[:, :, :sh], in_=sv[:, :, :sh]), R=[src_k], W=[dst_k])
                    else:
                        kb.op("dve", lambda e, sv=sv, dv=dv, sh=sh: e.tensor_tensor(out=dv[:, :, :64 - sh], in0=sv[:, :, :64 - sh], in1=sv[:, :, sh:], op=ALU.add), R=[src_k], W=[dst_k])
                        kb.op("pool", lambda e, sv=sv, dv=dv, sh=sh: e.tensor_copy(out=dv[:, :, 64 - sh:], in_=sv[:, :, 64 - sh:]), R=[src_k], W=[dst_k])
                    src, src_k, dst, dst_k = dst, dst_k, src, src_k
                cv = B[:, 0:L].rearrange("p (c s) -> p c s", s=64)
                mv = cv[:, :, 32]
                lv = cv[:, :, 63] if d == 0 else cv[:, :, 0]
                kb.op("act", lambda e, d=d, lv=lv: e.activation(out=stt[:, d, 0, 0:ncq], in_=lv, func=AF.Exp), R=[B_k], W=[stt_k])
                kb.op("act", lambda e, d=d, mv=mv: e.activation(out=stt[:, d, 1, 0:ncq], in_=mv, func=AF.Exp), R=[B_k], W=[stt_k])
                kb.op("dve", lambda e, d=d, lv=lv, mv=mv: e.tensor_tensor(out=stt[:, d, 2, 0:ncq], in0=lv, in1=mv, op=ALU.subtract), R=[B_k], W=[stt_k])
                kb.op("act", lambda e, d=d: e.activation(out=stt[:, d, 2, 0:ncq], in_=stt[:, d, 2, 0:ncq], func=AF.Exp), R=[stt_k], W=[stt_k])
                lvb = lv.unsqueeze(2).to_broadcast([128, ncq, 64])
                kb.op("dve", lambda e, lvb=lvb: e.tensor_tensor(out=C2[:, 0:L].rearrange("p (c s) -> p c s", s=64), in0=lvb, in1=cv, op=ALU.subtract), R=[B_k], W=[C2_k])
                kb.op("act", lambda e: e.activation(out=C2[:, 0:L], in_=C2[:, 0:L], func=AF.Exp), R=[C2_k], W=[C2_k])
                kb.op("dve", lambda e, d=d: e.tensor_tensor(out=kH[:, d, 0:L], in0=A[:, 0:L], in1=C2[:, 0:L], op=ALU.mult), R=[A_k, C2_k], W=[kH_k])
                mvb = mv.unsqueeze(2).to_broadcast([128, ncq, 64])
                kb.op("dve", lambda e, mvb=mvb: e.tensor_tensor(out=C2[:, 0:L].rearrange("p (c s) -> p c s", s=64), in0=cv, in1=mvb, op=ALU.subtract), R=[B_k, C2_k], W=[C2_k])
                kb.op("act", lambda e: e.activation(out=B[:, 0:L], in_=C2[:, 0:L], func=AF.Exp), R=[C2_k], W=[B_k])
                kb.op("dve", lambda e, d=d: e.tensor_tensor(out=qT[:, d, 0:L], in0=qt[:, 0:L], in1=B[:, 0:L], op=ALU.mult), R=[qt_k, B_k], W=[qT_k])
                kb.op("act", lambda e: e.activation(out=B[:, 0:L], in_=C2[:, 0:L], func=AF.Exp, scale=-1.0), R=[C2_k], W=[B_k])
                kb.op("dve", lambda e, d=d: e.tensor_tensor(out=kT[:, d, 0:L], in0=A[:, 0:L], in1=B[:, 0:L], op=ALU.mult), R=[A_k, B_k], W=[kT_k])
            def FF(step, d, c):
                c0_ = c * 64
                cs = slice(c0_, c0_ + 64)
                pss, pss_k = ps_s[d]
                AT, AT_k = ATr[d].next()
                if d == 0:
                    kb.op("pe", lambda e: e.matmul(pss[0:32, 0:64], lhsT=kT[:, 0, c0_:c0_ + 32], rhs=qT[:, 0, c0_:c0_ + 64], start=True, stop=True),
                          R=[kT_k, qT_k], W=[pss_k], inc=False)
                    kb.op("pe", lambda e: e.matmul(pss[32:64, 32:64], lhsT=kT[:, 0, c0_ + 32:c0_ + 64], rhs=qT[:, 0, c0_ + 32:c0_ + 64], start=True, stop=True,
                                                   tile_position=(0, 32)), R=[kT_k, qT_k], W=[pss_k])
                else:
                    kb.op("pe", lambda e: e.matmul(pss[32:64, 0:64], lhsT=kT[:, 1, c0_ + 32:c0_ + 64], rhs=qT[:, 1, c0_:c0_ + 64], start=True, stop=True,
                                                   tile_position=(0, 32)), R=[kT_k, qT_k], W=[pss_k], inc=False)
                    kb.op("pe", lambda e: e.matmul(pss[0:32, 0:32], lhsT=kT[:, 1, c0_:c0_ + 32], rhs=qT[:, 1, c0_:c0_ + 32], start=True, stop=True),
                          R=[kT_k, qT_k], W=[pss_k])
                kb.op("dve", lambda e: e.tensor_tensor(out=AT[:], in0=pss[0:64, 0:64], in1=mask[:, d, :], op=ALU.mult), R=[pss_k, mask_k], W=[AT_k])
                pst, pst_k = ps_t.next()
                kb.op("pe", lambda e: e.transpose(pst[0:64, 0:128], kH[:, d, cs], identb[:]), R=[kH_k, identb_k], W=[pst_k])
                kt, kt_k = ktr.next()
                kb.op("act", lambda e: e.copy(out=kt[:], in_=pst[0:64, 0:128]), R=[pst_k], W=[kt_k])
                pkv, pkv_k = ps_kv.next()
                kb.op("pe", lambda e: e.matmul(pkv[:, 0:128], lhsT=kt[:], rhs=V[:, c, :], start=True, stop=True), R=[kt_k, V_k], W=[pkv_k])
                Sp, Sp_k = S[d][step % 2]
                Sn, Sn_k = S[d][(step + 1) % 2]
                Sb, Sb_k = Sbf[d][step % 2]
                kb.op("act", lambda e: e.mul(out=Sb[:], in_=Sp[:], mul=stt[:, d, 1, c:c + 1]), R=[Sp_k, stt_k], W=[Sb_k])
                kb.op("dve", lambda e: e.scalar_tensor_tensor(out=Sn[:], in0=Sp[:], scalar=stt[:, d, 0, c:c + 1], in1=pkv[:, 0:128], op0=ALU.mult, op1=ALU.add),
                      R=[pkv_k, stt_k, Sp_k], W=[Sn_k])
                return (c, AT, AT_k, Sb, Sb_k)

            def ST(step, d, ff):
                c, AT, AT_k, Sb, Sb_k = ff
                cs = slice(c * 64, (c + 1) * 64)
                pso, pso_k = ps_o.next()
                kb.op("pe", lambda e: e.matmul(pso[0:64, 0:128], lhsT=qT[:, d, cs], rhs=Sb[:], start=True, stop=False),
                      R=[qT_k, Sb_k], W=[pso_k], inc=False)
                kb.op("pe", lambda e: e.matmul(pso[0:64, 0:128], lhsT=AT[:], rhs=V[:, c, :], start=False, stop=True),
                      R=[AT_k, V_k], W=[pso_k])
                if d == 0:
                    kb.op("act", lambda e: e.copy(out=O[:, c, :], in_=pso[0:64, 0:128]), R=[pso_k], W=[O_k])
                else:
                    kb.op("act", lambda e: e.copy(out=O2[:, c, :], in_=pso[0:64, 0:128]), R=[pso_k], W=[O2_k])

            for (cbase, cnum, bidx) in segs:
                for d in range(2):
                    Sd, Sd_k = S[d][0]
                    if bidx is None:
                        kb.dma("sp", Sd[:], P["hg_s0"][l, d, h, :, :], W=[Sd_k])
                    else:
                        kb.op("dve", lambda e, Sd=Sd: e.memset(Sd[:], 0.0), W=[Sd_k])
                ffs = [{}, {}]
                for step in range(cnum + 1):
                    if step < cnum:
                        for d in range(2):
                            c = cbase + step if d == 0 else cbase + cnum - 1 - step
                            ffs[d][step] = FF(step, d, c)
                    if step >= 1:
                        for d in range(2):
                            ST(step - 1, d, ffs[d].pop(step - 1))
                if bidx is not None:
                    for d in range(2):
                        kb.dma("sp", P["nhg"][bidx, l, d, h, :, :], S[d][cnum % 2][0][:], R=[S[d][cnum % 2][1]], Wp=[TK["nhg"]])
            kb.op("dve", lambda e: e.tensor_tensor(out=O[:, 0:ncq, :], in0=O[:, 0:ncq, :], in1=O2[:, 0:ncq, :], op=ALU.add), R=[O_k, O2_k], W=[O_k])
            kb.op("dve", lambda e: e.tensor_tensor(out=O2[:, 0:ncq, :], in0=O[:, 0:ncq, :], in1=O[:, 0:ncq, :], op=ALU.mult), R=[O_k], W=[O2_k])
            kb.op("dve", lambda e: e.tensor_reduce(out=ost[:, 0, 0:ncq], in_=O2[:, 0:ncq, :], axis=AX.X, op=ALU.add), R=[O2_k], W=[ost_k])
            kb.op("dve", lambda e: e.tensor_scalar(out=ost[:, 1, 0:ncq], in0=ost[:, 0, 0:ncq], scalar1=1.0 / 128, scalar2=EPS, op0=ALU.mult, op1=ALU.add), R=[ost_k], W=[ost_k])
            kb.op("pool", lambda e: e.tensor_tensor(out=ost[:, 2, 0:ncq], in0=ost[:, 1, 0:ncq], in1=mh[:, 0:ncq], op=ALU.pow), R=[ost_k, mh_k], W=[ost_k])
            for c in range(ncq):
                kb.op("dve", lambda e, c=c: e.scalar_tensor_tensor(out=onb[:, c, :], in0=O[:, c, :], scalar=ost[:, 2, c:c + 1], in1=Gh[:], op0=ALU.mult, op1=ALU.mult),
                      R=[O_k, ost_k, Gh_k], W=[onb_k])
            nblk = max(1, L // 512)
            cpb = ncq // nblk
            for blk in range(nblk):
                pst, pst_k = ps_t.next()
                for i in range(cpb):
                    c = blk * cpb + i
                    kb.op("pe", lambda e, c=c, i=i, pst=pst: e.transpose(pst[:, i * 64:(i + 1) * 64], onb[:, c, :], identb[0:64, 0:64]),
                          R=[onb_k, identb_k], W=[pst_k], inc=(i == cpb - 1))
                n0 = tok0 + blk * 512
                nn = cpb * 64
                gt, gt_k = gtr.next(); yo, yo_k = yor.next()
                kb.dma("sp", gt[:, 0:nn], P["gdT"][r0:r0 + 128, n0:n0 + nn], R=[TK["gdT"]], W=[gt_k])
                kb.op("dve", lambda e, pst=pst, gt=gt, yo=yo, nn=nn: e.tensor_tensor(out=yo[:, 0:nn], in0=pst[:, 0:nn], in1=gt[:, 0:nn], op=ALU.mult),
                      R=[pst_k, gt_k], W=[yo_k])
                kb.dma("sp", P["ygT"][2048 + r0:2048 + r0 + 128, n0:n0 + nn], yo[:, 0:nn], R=[yo_k], Wp=[TK["ygT"]])

        for h in range(4):
            seq_head(0, LS, h, [(0, LS // 64, None)])
            seq_head(LS, 2 * LP, h, [(0, LP // 64, 0), (LP // 64, LP // 64, 1)])
        kb.barrier()
    kb.stk = kb.top


def stage_p(kb, P, TK, l, last):
    KR = ((0, 4), (4, 8), (8, 16), (16, 20))
    with ExitStack() as st:
        kb.stk = st
        wp, wp_k = kb.sb([128, 20, D], BF16, "wp")
        for kk in range(0, 20, 4):
            kb.dma("sp", wp[:, kk:kk + 4, :], P["wpB"][kk * 128:(kk + 4) * 128, :].rearrange("(k p) c -> p k c", p=128), R=[TK["wpB"]], Wp=[wp_k])
        ygr = Rot(kb, 2, [128, 20, 512], BF16, "yg")
        mixr = Rot(kb, 2, [128, 16, 512], BF16, "mixT")
        mrot = Rot(kb, 6, [128, 512], BF16, "mt")
        accrot = Rot(kb, 2, [128, 512], F32, "acc")
        tmprot = Rot(kb, 3, [128, 512], F32, "ptmp")
        pp = Rot(kb, 4, [128, 512], F32, "pp", psum=True)
        for tb in range(NTOK // 512):
            t0 = tb * 512
            yg, yg_k = ygr.next()
            mixT, mixT_k = mixr.next()
            kb.dma("sp", yg[:], P["ygT"][:, t0:t0 + 512].rearrange("(k p) t -> p k t", p=128), R=[TK["ygT"]], W=[yg_k])
            for ct in range(16):
                acc, acc_k = accrot.next()
                pss, mts = [], []
                for j, (ka, kb_) in enumerate(KR):
                    ps, ps_k = pp.next()
                    for k in range(ka, kb_):
                        kb.op("pe", lambda e, ps=ps, k=k, ct=ct, ka=ka, kb_=kb_, yg=yg: e.matmul(ps[:], lhsT=wp[:, k, ct * 128:(ct + 1) * 128], rhs=yg[:, k, :],
                                                                                        start=(k == ka), stop=(k == kb_ - 1)), R=[wp_k, yg_k], W=[ps_k], inc=(k == kb_ - 1))
                    mt, mt_k = mrot.next()
                    row0 = j * D + ct * 128
                    kb.dma("sp", mt[:], P["mT"][row0:row0 + 128, t0:t0 + 512], R=[TK["mT"]], W=[mt_k])
                    pss.append((ps, ps_k)); mts.append((mt, mt_k))
                tmps = [None] + [tmprot.next() for _ in range(3)]

                def gmul(j):
                    (ps, ps_k), (mt, mt_k) = pss[j], mts[j]
                    o, o_k = (acc, acc_k) if j == 0 else tmps[j]
                    kb.op("dve", lambda e: e.tensor_tensor(out=o[:], in0=ps[:], in1=mt[:], op=ALU.mult), R=[ps_k, mt_k], W=[o_k])

                def gadd(j):
                    tmp, tmp_k = tmps[j]
                    kb.op("dve", lambda e: e.tensor_tensor(out=acc[:], in0=acc[:], in1=tmp[:], op=ALU.add), R=[tmp_k, acc_k], W=[acc_k])
                gmul(0); gmul(1); gmul(2); gadd(1); gmul(3); gadd(2); gadd(3)
                kb.op("act", lambda e, acc=acc, ct=ct, mixT=mixT: e.copy(out=mixT[:, ct, :], in_=acc[:]), R=[acc_k], W=[mixT_k])
            kb.dma("sp", P["mixD"][:, t0:t0 + 512].rearrange("(k p) t -> p k t", p=128), mixT[:], R=[mixT_k], Wp=[TK["mixD"]])
        kb.barrier()
    kb.stk = kb.top
    with ExitStack() as st:
        kb.stk = st
        wo, wo_k = kb.sb([128, 16, D], BF16, "wo")
        for kk in range(0, 16, 4):
            kb.dma("sp", wo[:, kk:kk + 4, :], P["woB"][kk * 128:(kk + 4) * 128, :].rearrange("(k p) c -> p k c", p=128), R=[TK["woB"]], Wp=[wo_k])
        gprow, gprow_k = kb.sb([128, D], F32, "gprow")
        mhalf, mhalf_k = kb.sb([128, 1], F32, "mhalf")
        kb.op("dve", lambda e: e.memset(mhalf[:], -0.5), W=[mhalf_k])
        mixr = Rot(kb, 2, [128, 16, 512], BF16, "mixT2")
        orot = Rot(kb, 3, [128, D], F32, "oT")
        xrot = Rot(kb, 3, [128, D], F32, "xp")
        strot = Rot(kb, 3, [128, 4], F32, "pst")
        po = Rot(kb, 4, [128, 512], F32, "po", psum=True)
        for tb in range(NTOK // 512):
            r = 0 if tb < 8 else 1
            if tb == 0 or tb == 8:
                kb.dma("sp", gprow[:], P["gpD"][r].partition_broadcast(128), R=[TK["gpD"]], W=[gprow_k])
            t0 = tb * 512
            mixT, mixT_k = mixr.next()
            kb.dma("sp", mixT[:], P["mixD"][:, t0:t0 + 512].rearrange("(k p) t -> p k t", p=128), R=[TK["mixD"]], W=[mixT_k])
            for tt in range(4):
                g = tb * 4 + tt
                o, o_k = orot.next()
                xt, xt_k = xrot.next()
                stt, stt_k = strot.next()
                kb.dma("sp", xt[:], tok_src(P, g), W=[xt_k])
                for cb in range(4):
                    ps, ps_k = po.next()
                    for k in range(16):
                        kb.op("pe", lambda e, ps=ps, k=k, tt=tt, cb=cb, mixT=mixT: e.matmul(ps[:], lhsT=mixT[:, k, tt * 128:(tt + 1) * 128], rhs=wo[:, k, cb * 512:(cb + 1) * 512],
                                                                                       start=(k == 0), stop=(k == 15)), R=[mixT_k, wo_k], W=[ps_k], inc=(k == 15))
                    if cb % 2 == 0:
                        kb.op("act", lambda e, o=o, ps=ps, cb=cb: e.copy(out=o[:, cb * 512:(cb + 1) * 512], in_=ps[:]), R=[ps_k], W=[o_k])
                    else:
                        kb.op("dve", lambda e, o=o, ps=ps, cb=cb: e.tensor_copy(out=o[:, cb * 512:(cb + 1) * 512], in_=ps[:]), R=[ps_k], W=[o_k])
                jk, jk_k = xrot.next()
                kb.op("act", lambda e, jk=jk, o=o, stt=stt: e.activation(out=jk[:], in_=o[:], func=AF.Square, accum_out=stt[:, 0:1]), R=[o_k], W=[jk_k, stt_k])
                kb.op("dve", lambda e, stt=stt: e.tensor_scalar(out=stt[:, 1:2], in0=stt[:, 0:1], scalar1=1.0 / D, scalar2=EPS, op0=ALU.mult, op1=ALU.add), R=[stt_k], W=[stt_k])
                kb.op("pool", lambda e, stt=stt: e.tensor_tensor(out=stt[:, 2:3], in0=stt[:, 1:2], in1=mhalf[:], op=ALU.pow), R=[stt_k, mhalf_k], W=[stt_k])
                kb.op("dve", lambda e, o=o, stt=stt: e.scalar_tensor_tensor(out=o[:], in0=o[:], scalar=stt[:, 2:3], in1=gprow[:], op0=ALU.mult, op1=ALU.mult),
                      R=[o_k, stt_k, gprow_k], W=[o_k])
                kb.op("dve", lambda e, o=o, xt=xt: e.tensor_tensor(out=xt[:], in0=o[:], in1=xt[:], op=ALU.add), R=[o_k, xt_k], W=[xt_k])
                if g < 32:
                    dst = (P["y_s"] if last else P["x1_s"])[g * 128:(g + 1) * 128, :]
                    dk = TK["y_s" if last else "x1_s"]
                else:
                    dst = (P["y_p"] if last else P["x1_p"])[(g - 32) * 128:(g - 31) * 128, :]
                    dk = TK["y_p" if last else "x1_p"]
                kb.dma("sp", dst, xt[:], R=[xt_k], Wp=[dk])
        kb.barrier()
    kb.stk = kb.top


def sincos(kb, x, x_k, shape, sin_o, cos_o, o_k, tmps):
    MAGIC = 12582912.0
    TWO_PI = 6.283185307179586
    (t1, t1_k), (t2, t2_k), (t3, t3_k) = tmps
    for (off, out) in ((0.0, sin_o), (1.5707963267948966, cos_o)):
        kb.op("dve", lambda e, off=off: e.tensor_scalar(out=t3, in0=x, scalar1=off, scalar2=None, op0=ALU.add), R=[x_k], W=[t3_k])
        kb.op("dve", lambda e: e.tensor_scalar(out=t1, in0=t3, scalar1=1.0 / TWO_PI, scalar2=MAGIC, op0=ALU.mult, op1=ALU.add), R=[t3_k], W=[t1_k])
        kb.op("dve", lambda e: e.tensor_scalar(out=t2, in0=t1, scalar1=-MAGIC, scalar2=None, op0=ALU.add), R=[t1_k], W=[t2_k])
        kb.op("dve", lambda e: e.scalar_tensor_tensor(out=t1, in0=t2, scalar=-TWO_PI, in1=t3, op0=ALU.mult, op1=ALU.add), R=[t2_k, t3_k], W=[t1_k])
        kb.op("dve", lambda e: e.tensor_scalar(out=t1, in0=t1, scalar1=-3.1415925, scalar2=3.1415925, op0=ALU.max, op1=ALU.min), R=[t1_k], W=[t1_k])
        kb.op("act", lambda e, out=out: e.activation(out=out, in_=t1, func=AF.Sin), R=[t1_k], W=[o_k])


def cmul(kb, o_re, o_im, o_k, a_re, a_im, a_k, b_re, b_im, b_k, t1, t1_k, t2, t2_k, eng="dve"):
    kb.op(eng, lambda e: e.tensor_tensor(out=t1, in0=a_re, in1=b_re, op=ALU.mult), R=[a_k, b_k], W=[t1_k])
    kb.op(eng, lambda e: e.tensor_tensor(out=t2, in0=a_im, in1=b_im, op=ALU.mult), R=[a_k, b_k], W=[t2_k])
    kb.op(eng, lambda e: e.tensor_tensor(out=o_re, in0=t1, in1=t2, op=ALU.subtract), R=[t1_k, t2_k], W=[o_k])
    kb.op(eng, lambda e: e.tensor_tensor(out=t1, in0=a_re, in1=b_im, op=ALU.mult), R=[a_k, b_k], W=[t1_k])
    kb.op(eng, lambda e: e.tensor_tensor(out=t2, in0=a_im, in1=b_re, op=ALU.mult), R=[a_k, b_k], W=[t2_k])
    kb.op(eng, lambda e: e.tensor_tensor(out=o_im, in0=t1, in1=t2, op=ALU.add), R=[t1_k, t2_k], W=[o_k])


def s5_lambar(kb, lre, lim, dtt, k_in, np_, shape, pool):
    T = {}
    for nm in ("are", "th", "mag", "img", "sn", "cs", "lbr", "lbi", "t1", "t2", "t3"):
        T[nm] = kb.sb(shape, F32, "s5" + nm)
    a = lambda nm: T[nm][0][:]
    k = lambda nm: T[nm][1]
    kb.op("dve", lambda e: e.tensor_tensor(out=a("are"), in0=lre, in1=dtt, op=ALU.mult), R=[k_in], W=[k("are")])
    kb.op("dve", lambda e: e.tensor_tensor(out=a("th"), in0=lim, in1=dtt, op=ALU.mult), R=[k_in], W=[k("th")])
    kb.op("act", lambda e: e.activation(out=a("mag"), in_=a("are"), func=AF.Exp), R=[k("are")], W=[k("mag")])
    kb.op("act", lambda e: e.activation(out=a("img"), in_=a("are"), func=AF.Exp, scale=-1.0), R=[k("are")], W=[k("img")])
    sincos(kb, a("th"), k("th"), shape, a("sn"), a("cs"), k("sn"), [(a("t1"), k("t1")), (a("t2"), k("t2")), (a("t3"), k("t3"))])
    T["cs"] = (T["cs"][0], T["sn"][1])
    kb.op("dve", lambda e: e.tensor_tensor(out=a("lbr"), in0=a("mag"), in1=a("cs"), op=ALU.mult), R=[k("mag"), k("sn")], W=[k("lbr")])
    kb.op("dve", lambda e: e.tensor_tensor(out=a("lbi"), in0=a("mag"), in1=a("sn"), op=ALU.mult), R=[k("mag"), k("sn")], W=[k("lbi")])
    return T


def stage_s5(kb, P, TK, l, ident, ident_k):
    GP = 4
    with ExitStack() as st:
        kb.stk = st
        W1t, W1t_k = kb.sb([128, 2, 2, 32, 64], BF16, "W1t")
        Wct, Wct_k = kb.sb([128, 2, 2, 16, 128], BF16, "Wct")
        Toep, Toep_k = kb.sb([128, 32, 128], BF16, "Toep")
        apw, apw_k = kb.sb([128, 2, 2, 10, 16], F32, "apw")
        napi, napi_k = kb.sb([128, 2, 10, 16], F32, "napi")
        inj, inj_k = kb.sb([128, 2, 2, 16], F32, "inj")
        h0t, h0t_k = kb.sb([128, 2, 2, 16], F32, "h0t")
        h0b, h0b_k = kb.sb([128, 2, 2, 16], BF16, "h0b")
        fin, fin_k = kb.sb([128, 2, 2, 2, 16], F32, "fin")
        with ExitStack() as st2:
            kb.stk = st2
            Toep32, Toep32_k = kb.sb([128, 32, 128], F32, "Toep32")
            tmk, tmk_k = kb.sb([128, 2, 128], F32, "tmk")
            kb.dma("sp", tmk[:], P["tmask"].rearrange("d s t -> s d t"), W=[tmk_k])
            Dcol, Dcol_k = kb.sb([128, 32], F32, "Dcol")
            for t in range(8):
                kb.dma("sp", Dcol[16 * t:16 * (t + 1), :], P["s5_d"][l].rearrange("(g q) -> q g", q=16), Wp=[Dcol_k], slow=True)
            pw = Rot(kb, 2, [128, 512], F32, "pw", psum=True)
            ptp = Rot(kb, 2, [128, 512], F32, "ptp", psum=True)
            pT4 = Rot(kb, 2, [64, 4, 128], F32, "pT4", psum=True)
            ptf = Rot(kb, 2, [128, 512], F32, "ptf", psum=True)
            for d in range(2):
                with ExitStack() as st3:
                    kb.stk = st3
                    with ExitStack() as st4:
                        kb.stk = st4
                        cp, cp_k = kb.sb([32, 4, 64], F32, "cp")
                        kb.dma("sp", cp[:, 0, :], P["s5_lambda_re"][l, d], W=[cp_k])
                        kb.dma("sp", cp[:, 1, :], P["s5_lambda_im"][l, d], Wp=[cp_k])
                        ls, ls_k = kb.sb([32, 2], F32, "ls")
                        kb.dma("sp", ls[:, 0:1], P["s5_log_step"][l, d].rearrange("(g o) -> g o", o=1), W=[ls_k], slow=True)
                        kb.op("act", lambda e: e.activation(out=ls[:, 1:2], in_=ls[:, 0:1], func=AF.Exp), R=[ls_k], W=[ls_k])
                        dtt, dtt_k = kb.sb([32, 64], F32, "dtt")
                        kb.op("dve", lambda e: e.memset(dtt[:], 1.0), W=[dtt_k])
                        kb.op("dve", lambda e: e.tensor_scalar(out=dtt[:], in0=dtt[:], scalar1=ls[:, 1:2], scalar2=None, op0=ALU.mult), R=[ls_k, dtt_k], W=[dtt_k])
                        kall = Tk("kall")
                        kb.op("dve", lambda e: e.tensor_copy(out=cp[:, 2, :], in_=dtt[:]), R=[dtt_k, cp_k], W=[cp_k])
                        LB = s5_lambar(kb, cp[:, 0, :], cp[:, 1, :], cp[:, 2, :], cp_k, 32, [32, 64], None)
                        g_ = lambda nm: LB[nm][0][:]
                        gk = lambda nm: LB[nm][1]
                        fr, fr_k = kb.sb([32, 6, 64], F32, "fr")
                        lre, lim = cp[:, 0, :], cp[:, 1, :]
                        kb.op("dve", lambda e: e.tensor_scalar(out=fr[:, 0, :], in0=g_("lbr"), scalar1=-1.0, scalar2=None, op0=ALU.add), R=[gk("lbr")], W=[fr_k])
                        kb.op("dve", lambda e: e.tensor_tensor(out=fr[:, 1, :], in0=lre, in1=lre, op=ALU.mult), R=[cp_k, fr_k], W=[fr_k])
                        kb.op("dve", lambda e: e.tensor_tensor(out=fr[:, 2, :], in0=lim, in1=lim, op=ALU.mult), R=[cp_k, fr_k], W=[fr_k])
                        kb.op("dve", lambda e: e.tensor_tensor(out=fr[:, 1, :], in0=fr[:, 1, :], in1=fr[:, 2, :], op=ALU.add), R=[fr_k], W=[fr_k])
                        kb.op("dve", lambda e: e.reciprocal(out=fr[:, 1, :], in_=fr[:, 1, :]), R=[fr_k], W=[fr_k])
                        kb.op("dve", lambda e: e.tensor_tensor(out=fr[:, 2, :], in0=fr[:, 0, :], in1=lre, op=ALU.mult), R=[fr_k, cp_k], W=[fr_k])
                        kb.op("dve", lambda e: e.tensor_tensor(out=fr[:, 3, :], in0=g_("lbi"), in1=lim, op=ALU.mult), R=[gk("lbi"), cp_k, fr_k], W=[fr_k])
                        kb.op("dve", lambda e: e.tensor_tensor(out=fr[:, 2, :], in0=fr[:, 2, :], in1=fr[:, 3, :], op=ALU.add), R=[fr_k], W=[fr_k])
                        kb.op("dve", lambda e: e.tensor_tensor(out=fr[:, 4, :], in0=fr[:, 2, :], in1=fr[:, 1, :], op=ALU.mult), R=[fr_k], W=[fr_k])
                        kb.op("dve", lambda e: e.tensor_tensor(out=fr[:, 2, :], in0=g_("lbi"), in1=lre, op=ALU.mult), R=[gk("lbi"), cp_k, fr_k], W=[fr_k])
                        kb.op("dve", lambda e: e.tensor_tensor(out=fr[:, 3, :], in0=fr[:, 0, :], in1=lim, op=ALU.mult), R=[fr_k, cp_k], W=[fr_k])
                        kb.op("dve", lambda e: e.tensor_tensor(out=fr[:, 2, :], in0=fr[:, 2, :], in1=fr[:, 3, :], op=ALU.subtract), R=[fr_k], W=[fr_k])
                        kb.op("dve", lambda e: e.tensor_tensor(out=fr[:, 5, :], in0=fr[:, 2, :], in1=fr[:, 1, :], op=ALU.mult), R=[fr_k], W=[fr_k])
                        par = lambda idx: P["s5par"][d, idx].rearrange("(g n) -> g n", n=64)
                        kb.dma("sp", par(32), fr[:, 4, :], R=[fr_k], Wp=[TK["s5par"]])
                        kb.dma("sp", par(33), fr[:, 5, :], R=[fr_k], Wp=[TK["s5par"]])
                        ivr, ivr_k = kb.sb([32, 2, 64], F32, "ivr")
                        kb.op("dve", lambda e: e.tensor_tensor(out=ivr[:, 0, :], in0=g_("img"), in1=g_("cs"), op=ALU.mult), R=[gk("img"), gk("sn")], W=[ivr_k])
                        kb.op("dve", lambda e: e.scalar_tensor_tensor(out=ivr[:, 1, :], in0=g_("img"), scalar=-1.0, in1=g_("sn"), op0=ALU.mult, op1=ALU.mult), R=[gk("img"), gk("sn"), ivr_k], W=[ivr_k])
                        pwr = Rot(kb, 3, [32, 2, 64], F32, "pwr")
                        ct1, ct1_k = kb.sb([32, 64], F32, "ct1")
                        ct2, ct2_k = kb.sb([32, 64], F32, "ct2")
                        cur, cur_k = kb.sb([32, 2, 64], F32, "one")
                        kb.op("dve", lambda e, cur=cur: e.memset(cur[:, 0, :], 1.0), W=[cur_k])
                        kb.op("dve", lambda e, cur=cur: e.memset(cur[:, 1, :], 0.0), R=[cur_k], W=[cur_k])
                        kb.dma("sp", par(0), cur[:, 0, :], R=[cur_k], Wp=[TK["s5par"]])
                        kb.dma("sp", par(1), cur[:, 1, :], R=[cur_k], Wp=[TK["s5par"]])
                        for sign in (0, 1):
                            c_, c_k = cur, cur_k
                            for k in range(1, 9 if sign == 0 else 8):
                                n_, n_k = pwr.next()
                                if sign == 0:
                                    cmul(kb, n_[:, 0, :], n_[:, 1, :], n_k, c_[:, 0, :], c_[:, 1, :], c_k, g_("lbr"), g_("lbi"), gk("lbi"), ct1[:], ct1_k, ct2[:], ct2_k)
                                    base = 2 * k
                                else:
                                    cmul(kb, n_[:, 0, :], n_[:, 1, :], n_k, c_[:, 0, :], c_[:, 1, :], c_k, ivr[:, 0, :], ivr[:, 1, :], ivr_k, ct1[:], ct1_k, ct2[:], ct2_k)
                                    base = 18 + 2 * (k - 1)
                                kb.dma("sp", par(base), n_[:, 0, :], R=[n_k], Wp=[TK["s5par"]])
                                kb.dma("sp", par(base + 1), n_[:, 1, :], R=[n_k], Wp=[TK["s5par"]])
                                c_, c_k = n_, n_k
                        kb.barrier()
                    kb.stk = st3
                    X = [kb.sb([128, 2048], F32, "X") for _ in range(14)]
                    POS, NEG, Fr, BT, CT, FB, Rr = (X[0], X[1]), (X[2], X[3]), (X[4], X[5]), (X[6], X[7]), (X[8], X[9]), (X[10], X[11]), (X[12], X[13])
                    bn, bn_k = kb.sb([128, 16, 16], F32, "bn")
                    bnr, bnr_k = kb.sb([128, 16, 8, 16], F32, "bnr")
                    tA, tA_k = kb.sb([128, 2048], F32, "tA")
                    tB, tB_k = kb.sb([128, 2048], F32, "tB")
                    for comp in range(2):
                        for sl in range(8):
                            kb.dma("sp", POS[comp][0][16 * sl:16 * (sl + 1), :], P["s5par"][d, 2 * sl + comp].partition_broadcast(16), R=[TK["s5par"]], Wp=[POS[comp][1]])
                            nidx = (0 + comp) if sl == 0 else (18 + 2 * (sl - 1) + comp)
                            kb.dma("sp", NEG[comp][0][16 * sl:16 * (sl + 1), :], P["s5par"][d, nidx].partition_broadcast(16), R=[TK["s5par"]], Wp=[NEG[comp][1]])
                        kb.dma("sp", Fr[comp][0][:], P["s5par"][d, 32 + comp].partition_broadcast(128), R=[TK["s5par"]], W=[Fr[comp][1]])
                        csrc = P["s5_c_re" if comp == 0 else "s5_c_im"][l, d].rearrange("g p n -> p g n")
                        for sl in range(8):
                            kb.dma("sp", CT[comp][0][16 * sl:16 * (sl + 1), :].rearrange("p (g n) -> p g n", n=64), csrc, Wp=[CT[comp][1]])
                        kb.dma("sp", bn[:], P["s5_b_re" if comp == 0 else "s5_b_im"][l, d].rearrange("(gp g2) n q -> (g2 n) gp q", g2=2), W=[bn_k])
                        for t in range(8):
                            kb.op("dve" if t % 2 == 0 else "pool", lambda e, t=t: e.tensor_copy(out=bnr[:, :, t, :], in_=bn[:]), R=[bn_k], W=[bnr_k])
                        for gq in range(4):
                            ps, ps_k = pw.next()
                            for i in range(4):
                                gp = gq * 4 + i
                                kb.op("pe", lambda e, ps=ps, i=i, gp=gp: e.matmul(ps[:, i * 128:(i + 1) * 128], lhsT=bnr[:, gp, :, :].rearrange("p t q -> p (t q)"), rhs=ident[:],
                                                                              start=True, stop=True), R=[bnr_k, ident_k], W=[ps_k], inc=(i == 3))
                            kb.op("act", lambda e, ps=ps, gq=gq, comp=comp: e.copy(out=BT[comp][0][:, gq * 512:(gq + 1) * 512], in_=ps[:]), R=[ps_k], W=[BT[comp][1]])
                    A_ = lambda pr, c: pr[c][0][:]
                    K_ = lambda pr: pr[0][1]
                    def both(pr):
                        return [pr[0][1], pr[1][1]]
                    def CM(o, a, b_):
                        ok = Tk("o")
                        kb.op("dve", lambda e: e.tensor_tensor(out=tA[:], in0=A_(a, 0), in1=A_(b_, 0), op=ALU.mult), R=both(a) + both(b_), W=[tA_k])
                        kb.op("dve", lambda e: e.tensor_tensor(out=tB[:], in0=A_(a, 1), in1=A_(b_, 1), op=ALU.mult), R=both(a) + both(b_), W=[tB_k])
                        kb.op("dve", lambda e: e.tensor_tensor(out=A_(o, 0), in0=tA[:], in1=tB[:], op=ALU.subtract), R=[tA_k, tB_k], W=[o[0][1]])
                        kb.op("dve", lambda e: e.tensor_tensor(out=tA[:], in0=A_(a, 0), in1=A_(b_, 1), op=ALU.mult), R=both(a) + both(b_), W=[tA_k])
                        kb.op("dve", lambda e: e.tensor_tensor(out=tB[:], in0=A_(a, 1), in1=A_(b_, 0), op=ALU.mult), R=both(a) + both(b_), W=[tB_k])
                        kb.op("dve", lambda e: e.tensor_tensor(out=A_(o, 1), in0=tA[:], in1=tB[:], op=ALU.add), R=[tA_k, tB_k], W=[o[1][1]])
                    CM(FB, Fr, BT)
                    if d == 0:
                        CM(BT, NEG, FB)
                        CM(Rr, POS, CT)
                        LAM = NEG
                        for comp in range(2):
                            kb.dma("sp", LAM[comp][0][:], P["s5par"][d, 2 + comp].partition_broadcast(128), R=[TK["s5par"]], W=[LAM[comp][1]])
                        CM(CT, Rr, LAM)
                        for comp in range(2):
                            kb.dma("sp", Fr[comp][0][:], P["s5par"][d, 14 + comp].partition_broadcast(128), R=[TK["s5par"]], W=[Fr[comp][1]])
                        CM(FB, BT, Fr)
                        W1src, Lsrc, Rsrc, Wcsrc = FB, BT, Rr, CT
                    else:
                        CM(BT, POS, FB)
                        CM(Rr, NEG, CT)
                        LAM = POS
                        for comp in range(2):
                            kb.dma("sp", LAM[comp][0][:], P["s5par"][d, 16 + comp].partition_broadcast(128), R=[TK["s5par"]], W=[LAM[comp][1]])
                        CM(CT, Rr, LAM)
                        W1src, Lsrc, Rsrc, Wcsrc = BT, BT, Rr, CT
                    for comp in range(2):
                        kb.op("act", lambda e, comp=comp: e.copy(out=W1t[:, d, comp, :, :].rearrange("p g n -> p (g n)"), in_=A_(W1src, comp)), R=both(W1src), W=[W1t_k])
                    for gp in range(16):
                        for comp in range(2):
                            ps, ps_k = ptp.next()
                            for g2 in range(2):
                                g = 2 * gp + g2
                                kb.op("pe", lambda e, ps=ps, g=g, g2=g2, comp=comp: e.matmul(ps[64 * g2:64 * g2 + 64, 0:128], lhsT=A_(Wcsrc, comp)[:, g * 64:(g + 1) * 64], rhs=ident[:],
                                                                                                start=True, stop=True, tile_position=(0, 64 * g2)), R=both(Wcsrc) + [ident_k], W=[ps_k], inc=(g2 == 1))
                            kb.op("dve", lambda e, ps=ps, gp=gp, comp=comp: e.tensor_scalar(out=Wct[:, d, comp, gp, :], in0=ps[:, 0:128], scalar1=(1.0 if comp == 0 else -1.0), scalar2=None, op0=ALU.mult),
                                  R=[ps_k], W=[Wct_k])
                    lrr = Rot(kb, 2, [64, 4, 128], F32, "lr")
                    for g in range(32):
                        p4, p4_k = pT4.next()
                        for i, (src, comp) in enumerate(((Lsrc, 0), (Lsrc, 1), (Rsrc, 0), (Rsrc, 1))):
                            kb.op("pe", lambda e, p4=p4, i=i, src=src, comp=comp, g=g: e.transpose(p4[:, i, :], A_(src, comp)[:, g * 64:(g + 1) * 64], ident[:]),
                                  R=both(src) + [ident_k], W=[p4_k], inc=(i == 3))
                        lr, lr_k = lrr.next()
                        kb.op("act", lambda e, lr=lr, p4=p4: e.copy(out=lr[:, 0:3, :], in_=p4[:, 0:3, :]), R=[p4_k], W=[lr_k])
                        kb.op("dve", lambda e, lr=lr, p4=p4: e.tensor_scalar(out=lr[:, 3, :], in0=p4[:, 3, :], scalar1=-1.0, scalar2=None, op0=ALU.mult), R=[p4_k], W=[lr_k])
                        pt, pt_k = ptf.next()
                        kb.op("pe", lambda e, pt=pt, lr=lr: e.matmul(pt[:, 0:128], lhsT=lr[:, 0, :], rhs=lr[:, 2, :], start=True, stop=False), R=[lr_k], W=[pt_k], inc=False)
                        kb.op("pe", lambda e, pt=pt, lr=lr: e.matmul(pt[:, 0:128], lhsT=lr[:, 1, :], rhs=lr[:, 3, :], start=False, stop=True), R=[lr_k], W=[pt_k])
                        if d == 0:
                            kb.op("dve", lambda e, pt=pt, g=g: e.tensor_tensor(out=Toep32[:, g, :], in0=pt[:, 0:128], in1=tmk[:, 0, :], op=ALU.mult), R=[pt_k, tmk_k], W=[Toep32_k])
                        else:
                            tt_, tt_k = lrr.next() if False else (None, None)
                            kb.op("dve", lambda e, pt=pt, g=g: e.tensor_tensor(out=tA[:, 0:128], in0=pt[:, 0:128], in1=tmk[:, 1, :], op=ALU.mult), R=[pt_k, tmk_k], W=[tA_k])
                            kb.op("dve", lambda e, g=g: e.tensor_tensor(out=Toep32[:, g, :], in0=Toep32[:, g, :], in1=tA[:, 0:128], op=ALU.add), R=[tA_k, Toep32_k], W=[Toep32_k])
                            kb.op("dve", lambda e, g=g: e.scalar_tensor_tensor(out=Toep[:, g, :], in0=ident[:], scalar=Dcol[:, g:g + 1], in1=Toep32[:, g, :], op0=ALU.mult, op1=ALU.add),
                                  R=[ident_k, Dcol_k, Toep32_k], W=[Toep_k])
                    sp_, sp_k = kb.sb([128, 4, 16], F32, "sp_")
                    kb.dma("sp", sp_[:, 0, :], P["s5_lambda_re"][l, d].rearrange("(gp g2) n -> (g2 n) gp", g2=2), W=[sp_k], slow=True)
                    kb.dma("sp", sp_[:, 1, :], P["s5_lambda_im"][l, d].rearrange("(gp g2) n -> (g2 n) gp", g2=2), Wp=[sp_k], slow=True)
                    for g2 in range(2):
                        kb.dma("sp", sp_[64 * g2:64 * g2 + 64, 2, :], P["s5_log_step"][l, d, g2::2].partition_broadcast(64), Wp=[sp_k], slow=True)
                    kb.op("act", lambda e: e.activation(out=sp_[:, 2, :], in_=sp_[:, 2, :], func=AF.Exp), R=[sp_k], W=[sp_k])
                    LS_ = s5_lambar(kb, sp_[:, 0, :], sp_[:, 1, :], sp_[:, 2, :], sp_k, 128, [128, 16], None)
                    s1, s1_k = kb.sb([128, 16], F32, "s1")
                    s2, s2_k = kb.sb([128, 16], F32, "s2")
                    sq = [kb.sb([128, 2, 16], F32, "sq") for _ in range(2)]
                    c_re, c_im, c_k = LS_["lbr"][0][:], LS_["lbi"][0][:], Tk("ck")
                    cks = [LS_["lbr"][1], LS_["lbi"][1]]
                    for it in range(12):
                        o, o_k = sq[it % 2]
                        kb.op("dve", lambda e, c_re=c_re: e.tensor_tensor(out=s1[:], in0=c_re, in1=c_re, op=ALU.mult), R=cks, W=[s1_k])
                        kb.op("dve", lambda e, c_im=c_im: e.tensor_tensor(out=s2[:], in0=c_im, in1=c_im, op=ALU.mult), R=cks, W=[s2_k])
                        kb.op("dve", lambda e, o=o: e.tensor_tensor(out=o[:, 0, :], in0=s1[:], in1=s2[:], op=ALU.subtract), R=[s1_k, s2_k], W=[o_k])
                        kb.op("dve", lambda e, o=o, c_re=c_re, c_im=c_im: e.scalar_tensor_tensor(out=o[:, 1, :], in0=c_re, scalar=2.0, in1=c_im, op0=ALU.mult, op1=ALU.mult), R=cks + [o_k], W=[o_k])
                        c_re, c_im, cks = o[:, 0, :], o[:, 1, :], [o_k]
                        if it >= 2:
                            k = it - 2
                            kb.op("dve", lambda e, o=o, k=k: e.tensor_copy(out=apw[:, d, :, k, :], in_=o[:]), R=[o_k], W=[apw_k])
                            kb.op("dve", lambda e, o=o, k=k: e.tensor_scalar(out=napi[:, d, k, :], in0=o[:, 1, :], scalar1=-1.0, scalar2=None, op0=ALU.mult), R=[o_k], W=[napi_k])
                    kb.dma("sp", h0t[:, d, 0, :], P["s5_h0_re"][l, d].rearrange("(gp g2) n -> (g2 n) gp", g2=2), W=[h0t_k] if d == 0 else [], Wp=[] if d == 0 else [h0t_k], slow=True)
                    kb.dma("sp", h0t[:, d, 1, :], P["s5_h0_im"][l, d].rearrange("(gp g2) n -> (g2 n) gp", g2=2), Wp=[h0t_k], slow=True)
                    cmul(kb, inj[:, d, 0, :], inj[:, d, 1, :], inj_k, apw[:, d, 0, 0, :], apw[:, d, 1, 0, :], apw_k, h0t[:, d, 0, :], h0t[:, d, 1, :], h0t_k, s1[:], s1_k, s2[:], s2_k)
                    if d == 0:
                        kb.dump("d_pos", POS[0][0][:], POS[0][1], [128, 2048])
                        kb.dump("d_w1src", W1src[0][0][:], W1src[0][1], [128, 2048])
                    kb.barrier()
                kb.stk = st2
            kb.dump("d_apw", apw[:], apw_k, [128, 2, 2, 10, 16])
            kb.dump("d_inj", inj[:], inj_k, [128, 2, 2, 16])
            kb.op("dve", lambda e: e.tensor_copy(out=h0b[:], in_=h0t[:]), R=[h0t_k], W=[h0b_k])
            kb.barrier()
        kb.stk = st
        U, U_k = kb.sb([128, 32, NCH], BF16, "U")
        for g in range(32):
            for t in range(8):
                kb.dma("sp", U[16 * t:16 * (t + 1), g, :], P["UD"][g * 16:(g + 1) * 16, t, :], R=[TK["UD"]], Wp=[U_k])
        Hrot = [[kb.sb([128, GP, NCH], F32, "H") for _ in range(2)] for _ in range(2)]
        Hp, Hp_k = kb.sb([128, 2, 2, GP, NCH], BF16, "Hp")
        ph = Rot(kb, 2, [128, 1024], F32, "ph", psum=True)
        py = Rot(kb, 2, [128, 1024], F32, "py", psum=True)
        yrot = Rot(kb, 2, [128, NCH], F32, "yv")
        xsr = Rot(kb, 2, [128, NCH], F32, "xs5")
        g1, g1_k = kb.sb([128, NCH], F32, "g1")
        g2t, g2t_k = kb.sb([128, NCH], F32, "g2t")
        SEG = ((0, 512), (512, 32), (544, 32))
        for sub in range(16 // GP):
            for d in range(2):
                cur = 0
                for gl in range(GP):
                    gp = sub * GP + gl
                    for comp in range(2):
                        ps, ps_k = ph.next()
                        for (c0, cn) in ((0, 512), (512, 64)):
                            for g2 in range(2):
                                g = 2 * gp + g2
                                kb.op("pe", lambda e, ps=ps, g=g, g2=g2, comp=comp, c0=c0, cn=cn: e.matmul(
                                    ps[64 * g2:64 * g2 + 64, c0:c0 + cn], lhsT=W1t[:, d, comp, g, :], rhs=U[:, g, c0:c0 + cn], start=True, stop=True, tile_position=(0, 64 * g2)),
                                    R=[W1t_k, U_k], W=[ps_k], inc=(g2 == 1 and c0 == 512))
                        H, H_k = Hrot[cur][comp]
                        kb.op("act", lambda e, H=H, ps=ps, gl=gl: e.copy(out=H[:, gl, :], in_=ps[:, 0:NCH]), R=[ps_k], W=[H_k])
                        ci = 0 if d == 0 else 511
                        kb.op("dve", lambda e, H=H, gl=gl, gp=gp, comp=comp, ci=ci: e.tensor_tensor(out=H[:, gl, ci:ci + 1], in0=H[:, gl, ci:ci + 1], in1=inj[:, d, comp, gp:gp + 1], op=ALU.add),
                              R=[H_k, inj_k], W=[H_k])
                if sub == 0 and d == 0:
                    kb.dump("d_hc", Hrot[cur][0][0][:], Hrot[cur][0][1], [128, GP, NCH])
                for k in range(9):
                    sh = 1 << k
                    o_re, o_re_k = Hrot[cur][0]; o_im, o_im_k = Hrot[cur][1]
                    n_re, n_re_k = Hrot[1 - cur][0]; n_im, n_im_k = Hrot[1 - cur][1]
                    items = []
                    for gl in range(GP):
                        gp = sub * GP + gl
                        are = apw[:, d, 0, k, gp:gp + 1]; aim = apw[:, d, 1, k, gp:gp + 1]; naim = napi[:, d, k, gp:gp + 1]
                        for si, (c0, cn) in enumerate(SEG):
                            if si == 2:
                                continue
                            if si == 0:
                                V = (lambda gl, c0: (lambda t, a, b: t[:, gl, c0 + a:c0 + b]))(gl, c0)
                                n = cn
                            else:
                                V = (lambda gl: (lambda t, a, b: t[:, gl, 512:576].rearrange("p (s c) -> p s c", s=2)[:, :, a:b]))(gl)
                                n = 32
                            items.append((V, n, are, aim, naim))
                    for (V, n, are, aim, naim) in items:
                        if sh >= n:
                            kb.op("pool", lambda e, V=V, n=n: e.tensor_copy(out=V(n_re, 0, n), in_=V(o_re, 0, n)), R=[o_re_k], W=[n_re_k])
                            kb.op("pool", lambda e, V=V, n=n: e.tensor_copy(out=V(n_im, 0, n), in_=V(o_im, 0, n)), R=[o_im_k], W=[n_im_k])
                    act = []
                    for (V, n, are, aim, naim) in items:
                        if sh >= n:
                            continue
                        if d == 0:
                            dst, srcs, same, keep = (sh, n), (0, n - sh), (sh, n), (0, sh)
                        else:
                            dst, srcs, same, keep = (0, n - sh), (sh, n), (0, n - sh), (n - sh, n)
                        act.append((V, are, aim, naim, dst, srcs, same, keep))
                    for (V, are, aim, naim, dst, srcs, same, keep) in act:
                        kb.op("dve", lambda e, V=V, dst=dst, srcs=srcs, same=same, are=are: e.scalar_tensor_tensor(
                            out=V(n_re, *dst), in0=V(o_re, *srcs), scalar=are, in1=V(o_re, *same), op0=ALU.mult, op1=ALU.add), R=[o_re_k, apw_k], W=[n_re_k])
                        kb.op("dve", lambda e, V=V, dst=dst, srcs=srcs, same=same, are=are: e.scalar_tensor_tensor(
                            out=V(n_im, *dst), in0=V(o_im, *srcs), scalar=are, in1=V(o_im, *same), op0=ALU.mult, op1=ALU.add), R=[o_im_k, apw_k], W=[n_im_k])
                    for (V, are, aim, naim, dst, srcs, same, keep) in act:
                        kb.op("dve", lambda e, V=V, dst=dst, srcs=srcs, naim=naim: e.scalar_tensor_tensor(
                            out=V(n_re, *dst), in0=V(o_im, *srcs), scalar=naim, in1=V(n_re, *dst), op0=ALU.mult, op1=ALU.add), R=[o_im_k, napi_k, n_re_k], W=[n_re_k])
                        kb.op("dve", lambda e, V=V, dst=dst, srcs=srcs, aim=aim: e.scalar_tensor_tensor(
                            out=V(n_im, *dst), in0=V(o_re, *srcs), scalar=aim, in1=V(n_im, *dst), op0=ALU.mult, op1=ALU.add), R=[o_re_k, apw_k, n_im_k], W=[n_im_k])
                    for (V, are, aim, naim, dst, srcs, same, keep) in act:
                        kb.op("pool", lambda e, V=V, keep=keep: e.tensor_copy(out=V(n_re, *keep), in_=V(o_re, *keep)), R=[o_re_k], W=[n_re_k])
                        kb.op("pool", lambda e, V=V, keep=keep: e.tensor_copy(out=V(n_im, *keep), in_=V(o_im, *keep)), R=[o_im_k], W=[n_im_k])
                    cur = 1 - cur
                for comp in range(2):
                    H, H_k = Hrot[cur][comp]
                    for (c0, cn) in SEG:
                        if d == 0:
                            kb.op("act", lambda e, H=H, comp=comp, c0=c0, cn=cn: e.copy(out=Hp[:, d, comp, :, c0 + 1:c0 + cn], in_=H[:, :, c0:c0 + cn - 1]), R=[H_k], W=[Hp_k])
                            edge = c0
                        else:
                            kb.op("act", lambda e, H=H, comp=comp, c0=c0, cn=cn: e.copy(out=Hp[:, d, comp, :, c0:c0 + cn - 1], in_=H[:, :, c0 + 1:c0 + cn]), R=[H_k], W=[Hp_k])
                            edge = c0 + cn - 1
                        if c0 == 0:
                            kb.op("dve", lambda e, comp=comp, edge=edge: e.tensor_copy(out=Hp[:, d, comp, :, edge], in_=h0b[:, d, comp, sub * GP:(sub + 1) * GP]), R=[h0b_k, Hp_k], W=[Hp_k])
                        else:
                            kb.op("dve", lambda e, comp=comp, edge=edge: e.memset(Hp[:, d, comp, :, edge:edge + 1], 0.0), R=[Hp_k], W=[Hp_k])
                            b = 0 if c0 == 512 else 1
                            fc = (c0 + cn - 1) if d == 0 else c0
                            kb.op("dve", lambda e, H=H, comp=comp, b=b, fc=fc: e.tensor_copy(out=fin[:, b, d, comp, sub * GP:(sub + 1) * GP], in_=H[:, :, fc]), R=[H_k, fin_k], W=[fin_k])
            for gl in range(GP):
                gp = sub * GP + gl
                for g2 in range(2):
                    g = 2 * gp + g2
                    ps, ps_k = py.next()
                    for (c0, cn) in ((0, 512), (512, 64)):
                        kb.op("pe", lambda e, ps=ps, g=g, c0=c0, cn=cn: e.matmul(ps[:, c0:c0 + cn], lhsT=Toep[:, g, :], rhs=U[:, g, c0:c0 + cn], start=True, stop=False),
                              R=[Toep_k, U_k], W=[ps_k], inc=False)
                        for d in range(2):
                            for comp in range(2):
                                last = (d == 1 and comp == 1)
                                kb.op("pe", lambda e, ps=ps, g2=g2, gp=gp, gl=gl, d=d, comp=comp, c0=c0, cn=cn, last=last: e.matmul(
                                    ps[:, c0:c0 + cn], lhsT=Wct[64 * g2:64 * g2 + 64, d, comp, gp, :], rhs=Hp[64 * g2:64 * g2 + 64, d, comp, gl, c0:c0 + cn], start=False, stop=last),
                                    R=[Wct_k, Hp_k], W=[ps_k], inc=(last and c0 == 512))
                    yv, yv_k = yrot.next()
                    xs_, xs_k = xsr.next()
                    kb.op("act", lambda e, xs_=xs_, ps=ps: e.copy(out=xs_[:], in_=ps[:, 0:NCH]), R=[ps_k], W=[xs_k])
                    x = xs_[:]
                    ps_k = xs_k
                    kb.op("dve", lambda e, x=x: e.tensor_tensor(out=g1[:], in0=x, in1=x, op=ALU.mult), R=[ps_k], W=[g1_k])
                    kb.op("dve", lambda e: e.tensor_scalar(out=g1[:], in0=g1[:], scalar1=0.044715, scalar2=1.0, op0=ALU.mult, op1=ALU.add), R=[g1_k], W=[g1_k])
                    kb.op("dve", lambda e, x=x: e.tensor_tensor(out=g1[:], in0=g1[:], in1=x, op=ALU.mult), R=[g1_k, ps_k], W=[g1_k])
                    kb.op("act", lambda e: e.activation(out=g2t[:], in_=g1[:], func=AF.Tanh, scale=0.7978845608028654), R=[g1_k], W=[g2t_k])
                    kb.op("dve", lambda e: e.tensor_scalar(out=g2t[:], in0=g2t[:], scalar1=1.0, scalar2=0.5, op0=ALU.add, op1=ALU.mult), R=[g2t_k], W=[g2t_k])
                    kb.op("dve", lambda e, x=x, yv=yv: e.tensor_tensor(out=yv[:], in0=g2t[:], in1=x, op=ALU.mult), R=[g2t_k, ps_k], W=[yv_k])
                    for t in range(8):
                        kb.dma("sp", P["YD"][g * 16:(g + 1) * 16, t, :], yv[16 * t:16 * (t + 1), :], R=[yv_k], Wp=[TK["YD"]])
        for b in range(2):
            for d in range(2):
                kb.dma("sp", P["ns5re"][b, l, d].rearrange("(gp g2) n -> (g2 n) gp", g2=2), fin[:, b, d, 0, :], R=[fin_k], Wp=[TK["ns5re"]], slow=True)
                kb.dma("sp", P["ns5im"][b, l, d].rearrange("(gp g2) n -> (g2 n) gp", g2=2), fin[:, b, d, 1, :], R=[fin_k], Wp=[TK["ns5im"]], slow=True)
        kb.barrier()
    kb.stk = kb.top
    with ExitStack() as st:
        kb.stk = st
        YT, YT_k = kb.sb([128, 4, 8, NCH], F32, "YT")
        YB, YB_k = kb.sb([128, 4, 8, NCH], BF16, "YB")
        gw, gw_k = kb.sb([128, 4, 512], BF16, "gw")
        gb, gb_k = kb.sb([128, 4], F32, "gb")
        kb.dma("pool", gw[:], P["s5_glu_w"][l].rearrange("(k p) c -> p k c", p=128), W=[gw_k])
        kb.dma("sp", gb[:], P["s5_glu_b"][l].rearrange("(j p) -> p j", p=128), W=[gb_k], slow=True)
        for j in range(4):
            kb.dma("sp", YT[:, j, :, :], P["YD"][j * 128:(j + 1) * 128, :, :], R=[TK["YD"]], Wp=[YT_k])
        for j in range(4):
            if j % 2 == 0:
                kb.op("dve", lambda e, j=j: e.tensor_copy(out=YB[:, j, :, :], in_=YT[:, j, :, :]), R=[YT_k], W=[YB_k] if j == 0 else [], inc=True)
            else:
                kb.op("act", lambda e, j=j: e.copy(out=YB[:, j, :, :], in_=YT[:, j, :, :]), R=[YT_k], W=[], inc=True)
        kb.barrier()
        pg = Rot(kb, 3, [128, 1024], F32, "pg", psum=True)
        gtr = Rot(kb, 2, [128, 8, NCH], BF16, "gbt")
        sgr = Rot(kb, 3, [128, NCH], F32, "sg")
        Z, Z_k = kb.sb([128, NTOK], BF16, "Z")
        for jo in range(4):
            gt, gt_k = gtr.next()
            kb.dma("sp", gt[:], P["gbP"][jo * 128:(jo + 1) * 128, :, :], R=[TK["gbP"]], W=[gt_k])
            for t in range(8):
                ps, ps_k = pg.next()
                for (c0, cn) in ((0, 512), (512, 64)):
                    for kc in range(4):
                        kb.op("pe", lambda e, ps=ps, kc=kc, jo=jo, t=t, c0=c0, cn=cn: e.matmul(ps[:, c0:c0 + cn], lhsT=gw[:, kc, jo * 128:(jo + 1) * 128], rhs=YB[:, kc, t, c0:c0 + cn],
                                                                                 start=(kc == 0), stop=(kc == 3)), R=[gw_k, YB_k], W=[ps_k], inc=(kc == 3 and c0 == 512))
                sg, sg_k = sgr.next()
                kb.op("act", lambda e, sg=sg, ps=ps, jo=jo: e.activation(out=sg[:], in_=ps[:, 0:NCH], func=AF.Sigmoid, bias=gb[:, jo:jo + 1]), R=[ps_k, gb_k], W=[sg_k])
                kb.op("dve", lambda e, sg=sg, jo=jo, t=t: e.tensor_tensor(out=sg[:], in0=sg[:], in1=YT[:, jo, t, :], op=ALU.mult), R=[sg_k, YT_k], W=[sg_k])
                kb.op("dve", lambda e, sg=sg, gt=gt, t=t: e.tensor_tensor(out=Z[:, t::8], in0=sg[:], in1=gt[:, t, :], op=ALU.mult), R=[sg_k, gt_k], W=[Z_k])
            kb.dma("sp", P["ygT"][512 + jo * 128:512 + (jo + 1) * 128, :], Z[:], R=[Z_k], Wp=[TK["ygT"]])
        kb.barrier()
    kb.stk = kb.top


def rope_consts():
    t = np.arange(LS)
    row = (t // 64).astype(np.float32)
    col = (t % 64).astype(np.float32)
    inv = (np.float32(10000.0) ** (-(np.arange(0, 64, 2, dtype=np.float32)) / np.float32(64))).astype(np.float32)
    ang = np.stack([row[:, None] * inv[None, :], col[:, None] * inv[None, :]], axis=1).astype(np.float32)
    c = np.cos(ang.astype(np.float64)).astype(np.float32)
    s_ = np.sin(ang.astype(np.float64)).astype(np.float32)
    c = np.broadcast_to(c[:, None], (LS, 10, 2, 32)).reshape(LS, 640)
    s_ = np.broadcast_to(s_[:, None], (LS, 10, 2, 32)).reshape(LS, 640)
    return np.ascontiguousarray(c), np.ascontiguousarray(s_)


def dft_consts():
    def cs(L):
        j = np.arange(L, dtype=np.int64)
        jk = (j[:, None] * j[None, :]) % L
        ang = 2.0 * np.pi * jk.astype(np.float64) / L
        return np.stack([np.cos(ang), np.sin(ang)]) / np.sqrt(L)
    return (cs(128).astype(np.float32), cs(LS).astype(ml_dtypes.bfloat16), cs(LP).astype(ml_dtypes.bfloat16))


def make_in_maps(inputs, n_cores=8):
    ident = np.eye(128, dtype=np.float32)
    f = lambda a: np.ascontiguousarray(np.asarray(a, dtype=np.float32))
    shared = {k: f(inputs[k]) for k in (
        "c_ctx", "norm_pre", "norm_post", "w_mod", "b_mod", "w_in", "fourier_w", "s5_lambda_re", "s5_lambda_im",
        "s5_log_step", "s5_b_re", "s5_b_im", "s5_c_re", "s5_c_im", "s5_d", "s5_glu_w", "s5_glu_b", "q_norm",
        "k_norm", "hgrn_lb_logits", "hgrn_norm", "w_proj_a", "w_proj_b", "w_proj_c", "w_proj_d", "w_out")}
    shared["ident"] = ident
    shared["dft128"], shared["dftS"], shared["dftP"] = dft_consts()
    shared["ropeC"], shared["ropeS"] = rope_consts()
    tri = np.triu(np.ones((64, 64), np.float32))
    shared["hmask"] = np.ascontiguousarray(np.stack([tri, tri.T]))
    sidx = np.arange(128) // 16
    tf = (sidx[:, None] <= sidx[None, :]).astype(np.float32)
    shared["tmask"] = np.ascontiguousarray(np.stack([tf, tf.T]))
    maps = []
    for i in range(n_cores):
        m = dict(shared)
        m["x_s"] = f(inputs["x_sample"][i])
        m["x_p"] = f(inputs["x_prompt"][2 * i:2 * i + 2]).reshape(2 * LP, D)
        m["c"] = f(inputs["c"][i:i + 1])
        m["cache_k"] = f(inputs["cache_k"][i]); m["cache_v"] = f(inputs["cache_v"][i])
        m["s5_h0_re"] = f(inputs["state_s5_re"][i]); m["s5_h0_im"] = f(inputs["state_s5_im"][i])
        m["hg_s0"] = f(inputs["state_hgrn"][i])
        maps.append(m)
    return maps


def kernel(**inputs):
    nc, kb = build()
    maps = make_in_maps(inputs)
    res = run_bass_kernel_spmd(nc, maps, core_ids=list(range(8))).results
    y_p = np.concatenate([r["y_p"].reshape(2, LP, D) for r in res], axis=0)
    y_s = np.stack([r["y_s"] for r in res], axis=0)
    nk = np.concatenate([r["nk"] for r in res], axis=0)
    nv = np.concatenate([r["nv"] for r in res], axis=0)
    s5r = np.concatenate([r["ns5re"] for r in res], axis=0)
    s5i = np.concatenate([r["ns5im"] for r in res], axis=0)
    hg = np.concatenate([r["nhg"] for r in res], axis=0)
    return (y_p.astype(np.float32), y_s.astype(np.float32), nk.astype(np.float32), nv.astype(np.float32),
            s5r.astype(np.float32), s5i.astype(np.float32), hg.astype(np.float32))
```
